# Optimizing a Trainium2 kernel written in Bass

```python
import math
import jax
import jax.numpy as jnp
from jax import lax

D_MODEL = 2048
BATCH = 4
SEQ = 2048
DEPTH = 4

GRID_W = 64
CTX_LEN = 256
N_EVEN = (DEPTH + 1) // 2
N_ODD = DEPTH // 2
EPS = 1e-6
N_DIR = 2

HEAD_DIM = 128
A_Q_HEADS = 12
A_KV_HEADS = 4
A_GROUP = A_Q_HEADS // A_KV_HEADS
A_WIDTH = A_Q_HEADS * HEAD_DIM
KV_WIDTH = A_KV_HEADS * HEAD_DIM
ROPE_THETA = 10000.0
ROPE_AXIS_DIM = HEAD_DIM // 2
Q_BLOCK = 128

B_GROUPS = 4
B_GROUP_DIM = 128
B_WIDTH = B_GROUPS * B_GROUP_DIM
CHUNK = 128

EVEN_IN = 2 * KV_WIDTH + A_WIDTH + 2 * B_WIDTH + A_WIDTH + B_WIDTH
EVEN_MIX = A_WIDTH + B_WIDTH
EVEN_SPLITS = (KV_WIDTH, 2 * KV_WIDTH, 2 * KV_WIDTH + A_WIDTH, 2 * KV_WIDTH + A_WIDTH + B_WIDTH, 2 * KV_WIDTH + A_WIDTH + 2 * B_WIDTH, 2 * KV_WIDTH + 2 * A_WIDTH + 2 * B_WIDTH)

C_WIDTH = 1024
C_GROUP_DIM = 16
C_GROUPS = C_WIDTH // C_GROUP_DIM
C_STATE = 64

D_WIDTH = 1024
D_BLOCKS = 8
D_BLOCK_DIM = D_WIDTH // D_BLOCKS
CONV_W = 4
CONV_PAD_LO = (CONV_W - 1) // 2
LRU_C = 8.0

ODD_IN = 2 * (C_WIDTH + D_WIDTH)
ODD_MIX = C_WIDTH + D_WIDTH
ODD_SPLITS = (C_WIDTH, C_WIDTH + D_WIDTH, 2 * C_WIDTH + D_WIDTH)

kernel_name = 'hybrid_diffusion_attn_sgu_s5_rglru'


def rms_norm(x, g):
    xf = x.astype(jnp.float32)
    y = xf * lax.rsqrt(jnp.mean(xf * xf, axis=-1, keepdims=True) + EPS)
    return (y * g.astype(jnp.float32)).astype(x.dtype)


def modulate(x, g, shift, scale):
    return rms_norm(x, g) * (1 + scale) + shift


def adaln(cond, w, b):
    m = jax.nn.silu(cond) @ w + b
    return jnp.split(m, 3, axis=-1)


def flip(t):
    return jnp.flip(t, axis=1)


def axial_rope_tables(rows):
    row = jnp.repeat(jnp.arange(rows), GRID_W).astype(jnp.float32)
    col = jnp.tile(jnp.arange(GRID_W), rows).astype(jnp.float32)
    inv = ROPE_THETA ** (-jnp.arange(0, ROPE_AXIS_DIM, 2, dtype=jnp.float32) / ROPE_AXIS_DIM)
    ang = jnp.concatenate([row[:, None] * inv, col[:, None] * inv], axis=-1)
    return jnp.cos(ang), jnp.sin(ang)


def apply_rope(x, cos, sin):
    xf = x.astype(jnp.float32).reshape(x.shape[:-1] + (HEAD_DIM // 2, 2))
    x0, x1 = xf[..., 0], xf[..., 1]
    cs, sn = cos[None, :, None, :], sin[None, :, None, :]
    out = jnp.stack([x0 * cs - x1 * sn, x0 * sn + x1 * cs], axis=-1)
    return out.reshape(x.shape).astype(x.dtype)


def heads(t, n):
    return t.reshape(t.shape[0], t.shape[1], n, HEAD_DIM)


def gqa_attend(q, k, v):
    s = jnp.einsum('bqkgd,bskd->bkgqs', q, k).astype(jnp.float32) * (HEAD_DIM ** -0.5)
    p = jax.nn.softmax(s, axis=-1).astype(v.dtype)
    return jnp.einsum('bkgqs,bskd->bqkgd', p, v)


def latent_attention(q_l, k_all, v_all):
    b, l = q_l.shape[:2]
    nb = l // Q_BLOCK
    qb = q_l.reshape(b, nb, Q_BLOCK, A_KV_HEADS, A_GROUP, HEAD_DIM).transpose(1, 0, 2, 3, 4, 5)
    out = lax.map(lambda qi: gqa_attend(qi, k_all, v_all), qb)
    return out.transpose(1, 0, 2, 3, 4, 5).reshape(b, l, A_WIDTH)


def chunk_mlp(u, v, norm_g, w_s, b_s):
    b, l = u.shape[:2]
    vn = rms_norm(v.reshape(b, l // CHUNK, CHUNK, B_GROUPS, B_GROUP_DIM), norm_g.reshape(B_GROUPS, B_GROUP_DIM))
    mixed = jnp.einsum('gpq,bnqgc->bnpgc', w_s, vn) + b_s.T[:, :, None]
    return u * mixed.reshape(b, l, B_WIDTH)


def even_mixer(h, hc, w_in, w_out, q_g, k_g, sgu_g, w_s, b_s, cos, sin, ctx_out):
    b, l = h.shape[:2]
    lc = hc.shape[1]
    k_l, v_l, q_l, bu_l, bv_l, ga_l, gb_l = jnp.split(h @ w_in, EVEN_SPLITS, axis=-1)
    q_l = apply_rope(rms_norm(heads(q_l, A_Q_HEADS), q_g), cos, sin)
    k_l = apply_rope(rms_norm(heads(k_l, A_KV_HEADS), k_g), cos, sin)
    v_l = heads(v_l, A_KV_HEADS)
    if ctx_out:
        k_c, v_c, q_c, bu_c, bv_c, ga_c, gb_c = jnp.split(hc @ w_in, EVEN_SPLITS, axis=-1)
    else:
        k_c, v_c = jnp.split(hc @ w_in[:, :2 * KV_WIDTH], 2, axis=-1)
    k_c = rms_norm(heads(k_c, A_KV_HEADS), k_g)
    v_c = heads(v_c, A_KV_HEADS)
    k_all = jnp.concatenate([k_c, k_l], axis=1)
    v_all = jnp.concatenate([v_c, v_l], axis=1)
    attn_l = latent_attention(q_l, k_all, v_all)
    sgu_l = chunk_mlp(bu_l, bv_l, sgu_g, w_s, b_s)
    mix_l = jnp.concatenate([attn_l * jax.nn.silu(ga_l), sgu_l * jax.nn.silu(gb_l)], axis=-1)
    out_l = mix_l @ w_out
    if not ctx_out:
        return out_l, None
    q_c = rms_norm(heads(q_c, A_Q_HEADS), q_g).reshape(b, lc, A_KV_HEADS, A_GROUP, HEAD_DIM)
    attn_c = gqa_attend(q_c, k_c, v_c).reshape(b, lc, A_WIDTH)
    sgu_c = chunk_mlp(bu_c, bv_c, sgu_g, w_s, b_s)
    mix_c = jnp.concatenate([attn_c * jax.nn.silu(ga_c), sgu_c * jax.nn.silu(gb_c)], axis=-1)
    return out_l, mix_c @ w_out


def linear_scan(a, b, h0=None):
    if h0 is not None:
        b = b.at[:, 0].add(a[:, 0] * h0)

    def combine(e1, e2):
        a1, b1 = e1
        a2, b2 = e2
        return a1 * a2, a2 * b1 + b2

    return lax.associative_scan(combine, (a, b), axis=1)[1]


def s5_direction(u_c, u_l, lam_re, lam_im, log_dt, b_re, b_im, c_re, c_im, ctx_out):
    lam = lax.complex(lam_re.astype(jnp.float32), lam_im.astype(jnp.float32))
    dt = jnp.exp(log_dt.astype(jnp.float32))[:, None]
    a_bar = jnp.exp(lam * dt)
    b_bar = ((a_bar - 1) / lam)[:, :, None] * lax.complex(b_re.astype(jnp.float32), b_im.astype(jnp.float32))
    c_mat = lax.complex(c_re.astype(jnp.float32), c_im.astype(jnp.float32))

    def drive(u):
        bu = jnp.einsum('blgp,gnp->blgn', u.astype(jnp.float32).astype(jnp.complex64), b_bar)
        return jnp.broadcast_to(a_bar, bu.shape), bu

    def readout(hs):
        return jnp.einsum('blgn,gpn->blgp', hs, c_mat).real

    a_c, bu_c = drive(u_c)
    h_c = linear_scan(a_c, bu_c)
    a_l, bu_l = drive(u_l)
    h_l = linear_scan(a_l, bu_l, h_c[:, -1])
    return (readout(h_c) if ctx_out else None), readout(h_l)


def s5_mixer(u_c, u_l, lam_re, lam_im, log_dt, b_re, b_im, c_re, c_im, d_skip, glu_w, glu_b, ctx_out):
    def grp(t):
        return t.reshape(t.shape[0], t.shape[1], C_GROUPS, C_GROUP_DIM)

    yc_f, yl_f = s5_direction(grp(u_c), grp(u_l), lam_re[0], lam_im[0], log_dt[0], b_re[0], b_im[0], c_re[0], c_im[0], ctx_out)
    yc_b, yl_b = s5_direction(grp(flip(u_c)), grp(flip(u_l)), lam_re[1], lam_im[1], log_dt[1], b_re[1], b_im[1], c_re[1], c_im[1], ctx_out)

    def finish(y_f, y_b, u):
        y = (y_f + flip(y_b)).reshape(u.shape).astype(u.dtype) + d_skip * u
        y = jax.nn.gelu(y)
        return y * jax.nn.sigmoid(y @ glu_w + glu_b)

    y_l = finish(yl_f, yl_b, u_l)
    return (finish(yc_f, yc_b, u_c) if ctx_out else None), y_l


def short_conv(x, w, b):
    l = x.shape[1]
    xp = jnp.pad(x, ((0, 0), (CONV_PAD_LO, CONV_W - 1 - CONV_PAD_LO), (0, 0)))
    return sum(xp[:, k:k + l] * w[k] for k in range(CONV_W)) + b


def rglru_direction(x_c, x_l, lam, wa, ba, wx, bx, ctx_out):
    def coeffs(xv):
        b, l = xv.shape[:2]
        xg = xv.reshape(b, l, D_BLOCKS, D_BLOCK_DIM)
        r = jax.nn.sigmoid(jnp.einsum('blhi,hij->blhj', xg, wa).reshape(b, l, D_WIDTH) + ba)
        i = jax.nn.sigmoid(jnp.einsum('blhi,hij->blhj', xg, wx).reshape(b, l, D_WIDTH) + bx)
        log_a = (-LRU_C * jax.nn.softplus(-lam) * r).astype(jnp.float32)
        mult = jnp.sqrt(-jnp.expm1(2 * log_a))
        return jnp.exp(log_a), mult * (i * xv).astype(jnp.float32)

    a_c, b_c = coeffs(x_c)
    h_c = linear_scan(a_c, b_c)
    a_l, b_l = coeffs(x_l)
    h_l = linear_scan(a_l, b_l, h_c[:, -1])
    return (h_c if ctx_out else None), h_l


def rglru_mixer(x_c, x_l, conv_w, conv_b, lam, wa, ba, wx, bx, ctx_out):
    x_c = short_conv(x_c, conv_w, conv_b)
    x_l = short_conv(x_l, conv_w, conv_b)
    hc_f, hl_f = rglru_direction(x_c, x_l, lam[0], wa[0], ba[0], wx[0], bx[0], ctx_out)
    hc_b, hl_b = rglru_direction(flip(x_c), flip(x_l), lam[1], wa[1], ba[1], wx[1], bx[1], ctx_out)
    y_l = (hl_f + flip(hl_b)).astype(x_l.dtype)
    y_c = (hc_f + flip(hc_b)).astype(x_c.dtype) if ctx_out else None
    return y_c, y_l


def odd_mixer(h, hc, w_in, w_out, lam_re, lam_im, log_dt, b_re, b_im, c_re, c_im, d_skip, glu_w, glu_b, conv_w, conv_b, lam, wa, ba, wx, bx, ctx_out):
    u_l, xd_l, gc_l, gd_l = jnp.split(h @ w_in, ODD_SPLITS, axis=-1)
    if ctx_out:
        u_c, xd_c, gc_c, gd_c = jnp.split(hc @ w_in, ODD_SPLITS, axis=-1)
    else:
        u_c, xd_c = jnp.split(hc @ w_in[:, :C_WIDTH + D_WIDTH], (C_WIDTH,), axis=-1)
    s5_c, s5_l = s5_mixer(u_c, u_l, lam_re, lam_im, log_dt, b_re, b_im, c_re, c_im, d_skip, glu_w, glu_b, ctx_out)
    lru_c, lru_l = rglru_mixer(xd_c, xd_l, conv_w, conv_b, lam, wa, ba, wx, bx, ctx_out)
    out_l = jnp.concatenate([s5_l * jax.nn.silu(gc_l), lru_l * jax.nn.silu(gd_l)], axis=-1) @ w_out
    if not ctx_out:
        return out_l, None
    out_c = jnp.concatenate([s5_c * jax.nn.silu(gc_c), lru_c * jax.nn.silu(gd_c)], axis=-1) @ w_out
    return out_l, out_c


def setup_inputs(seed: int = 0) -> dict:
    key = jax.random.key(seed)
    keys = jax.random.split(key, 40)
    ks = [keys[i] for i in range(40)]
    f32 = jnp.float32

    def nrm(shape, std):
        return jax.random.normal(ks.pop(), shape, f32) * std

    def near_one(shape):
        return 1.0 + nrm(shape, 0.02)

    d = D_MODEL
    x = nrm((BATCH, SEQ, d), 1.0)
    c = nrm((BATCH, d), 1.0)
    ctx = nrm((BATCH, CTX_LEN, d), 1.0)
    c_ctx = nrm((d,), 1.0)
    ada_w = nrm((DEPTH, d, 3 * d), 0.5 * d ** -0.5)
    ada_b = nrm((DEPTH, 3 * d), 0.02)
    norm_g = near_one((DEPTH, d))
    ev_w_in = nrm((N_EVEN, d, EVEN_IN), d ** -0.5)
    ev_w_out = nrm((N_EVEN, EVEN_MIX, d), EVEN_MIX ** -0.5)
    ev_q_g = near_one((N_EVEN, HEAD_DIM))
    ev_k_g = near_one((N_EVEN, HEAD_DIM))
    ev_sgu_g = near_one((N_EVEN, B_WIDTH))
    ev_ws = nrm((N_EVEN, B_GROUPS, CHUNK, CHUNK), CHUNK ** -0.5)
    ev_bs = near_one((N_EVEN, B_GROUPS, CHUNK))
    od_w_in = nrm((N_ODD, d, ODD_IN), d ** -0.5)
    od_w_out = nrm((N_ODD, ODD_MIX, d), ODD_MIX ** -0.5)
    sshape = (N_ODD, N_DIR, C_GROUPS, C_STATE)
    s5_lam_re = -0.5 + nrm(sshape, 0.01)
    s5_lam_im = math.pi * jnp.arange(C_STATE, dtype=f32) + nrm(sshape, 0.01)
    s5_log_dt = jax.random.uniform(ks.pop(), (N_ODD, N_DIR, C_GROUPS), f32, math.log(1e-3), math.log(1e-1))
    s5_b_re = nrm((N_ODD, N_DIR, C_GROUPS, C_STATE, C_GROUP_DIM), C_GROUP_DIM ** -0.5)
    s5_b_im = nrm((N_ODD, N_DIR, C_GROUPS, C_STATE, C_GROUP_DIM), C_GROUP_DIM ** -0.5)
    s5_c_re = nrm((N_ODD, N_DIR, C_GROUPS, C_GROUP_DIM, C_STATE), 0.5 ** 0.5)
    s5_c_im = nrm((N_ODD, N_DIR, C_GROUPS, C_GROUP_DIM, C_STATE), 0.5 ** 0.5)
    s5_d = nrm((N_ODD, C_WIDTH), 1.0)
    s5_glu_w = nrm((N_ODD, C_WIDTH, C_WIDTH), C_WIDTH ** -0.5)
    s5_glu_b = nrm((N_ODD, C_WIDTH), 0.02)
    lru_conv_w = nrm((N_ODD, CONV_W, D_WIDTH), CONV_W ** -0.5)
    lru_conv_b = nrm((N_ODD, D_WIDTH), 0.02)
    a0 = jax.random.uniform(ks.pop(), (N_ODD, N_DIR, D_WIDTH), f32, 0.9, 0.999)
    s = a0 ** (1.0 / LRU_C)
    lru_lam = jnp.log(s) - jnp.log1p(-s)
    lru_wa = nrm((N_ODD, N_DIR, D_BLOCKS, D_BLOCK_DIM, D_BLOCK_DIM), D_BLOCK_DIM ** -0.5)
    lru_ba = nrm((N_ODD, N_DIR, D_WIDTH), 0.02)
    lru_wx = nrm((N_ODD, N_DIR, D_BLOCKS, D_BLOCK_DIM, D_BLOCK_DIM), D_BLOCK_DIM ** -0.5)
    lru_bx = nrm((N_ODD, N_DIR, D_WIDTH), 0.02)
    return {'x': x, 'c': c, 'ctx': ctx, 'c_ctx': c_ctx, 'ada_w': ada_w, 'ada_b': ada_b, 'norm_g': norm_g,
            'ev_w_in': ev_w_in, 'ev_w_out': ev_w_out, 'ev_q_g': ev_q_g, 'ev_k_g': ev_k_g, 'ev_sgu_g': ev_sgu_g,
            'ev_ws': ev_ws, 'ev_bs': ev_bs, 'od_w_in': od_w_in, 'od_w_out': od_w_out,
            's5_lam_re': s5_lam_re, 's5_lam_im': s5_lam_im, 's5_log_dt': s5_log_dt, 's5_b_re': s5_b_re,
            's5_b_im': s5_b_im, 's5_c_re': s5_c_re, 's5_c_im': s5_c_im, 's5_d': s5_d, 's5_glu_w': s5_glu_w,
            's5_glu_b': s5_glu_b, 'lru_conv_w': lru_conv_w, 'lru_conv_b': lru_conv_b, 'lru_lam': lru_lam,
            'lru_wa': lru_wa, 'lru_ba': lru_ba, 'lru_wx': lru_wx, 'lru_bx': lru_bx}


def reference(x, c, ctx, c_ctx, ada_w, ada_b, norm_g, ev_w_in, ev_w_out, ev_q_g, ev_k_g, ev_sgu_g, ev_ws, ev_bs,
              od_w_in, od_w_out, s5_lam_re, s5_lam_im, s5_log_dt, s5_b_re, s5_b_im, s5_c_re, s5_c_im, s5_d,
              s5_glu_w, s5_glu_b, lru_conv_w, lru_conv_b, lru_lam, lru_wa, lru_ba, lru_wx, lru_bx):
    rows = x.shape[1] // GRID_W
    cos, sin = axial_rope_tables(rows)
    for layer in range(DEPTH):
        ctx_out = layer < DEPTH - 1
        shift, scale, gate = adaln(c, ada_w[layer], ada_b[layer])
        shift_c, scale_c, gate_c = adaln(c_ctx, ada_w[layer], ada_b[layer])
        h = modulate(x, norm_g[layer], shift[:, None], scale[:, None])
        hc = modulate(ctx, norm_g[layer], shift_c, scale_c)
        j = layer // 2
        if layer % 2 == 0:
            out_l, out_c = even_mixer(h, hc, ev_w_in[j], ev_w_out[j], ev_q_g[j], ev_k_g[j], ev_sgu_g[j], ev_ws[j], ev_bs[j], cos, sin, ctx_out)
        else:
            out_l, out_c = odd_mixer(h, hc, od_w_in[j], od_w_out[j], s5_lam_re[j], s5_lam_im[j], s5_log_dt[j], s5_b_re[j], s5_b_im[j], s5_c_re[j], s5_c_im[j], s5_d[j], s5_glu_w[j], s5_glu_b[j], lru_conv_w[j], lru_conv_b[j], lru_lam[j], lru_wa[j], lru_ba[j], lru_wx[j], lru_bx[j], ctx_out)
        x = x + gate[:, None] * out_l
        if ctx_out:
            ctx = ctx + gate_c * out_c
    return x
```

```python
import math
import numpy as np
from contextlib import ExitStack
import concourse.bass as bass
import concourse.mybir as mybir
from concourse.bass_utils import run_bass_kernel_spmd

F32 = mybir.dt.float32
BF16 = mybir.dt.bfloat16
ALU = mybir.AluOpType
AF = mybir.ActivationFunctionType
AX = mybir.AxisListType

T = 2304
NT = 18
D = 2048
KT = 16
CTXL = 256
EPS = 1e-6
CH5 = [(0, 512), (512, 512), (1024, 512), (1536, 512), (2048, 256)]
EVEN_IN = 5632
ODD_IN = 4096


class Buf:
    __slots__ = ("name", "w", "rs")

    def __init__(self, name=""):
        self.name = name
        self.w = None
        self.rs = []


class _Eng:
    def __init__(self, name, eng, sems, inc):
        self.name = name
        self.eng = eng
        self.sems = sems
        self.cnt = [0] * len(sems)
        self.inc = inc
        self.n = 0


class Sched:
    def __init__(self, nc, es, n_dma_sems=8, same_engine_sync=True):
        self.nc = nc
        self.same = same_engine_sync
        self.E = {}
        for name, e in (("pe", nc.tensor), ("dve", nc.vector), ("act", nc.scalar), ("pool", nc.gpsimd)):
            s = es.enter_context(nc.semaphore("s_" + name))
            self.E[name] = _Eng(name, e, [s], 1)
        self.Q = {}
        for qname, e in (("sp", nc.sync), ("actq", nc.scalar), ("poolq", nc.gpsimd)):
            sems = [es.enter_context(nc.semaphore("d_%s%d" % (qname, i))) for i in range(n_dma_sems)]
            self.Q[qname] = _Eng(qname, e, sems, 16)
        self.stream = {"pe": "pe", "dve": "dve", "act": "act", "pool": "pool",
                       "sp": "sp", "actq": "act", "poolq": "pool"}
        self.waited = {s: {} for s in ("pe", "dve", "act", "pool", "sp")}
        self.streng = {"pe": nc.tensor, "dve": nc.vector, "act": nc.scalar, "pool": nc.gpsimd, "sp": nc.sync}
        self.nwaits = 0
        self.ninst = 0
        self._cap = None

    def begin_capture(self):
        self._cap = []

    def end_capture(self):
        c = self._cap
        self._cap = None
        return c

    def replay(self, *lists):
        pos = [0] * len(lists)
        tot = [max(1, len(l)) for l in lists]
        while True:
            best, bi = None, -1
            for i, l in enumerate(lists):
                if pos[i] < len(l):
                    frac = pos[i] / tot[i]
                    if best is None or frac < best:
                        best, bi = frac, i
            if bi < 0:
                break
            kind, a, kw = lists[bi][pos[bi]]
            pos[bi] += 1
            if kind == "op":
                self.op(*a)
            else:
                self.dma(*a, **kw)

    def _wait(self, stream, tok):
        sem, val, _ = tok
        w = self.waited[stream]
        k = id(sem)
        if w.get(k, 0) >= val:
            return
        w[k] = val
        self.streng[stream].wait_ge(sem, val)
        self.nwaits += 1

    @staticmethod
    def _toks(reads, writes):
        toks = []
        for b in reads:
            if b.w is not None:
                toks.append(b.w)
        for b in writes:
            if b.w is not None:
                toks.append(b.w)
            toks.extend(b.rs)
        return toks

    @staticmethod
    def _commit(tok, reads, writes):
        for b in reads:
            b.rs.append(tok)
        for b in writes:
            b.w = tok
            b.rs = []

    def op(self, en, fn, reads=(), writes=()):
        if self._cap is not None:
            self._cap.append(("op", (en, fn, list(reads), list(writes)), None))
            return None
        E = self.E[en]
        stream = self.stream[en]
        skip_same = (en == "pe") or (not self.same)
        for t in self._toks(reads, writes):
            if skip_same and t[2] == en:
                continue
            self._wait(stream, t)
        ins = fn(E.eng)
        E.cnt[0] += 1
        ins.then_inc(E.sems[0], 1)
        tok = (E.sems[0], E.cnt[0], en)
        self._commit(tok, reads, writes)
        self.ninst += 1
        return tok

    def dma(self, qn, out, in_, reads=(), writes=(), **kw):
        if self._cap is not None:
            self._cap.append(("dma", (qn, out, in_, list(reads), list(writes)), kw))
            return None
        Q = self.Q[qn]
        stream = self.stream[qn]
        for t in self._toks(reads, writes):
            self._wait(stream, t)
        j = Q.n % len(Q.sems)
        Q.n += 1
        sem = Q.sems[j]
        if Q.cnt[j] > 0:
            self._wait(stream, (sem, Q.cnt[j], qn))
        ins = Q.eng.dma_start(out=out, in_=in_, **kw)
        Q.cnt[j] += 16
        ins.then_inc(sem, 16)
        tok = (sem, Q.cnt[j], qn)
        self._commit(tok, reads, writes)
        self.ninst += 1
        return tok

    def barrier(self):
        toks = []
        for en, E in self.E.items():
            if E.cnt[0] > 0:
                toks.append((E.sems[0], E.cnt[0], en))
        for qn, Q in self.Q.items():
            for j, sem in enumerate(Q.sems):
                if Q.cnt[j] > 0:
                    toks.append((sem, Q.cnt[j], qn))
        for stream in self.waited:
            for t in toks:
                self._wait(stream, t)


def vw(ap, dims, off=0):
    return bass.AP(tensor=ap.tensor, offset=ap.offset + off, ap=[list(ap.ap[0])] + [list(d) for d in dims])


class Ctx:
    pass


def build(nlayers=4, debug=(), skip=()):
    nc = bass.Bass("TRN2", target_bir_lowering=False)
    K = Ctx()
    K.nc = nc

    def din(name, shape):
        return nc.dram_tensor(name, list(shape), F32, kind="ExternalInput").ap()

    def dscr(name, shape, dt=F32):
        kind = "ExternalOutput" if name in debug else "Internal"
        return nc.dram_tensor(name, list(shape), dt, kind=kind).ap()

    xin = din("xin", [T, D])
    cond = din("cond", [2, D])
    ada_w = din("ada_w", [4, D, 3 * D])
    ada_b = din("ada_b", [4, 3 * D])
    norm_g = din("norm_g", [4, D])
    ev_w_in = din("ev_w_in", [2, D, EVEN_IN])
    ev_w_out = din("ev_w_out", [2, D, D])
    ev_q_g = din("ev_q_g", [2, 128])
    ev_k_g = din("ev_k_g", [2, 128])
    ev_sgu_g = din("ev_sgu_g", [2, 512])
    ev_ws = din("ev_ws", [2, 4, 128, 128])
    ev_bs = din("ev_bs", [2, 4, 128])
    rope_cos = din("rope_cos", [T, 64])
    rope_sin = din("rope_sin", [T, 64])
    od = {}
    for nm, shp in (("od_w_in", [2, D, ODD_IN]), ("od_w_out", [2, D, D]),
                    ("s5_lam_re", [2, 2, 64, 64]), ("s5_lam_im", [2, 2, 64, 64]), ("s5_log_dt", [2, 2, 64]),
                    ("s5_b_re", [2, 2, 64, 64, 16]), ("s5_b_im", [2, 2, 64, 64, 16]),
                    ("s5_c_re", [2, 2, 64, 16, 64]), ("s5_c_im", [2, 2, 64, 16, 64]),
                    ("s5_d", [2, 1024]), ("s5_glu_w", [2, 1024, 1024]), ("s5_glu_b", [2, 1024]),
                    ("lru_conv_w", [2, 4, 1024]), ("lru_conv_b", [2, 1024]), ("lru_lam", [2, 2, 1024]),
                    ("lru_wa", [2, 2, 8, 128, 128]), ("lru_ba", [2, 2, 1024]),
                    ("lru_wx", [2, 2, 8, 128, 128]), ("lru_bx", [2, 2, 1024])):
        od[nm] = din(nm, shp)
    out = nc.dram_tensor("out", [2048, D], F32, kind="ExternalOutput").ap()

    xT = dscr("xT", [D, T])
    ZE = dscr("ZE", [T, EVEN_IN])

    with ExitStack() as es:
        S = Sched(nc, es)
        K.S = S

        uid = [0]

        def sb(stack, name, shape, dt):
            uid[0] += 1
            return stack.enter_context(nc.sbuf_tensor("%s_%d" % (name, uid[0]), list(shape), dt))

        def ps(stack, name, shape, dt=F32):
            uid[0] += 1
            return stack.enter_context(nc.psum_tensor("%s_%d" % (name, uid[0]), list(shape), dt))

        identf = sb(es, "identf", [128, 128], F32)
        identb = sb(es, "identb", [128, 128], BF16)
        onesb = sb(es, "onesb", [128, 128], BF16)
        epsT = sb(es, "epsT", [128, 1], F32)
        mhalf = sb(es, "mhalf", [128, 16], F32)
        MODS = sb(es, "MODS", [128, 4, 48, 2], F32)
        GS = sb(es, "GS", [128, 4, 16, 2], F32)
        ATS = {"st": None, "AT": None}
        bC = Buf("const")
        bMODS = Buf("MODS")
        bAT = [Buf("AT%d" % k) for k in range(KT)]

        S.op("pool", lambda e: e.memset(identf[:], 0.0), writes=[bC])
        S.op("pool", lambda e: e.affine_select(out=identf[:], in_=identf[:], pattern=[[-1, 128]],
                                               compare_op=ALU.not_equal, fill=1.0, base=0, channel_multiplier=1),
             reads=[bC], writes=[bC])
        S.op("dve", lambda e: e.tensor_copy(out=identb[:], in_=identf[:]), reads=[bC], writes=[bC])
        S.op("dve", lambda e: e.memset(onesb[:], 1.0), writes=[bC])
        S.op("dve", lambda e: e.memset(epsT[:], EPS), writes=[bC])
        S.op("dve", lambda e: e.memset(mhalf[:], -0.5), writes=[bC])

        tstage = sb(es, "tstage", [32, 128], F32)
        btst = Buf("tstage")

        def load_T(dst, src_rows, n, ptile, bptile, bdst):
            S.dma("sp", tstage[0:n, :], src_rows, writes=[btst])
            S.op("pe", lambda e: e.transpose(out=ptile[:, 0:n], in_=tstage[0:n, :], identity=identf[0:n, 0:n]), reads=[btst, bC], writes=[bptile])
            S.op("dve", lambda e: e.tensor_copy(out=dst, in_=ptile[:, 0:n]), reads=[bptile], writes=[bdst])

        scTb = sb(es, "scTb", [128, KT, 2], BF16)
        ngT = sb(es, "ngT", [128, 4, KT], F32)
        ones2 = sb(es, "ones2", [1, 2], F32)
        bcond, bng = Buf("cond"), Buf("ng")
        with ExitStack() as ph:
            condT = sb(ph, "condT", [128, KT, 2], F32)
            scT = sb(ph, "scT", [128, KT, 2], F32)
            pm0 = ps(ph, "pm0", [128, 512])
            bpm0 = Buf()
            for kt in range(KT):
                load_T(condT[:, kt, :], cond[:, kt * 128:(kt + 1) * 128], 2, pm0, bpm0, bcond)
                load_T(ngT[:, :, kt], norm_g[:, kt * 128:(kt + 1) * 128], 4, pm0, bpm0, bng)
            S.op("act", lambda e: e.activation(out=scT[:], in_=condT[:], func=AF.Silu), reads=[bcond], writes=[bcond])
            S.op("dve", lambda e: e.memset(ones2[:], 1.0), writes=[bcond])
            S.op("dve", lambda e: e.tensor_copy(out=scTb[:], in_=scT[:]), reads=[bcond], writes=[bcond])
            S.barrier()

        class ModsJob:
            NCHK = 24

            def __init__(self, l, stack):
                self.l = l
                self.wbuf = [sb(stack, "adaw%d" % i, [128, KT, 256], BF16) for i in range(4)]
                self.brow = sb(stack, "brow", [1, 3 * D], F32)
                self.pm = ps(stack, "pm", [128, 512])
                self.bw = [Buf() for _ in range(4)]
                self.bbrow, self.bpm = Buf(), Buf()
                S.dma("sp", self.brow[:], ada_b[l:l + 1, :], writes=[self.bbrow])

            def dma(self, c):
                if c >= self.NCHK:
                    return
                S.dma("poolq", self.wbuf[c % 4][:], ada_w[self.l, :, c * 256:(c + 1) * 256].rearrange("(kt p) n -> p kt n", p=128),
                      writes=[self.bw[c % 4]])

            def compute(self, c):
                if c >= self.NCHK:
                    return
                l, wb_, bwb, pm, brow = self.l, self.wbuf[c % 4], self.bw[c % 4], self.pm, self.brow
                for mt in range(2):
                    m = c * 2 + mt
                    for kt in range(KT):
                        S.op("pe", lambda e, kt=kt, mt=mt: e.matmul(pm[:, 2 * mt:2 * mt + 2], lhsT=wb_[:, kt, mt * 128:(mt + 1) * 128], rhs=scTb[:, kt, :],
                                                                  start=(kt == 0), stop=False), reads=[bwb, bcond], writes=[self.bpm])
                    S.op("pe", lambda e, mt=mt, m=m: e.matmul(pm[:, 2 * mt:2 * mt + 2], lhsT=brow[0:1, m * 128:(m + 1) * 128], rhs=ones2[0:1, :],
                                                             start=False, stop=True), reads=[self.bbrow, bcond], writes=[self.bpm])
                S.op("dve", lambda e: e.tensor_copy(out=MODS[:, l, 2 * c:2 * c + 2, :], in_=pm[:, 0:4].rearrange("p (m r) -> p m r", r=2)),
                     reads=[self.bpm], writes=[bMODS])
                if c == self.NCHK - 1:
                    for r in range(2):
                        S.op("dve", lambda e, r=r: e.scalar_tensor_tensor(out=GS[:, l, :, r], in0=MODS[:, l, 16:32, r], scalar=1.0, in1=ngT[:, l, :],
                                                                           op0=ALU.add, op1=ALU.mult), reads=[bMODS, bng], writes=[bMODS])

        mods0_stack = ExitStack()
        lmods0 = []
        if nlayers > 0:
            job0 = ModsJob(0, mods0_stack)
            S.begin_capture()
            job0.dma(0)
            job0.dma(1)
            for c in range(ModsJob.NCHK):
                job0.dma(c + 2)
                job0.compute(c)
            lmods0 = S.end_capture()

        bxT = [Buf("xT%d" % k) for k in range(KT)]
        with ExitStack() as ph:
            xa = [sb(ph, "t0a%d" % i, [128, NT, 128], F32) for i in range(2)]
            xb = [sb(ph, "t0b%d" % i, [128, T], F32) for i in range(2)]
            pt = [ps(ph, "t0p%d" % i, [128, 512]) for i in range(4)]
            ba = [Buf(), Buf()]
            bb = [Buf(), Buf()]
            bp = [Buf() for _ in range(4)]
            pi = 0
            S.begin_capture()
            for kt in range(KT):
                a = xa[kt % 2]
                b = xb[kt % 2]
                S.dma("sp", a[:], xin[:, kt * 128:(kt + 1) * 128].rearrange("(t p) f -> p t f", p=128), writes=[ba[kt % 2]])
                for c, (c0, cl) in enumerate(CH5):
                    p = pt[pi % 4]
                    bpp = bp[pi % 4]
                    pi += 1
                    for i in range(cl // 128):
                        t = c0 // 128 + i
                        S.op("pe", lambda e, p=p, a=a, t=t, i=i: e.transpose(out=p[:, i * 128:(i + 1) * 128], in_=a[:, t, :], identity=identf[:]),
                             reads=[ba[kt % 2], bC], writes=[bpp])
                    en = "dve" if c % 2 == 0 else "act"
                    if en == "dve":
                        S.op("dve", lambda e, p=p, b=b, c0=c0, cl=cl: e.tensor_copy(out=b[:, c0:c0 + cl], in_=p[:, 0:cl]), reads=[bpp], writes=[bb[kt % 2]])
                    else:
                        S.op("act", lambda e, p=p, b=b, c0=c0, cl=cl: e.copy(out=b[:, c0:c0 + cl], in_=p[:, 0:cl]), reads=[bpp], writes=[bb[kt % 2]])
                S.dma("sp", xT[kt * 128:(kt + 1) * 128, :], b[:], reads=[bb[kt % 2]], writes=[bxT[kt]])
            lt0 = S.end_capture()
            S.replay(lt0, lmods0)
            S.barrier()
        mods0_stack.close()

        def at_alloc():
            ATS["st"] = ExitStack()
            ATS["AT"] = sb(ATS["st"], "AT", [128, KT, T], BF16)
            return ATS["AT"]

        def at_free():
            S.barrier()
            ATS["st"].close()
            ATS["AT"] = None

        def rev(ap, n):
            return bass.AP(tensor=ap.tensor, offset=ap.offset + n - 1, ap=[list(ap.ap[0]), [-1, n]])

        ZO = dscr("ZO", [ODD_IN, T])
        RSTDd = dscr("RSTDd", [1, T])
        bRd = Buf("RSTDd")
        have_rstd = [False]
        ZU = dscr("ZU", [1024, T], BF16)
        ZY = dscr("ZY", [1024, T])
        def phase_norm(l):
            AT = ATS["AT"]
            with ExitStack() as ph:
                xa = [sb(ph, "nx%d" % i, [128, T], F32) for i in range(2)]
                sq = [sb(ph, "nsq%d" % i, [128, T], BF16) for i in range(2)]
                tmp = [sb(ph, "ntmp%d" % i, [128, T], F32) for i in range(2)]
                RSTD = sb(ph, "RSTD", [128, T], F32)
                pss = [ps(ph, "nps%d" % i, [128, 512]) for i in range(5)]
                bx = [Buf(), Buf()]
                bsq = [Buf(), Buf()]
                btmp = [Buf(), Buf()]
                bps = [Buf() for _ in range(5)]
                bR = Buf()
                if have_rstd[0]:
                    S.dma("sp", RSTD[:], bass.AP(tensor=RSTDd.tensor, offset=RSTDd.offset, ap=[[0, 128], [1, T]]), reads=[bRd], writes=[bR])
                else:
                    for kt in range(KT):
                        i = kt % 2
                        S.dma("sp", xa[i][:], xT[kt * 128:(kt + 1) * 128, :], reads=[bxT[kt]], writes=[bx[i]])
                        S.op("act", lambda e, i=i: e.activation(out=sq[i][:], in_=xa[i][:], func=AF.Square), reads=[bx[i]], writes=[bsq[i]])
                        for c, (c0, cl) in enumerate(CH5):
                            S.op("pe", lambda e, i=i, c=c, c0=c0, cl=cl, kt=kt: e.matmul(
                                pss[c][:, 0:cl], lhsT=onesb[:], rhs=sq[i][:, c0:c0 + cl], start=(kt == 0), stop=(kt == KT - 1)),
                                reads=[bsq[i], bC], writes=[bps[c]])
                    for c, (c0, cl) in enumerate(CH5):
                        S.op("act", lambda e, c=c, c0=c0, cl=cl: e.activation(out=RSTD[:, c0:c0 + cl], in_=pss[c][:, 0:cl], func=AF.Sqrt,
                                                                             bias=epsT[:], scale=1.0 / D), reads=[bps[c], bC], writes=[bR])
                    S.op("dve", lambda e: e.reciprocal(out=RSTD[:], in_=RSTD[:]), reads=[bR], writes=[bR])
                for kt in range(KT):
                    i = kt % 2
                    S.dma("sp", xa[i][:], xT[kt * 128:(kt + 1) * 128, :], reads=[bxT[kt]], writes=[bx[i]])
                    for (a0, a1, r) in ((0, CTXL, 1), (CTXL, T, 0)):
                        S.op("dve", lambda e, i=i, a0=a0, a1=a1, r=r, kt=kt: e.scalar_tensor_tensor(
                            out=tmp[i][:, a0:a1], in0=xa[i][:, a0:a1], scalar=GS[:, l, kt, r:r + 1], in1=RSTD[:, a0:a1],
                            op0=ALU.mult, op1=ALU.mult), reads=[bx[i], bR, bMODS], writes=[btmp[i]])
                        S.op("act", lambda e, i=i, a0=a0, a1=a1, r=r, kt=kt: e.activation(
                            out=AT[:, kt, a0:a1], in_=tmp[i][:, a0:a1], func=AF.Identity, bias=MODS[:, l, kt, r:r + 1], scale=1.0),
                            reads=[btmp[i], bMODS], writes=[bAT[kt]])
                S.barrier()

        def gemm_tok(l, W, ncols, Z):
            AT = ATS["AT"]
            with ExitStack() as ph:
                wc = [sb(ph, "gw%d" % i, [128, KT, 512], BF16) for i in range(2)]
                st = [sb(ph, "gst%d" % i, [128, 512], F32) for i in range(4)]
                pp = [ps(ph, "gps%d" % i, [128, 512]) for i in range(4)]
                bw = [Buf(), Buf()]
                bst = [Buf() for _ in range(4)]
                bpp = [Buf() for _ in range(4)]
                bz = Buf()
                it = 0
                for n in range(ncols // 512):
                    w = wc[n % 2]
                    S.dma("poolq", w[:], W[:, n * 512:(n + 1) * 512].rearrange("(kt p) n -> p kt n", p=128), writes=[bw[n % 2]])
                    for t in range(NT):
                        j = it % 4
                        it += 1
                        for kt in range(KT):
                            S.op("pe", lambda e, j=j, w=w, kt=kt, t=t: e.matmul(
                                pp[j][:], lhsT=AT[:, kt, t * 128:(t + 1) * 128], rhs=w[:, kt, :], start=(kt == 0), stop=(kt == KT - 1)),
                                reads=[bw[n % 2], bAT[kt]], writes=[bpp[j]])
                        if it % 2 == 0:
                            S.op("dve", lambda e, j=j: e.tensor_copy(out=st[j][:], in_=pp[j][:]), reads=[bpp[j]], writes=[bst[j]])
                        else:
                            S.op("act", lambda e, j=j: e.copy(out=st[j][:], in_=pp[j][:]), reads=[bpp[j]], writes=[bst[j]])
                        S.dma("sp", Z[t * 128:(t + 1) * 128, n * 512:(n + 1) * 512], st[j][:], reads=[bst[j]], writes=[bz])
                S.barrier()

        def headnorm_rope(ph_bufs, src, nh, gtile, cos_t, sin_t, dst, eng2="pool", ntile=1, cs_tstride=0):
            sq, ss, tn, ta, tb_ = ph_bufs["sq"], ph_bufs["ss"], ph_bufs["tn"], ph_bufs["ta"], ph_bufs["tb"]
            bsrc, bscr, bdst = ph_bufs["bsrc"], ph_bufs["bscr"], ph_bufs["bdst"]
            H = nh * ntile
            W_ = H * 128
            s3 = src.rearrange("p (h d) -> p h d", d=128)
            S.op("dve", lambda e: e.tensor_tensor(out=sq[:, 0:W_], in0=src, in1=src, op=ALU.mult), reads=[bsrc], writes=[bscr])
            S.op("dve", lambda e: e.tensor_reduce(out=ss[:, 0:H], in_=sq[:, 0:W_].rearrange("p (h d) -> p h d", d=128), axis=AX.X, op=ALU.add),
                 reads=[bscr], writes=[bscr])
            S.op("pool", lambda e: e.tensor_scalar(out=ss[:, 0:H], in0=ss[:, 0:H], scalar1=1.0 / 128, scalar2=EPS, op0=ALU.mult, op1=ALU.add), reads=[bscr], writes=[bscr])
            S.op("pool", lambda e: e.tensor_tensor(out=ss[:, 0:H], in0=ss[:, 0:H], in1=mhalf[:, 0:H], op=ALU.pow), reads=[bscr, bC], writes=[bscr])
            rb = vw(ss[:], [[1, H], [0, 128]])
            S.op("dve", lambda e: e.tensor_tensor(out=tn[:, 0:W_].rearrange("p (h d) -> p h d", d=128), in0=s3, in1=rb, op=ALU.mult),
                 reads=[bsrc, bscr], writes=[bscr])
            tn3 = tn[:, 0:W_].rearrange("p (h d) -> p h d", d=128)
            S.op(eng2, lambda e: e.tensor_tensor(out=tn3, in0=tn3, in1=vw(gtile[:], [[0, H], [1, 128]]), op=ALU.mult), reads=[bscr, bC], writes=[bscr])
            x0 = vw(tn[:, 0:1], [[nh * 128, ntile], [128, nh], [2, 64]], off=0)
            x1 = vw(tn[:, 0:1], [[nh * 128, ntile], [128, nh], [2, 64]], off=1)
            d0 = vw(dst, [[nh * 128, ntile], [128, nh], [2, 64]], off=0)
            d1 = vw(dst, [[nh * 128, ntile], [128, nh], [2, 64]], off=1)
            cb = vw(cos_t, [[cs_tstride, ntile], [0, nh], [1, 64]])
            sbb = vw(sin_t, [[cs_tstride, ntile], [0, nh], [1, 64]])
            a3 = vw(ta[:, 0:1], [[nh * 64, ntile], [64, nh], [1, 64]])
            b3 = vw(tb_[:, 0:1], [[nh * 64, ntile], [64, nh], [1, 64]])
            S.op("dve", lambda e: e.tensor_tensor(out=a3, in0=x0, in1=cb, op=ALU.mult), reads=[bscr, ph_bufs["bcs"]], writes=[ph_bufs["ba"]])
            S.op(eng2, lambda e: e.tensor_tensor(out=b3, in0=x1, in1=sbb, op=ALU.mult), reads=[bscr, ph_bufs["bcs"]], writes=[ph_bufs["bb"]])
            S.op("dve", lambda e: e.tensor_tensor(out=d0, in0=a3, in1=b3, op=ALU.subtract), reads=[ph_bufs["ba"], ph_bufs["bb"]], writes=[bdst])
            S.op("dve", lambda e: e.tensor_tensor(out=a3, in0=x0, in1=sbb, op=ALU.mult), reads=[bscr, ph_bufs["bcs"]], writes=[ph_bufs["ba"]])
            S.op(eng2, lambda e: e.tensor_tensor(out=b3, in0=x1, in1=cb, op=ALU.mult), reads=[bscr, ph_bufs["bcs"]], writes=[ph_bufs["bb"]])
            S.op("dve", lambda e: e.tensor_tensor(out=d1, in0=a3, in1=b3, op=ALU.add), reads=[ph_bufs["ba"], ph_bufs["bb"]], writes=[bdst])

        def gemm_tok_tail(W, Z, n_list, stack):
            AT = ATS["AT"]
            wc = [sb(stack, "gtw%d" % i, [128, KT, 512], BF16) for i in range(2)]
            st = [sb(stack, "gtst%d" % i, [128, 512], F32) for i in range(4)]
            pp = [ps(stack, "gtps%d" % i, [128, 512]) for i in range(4)]
            bw = [Buf(), Buf()]
            bst = [Buf() for _ in range(4)]
            bpp = [Buf() for _ in range(4)]
            bz = Buf()
            it = 0
            for ni, n in enumerate(n_list):
                w = wc[ni % 2]
                S.dma("poolq", w[:], W[:, n * 512:(n + 1) * 512].rearrange("(kt p) n -> p kt n", p=128), writes=[bw[ni % 2]])
                for t in range(NT):
                    j_ = it % 4
                    it += 1
                    for kt in range(KT):
                        S.op("pe", lambda e, j_=j_, w=w, kt=kt, t=t: e.matmul(
                            pp[j_][:], lhsT=AT[:, kt, t * 128:(t + 1) * 128], rhs=w[:, kt, :], start=(kt == 0), stop=(kt == KT - 1)),
                            reads=[bw[ni % 2], bAT[kt]], writes=[bpp[j_]])
                    S.op("act", lambda e, j_=j_: e.copy(out=st[j_][:], in_=pp[j_][:]), reads=[bpp[j_]], writes=[bst[j_]])
                    S.dma("sp", Z[t * 128:(t + 1) * 128, n * 512:(n + 1) * 512], st[j_][:], reads=[bst[j_]], writes=[bz])

        def even_mixers(l, j):
            AT = ATS["AT"]
            with ExitStack() as ph:
                CS = [sb(ph, "CS%d" % i, [128, 2, 64], F32) for i in range(1)]
                bCS = [Buf()]
                CS3 = sb(ph, "CS3", [128, 3, 2, 64], F32)
                bCS3 = Buf()
                KG = sb(ph, "KG", [128, 128], F32)
                QG = sb(ph, "QG", [128, 128], F32)
                SG = sb(ph, "SG", [128, 512], F32)
                WST = sb(ph, "WST", [128, 4, 128], BF16)
                BST = sb(ph, "BST", [128, 4], F32)
                kT = sb(ph, "kT", [128, 4, T], BF16)
                VA = sb(ph, "VA", [128, NT, 4, 130], BF16)
                raw = [sb(ph, "raw%d" % i, [128, 2048], F32) for i in range(2)]
                sq = sb(ph, "hsq", [128, 1536], F32)
                tn = sb(ph, "htn", [128, 1536], F32)
                ta = sb(ph, "hta", [128, 768], F32)
                tb_ = sb(ph, "htb", [128, 768], F32)
                ss = sb(ph, "hss", [128, 16], F32)
                rot = sb(ph, "rot", [128, 12, 128], BF16)
                pT = [ps(ph, "epT%d" % i, [128, 4, 128], BF16) for i in range(2)]
                braw = [Buf(), Buf()]
                buv, brot, bkT, bVA, bqT, bATT, bsgt, bmix, bvnb, bsgu, brc = (Buf() for _ in range(11))
                bESs = [[Buf() for _ in range(NT)] for _ in range(2)]
                bpT = [Buf(), Buf()]
                bpS = [Buf() for _ in range(3)]
                bpO = [Buf(), Buf()]
                bpG = Buf()
                hb = dict(sq=sq, ss=ss, tn=tn, ta=ta, tb=tb_, bscr=Buf(), ba=Buf(), bb=Buf(), bdst=brot, bsrc=None)

                def load_cs(t):
                    i = 0
                    S.dma("sp", CS[i][:, 0, :], rope_cos[t * 128:(t + 1) * 128, :], writes=[bCS[i]])
                    S.dma("sp", CS[i][:, 1, :], rope_sin[t * 128:(t + 1) * 128, :], writes=[bCS[i]])
                    hb["bcs"] = bCS[i]
                    return CS[i][:, 0, :], CS[i][:, 1, :]
                kg = ev_k_g[j:j + 1, :]
                S.dma("sp", KG[:], bass.AP(tensor=kg.tensor, offset=kg.offset, ap=[[0, 128], [1, 128]]), writes=[bC])
                S.op("dve", lambda e: e.memset(VA[:], 1.0), writes=[bVA])

                def kv_E(tg):
                    r = raw[tg % 2]
                    for i in range(3):
                        t = tg * 3 + i
                        S.dma("sp", r[:, i * 512:(i + 1) * 512], ZE[t * 128:(t + 1) * 128, 0:512], writes=[braw[tg % 2]])
                        S.dma("sp", CS3[:, i, 0, :], rope_cos[t * 128:(t + 1) * 128, :], writes=[bCS3])
                        S.dma("sp", CS3[:, i, 1, :], rope_sin[t * 128:(t + 1) * 128, :], writes=[bCS3])
                    hb["bsrc"] = braw[tg % 2]
                    hb["bcs"] = bCS3
                    headnorm_rope(hb, r[:, 0:1536], 4, KG, CS3[:, 0, 0, :], CS3[:, 0, 1, :], rot[:, 0:12, :], ntile=3, cs_tstride=128, eng2="dve")

                def kv_P(tg):
                    for i in range(3):
                        t = tg * 3 + i
                        S.dma("poolq", VA[:, t, :, 0:128], ZE[t * 128:(t + 1) * 128, 512:1024].rearrange("p (h d) -> p h d", d=128), writes=[bVA])
                        p = pT[i % 2]
                        for h in range(4):
                            S.op("pe", lambda e, p=p, h=h, i=i: e.transpose(out=p[:, h, :], in_=rot[:, i * 4 + h, :], identity=identb[:]), reads=[brot, bC], writes=[bpT[i % 2]])
                        S.op("dve", lambda e, p=p, t=t: e.tensor_copy(out=kT[:, :, t * 128:(t + 1) * 128], in_=p[:]), reads=[bpT[i % 2]], writes=[bkT])

                phg = ExitStack()
                S.begin_capture()
                gemm_tok_tail(ev_w_in[j], ZE, list(range(2, EVEN_IN // 512)), phg)
                lg = S.end_capture()
                NG_ = NT // 3
                cut = [len(lg) * k // (NG_ + 1) for k in range(NG_ + 2)]
                kv_E(0)
                for tg in range(NG_):
                    S.replay(lg[cut[tg]:cut[tg + 1]])
                    kv_P(tg)
                    if tg + 1 < NG_:
                        kv_E(tg + 1)
                S.replay(lg[cut[NG_]:])
                S.barrier()
                phg.close()

                qTs = [sb(ph, "qT%d" % i, [128, 12, 256], BF16) for i in range(2)]
                bqTs = [Buf(), Buf()]
                ESs = [sb(ph, "ES%d" % i, [128, NT, 256], BF16) for i in range(2)]
                ATTs = [sb(ph, "ATT%d" % i, [128, 2, 1536], BF16) for i in range(2)]
                bATTs = [Buf(), Buf()]
                uv = sb(ph, "uv", [128, 1024], F32)
                mixtok = sb(ph, "mixtok", [128, 2048], BF16)
                vnb = sb(ph, "vnb", [128, 4, 128], BF16)
                rc = sb(ph, "rc", [128, 4], F32)
                pS = [ps(ph, "epS%d" % i, [128, 512]) for i in range(3)]
                pO = [ps(ph, "epO%d" % i, [128, 512]) for i in range(2)]
                pG = ps(ph, "epG", [128, 512])
                qg = ev_q_g[j:j + 1, :]
                S.dma("sp", QG[:], bass.AP(tensor=qg.tensor, offset=qg.offset, ap=[[0, 128], [1, 128]]), writes=[bC])
                sg_ = ev_sgu_g[j:j + 1, :]
                S.dma("sp", SG[:], bass.AP(tensor=sg_.tensor, offset=sg_.offset, ap=[[0, 128], [1, 512]]), writes=[bC])
                WSn = sq[:, 0:512].rearrange("p (g q) -> p g q", q=128)
                S.dma("sp", WSn, ev_ws[j].rearrange("g p q -> p g q"), writes=[hb["bscr"]])
                load_T(BST[:], ev_bs[j], 4, pG, bpG, bC)
                S.op("dve", lambda e: e.tensor_scalar(out=QG[:], in0=QG[:], scalar1=128.0 ** -0.5, scalar2=None, op0=ALU.mult), reads=[bC], writes=[bC])
                for g in range(4):
                    S.op("pe", lambda e, g=g: e.transpose(out=pS[0][:, g * 128:(g + 1) * 128], in_=WSn[:, g, :], identity=identf[:]), reads=[bC, hb["bscr"]], writes=[bpS[0]])
                S.op("dve", lambda e: e.tensor_copy(out=WST[:].rearrange("p g q -> p (g q)"), in_=pS[0][:]), reads=[bpS[0]], writes=[bC])

                chunks = [([0, 1], [0, 1])] + [([2 + 2 * c, 3 + 2 * c], list(range(NT))) for c in range(8)]
                cnt = {"t": 0, "si": 0, "oi": 0}

                def qprep_tile(ci, tl):
                    qtiles, ktiles = chunks[ci]
                    t = qtiles[tl]
                    qT = qTs[ci % 2]
                    bqT = bqTs[ci % 2]
                    tcount = cnt["t"]
                    cnt["t"] += 1
                    r = raw[tcount % 2]
                    br = braw[tcount % 2]
                    S.dma("sp", r[:, 0:1536], ZE[t * 128:(t + 1) * 128, 1024:2560], writes=[br])
                    hb["bsrc"] = br
                    c_t, s_t = load_cs(t)
                    headnorm_rope(hb, r[:, 0:1536], 12, QG, c_t, s_t, rot[:, :, :])
                    for hg in range(3):
                        p = pT[hg % 2]
                        for hh in range(4):
                            h = hg * 4 + hh
                            S.op("pe", lambda e, p=p, hh=hh, h=h: e.transpose(out=p[:, hh, :], in_=rot[:, h, :], identity=identb[:]),
                                 reads=[brot, bC], writes=[bpT[hg % 2]])
                        S.op("dve", lambda e, p=p, hg=hg, tl=tl: e.tensor_copy(out=qT[:, hg * 4:(hg + 1) * 4, tl * 128:(tl + 1) * 128], in_=p[:]),
                             reads=[bpT[hg % 2]], writes=[bqT])

                def qk_ops(ci, h):
                    qtiles, ktiles = chunks[ci]
                    Q = len(qtiles) * 128
                    qT = qTs[ci % 2]
                    bqT = bqTs[ci % 2]
                    par = (ci * 12 + h) % 2
                    kvh = h // 3
                    ops = []
                    for kt in ktiles:
                        def f(kt=kt):
                            si = cnt["si"]
                            cnt["si"] += 1
                            pss_ = pS[si % 3]
                            bps_ = bpS[si % 3]
                            S.op("pe", lambda e: e.matmul(pss_[:, 0:Q], lhsT=kT[:, kvh, kt * 128:(kt + 1) * 128], rhs=qT[:, h, 0:Q], start=True, stop=True),
                                 reads=[bkT, bqT], writes=[bps_])
                            S.op("act", lambda e: e.activation(out=ESs[par][:, kt, 0:Q], in_=pss_[:, 0:Q], func=AF.Exp),
                                 reads=[bps_], writes=[bESs[par][kt]])
                        ops.append(f)
                    return ops

                def pv_ops(ci, h):
                    qtiles, ktiles = chunks[ci]
                    nq = len(qtiles)
                    ATT = ATTs[ci % 2]
                    bATT = bATTs[ci % 2]
                    par = (ci * 12 + h) % 2
                    kvh = h // 3
                    ops = []
                    for qs in range(nq):
                        for ki, kt in enumerate(ktiles):
                            def f(qs=qs, ki=ki, kt=kt):
                                if ki == 0:
                                    cnt["oi"] += 1
                                oi = cnt["oi"]
                                po = pO[oi % 2]
                                bpo = bpO[oi % 2]
                                S.op("pe", lambda e: e.matmul(po[:, 0:129], lhsT=ESs[par][:, kt, qs * 128:(qs + 1) * 128], rhs=VA[:, kt, kvh, 0:129],
                                                              start=(ki == 0), stop=(ki == len(ktiles) - 1)), reads=[bESs[par][kt], bVA], writes=[bpo])
                                if ki == len(ktiles) - 1:
                                    S.op("dve", lambda e: e.reciprocal(out=rc[:, qs:qs + 1], in_=po[:, 128:129]), reads=[bpo], writes=[brc])
                                    S.op("dve", lambda e: e.tensor_scalar(out=ATT[:, qs, h * 128:(h + 1) * 128], in0=po[:, 0:128], scalar1=rc[:, qs:qs + 1],
                                                                          scalar2=None, op0=ALU.mult), reads=[bpo, brc], writes=[bATT])
                            ops.append(f)
                    return ops

                def gate_tile(ci, tl):
                    qtiles, ktiles = chunks[ci]
                    t = qtiles[tl]
                    ATT = ATTs[ci % 2]
                    bATT = bATTs[ci % 2]
                    tcount = cnt["t"]
                    cnt["t"] += 1
                    r = raw[tcount % 2]
                    br = braw[tcount % 2]
                    S.dma("sp", r[:, 0:2048], ZE[t * 128:(t + 1) * 128, 3584:5632], writes=[br])
                    S.dma("sp", uv[:], ZE[t * 128:(t + 1) * 128, 2560:3584], writes=[buv])
                    S.op("act", lambda e, r=r: e.activation(out=r[:, 0:2048], in_=r[:, 0:2048], func=AF.Silu), reads=[br], writes=[br])
                    S.op("dve", lambda e, tl=tl, r=r: e.tensor_tensor(out=mixtok[:, 0:1536], in0=ATT[:, tl, :], in1=r[:, 0:1536], op=ALU.mult),
                         reads=[bATT, br], writes=[bmix])
                    bv = uv[:, 512:1024]
                    bscr = hb["bscr"]
                    S.op("pool", lambda e: e.tensor_tensor(out=sq[:, 0:512], in0=bv, in1=bv, op=ALU.mult), reads=[buv], writes=[bscr])
                    S.op("dve", lambda e: e.tensor_reduce(out=ss[:, 0:4], in_=sq[:, 0:512].rearrange("p (h d) -> p h d", d=128), axis=AX.X, op=ALU.add),
                         reads=[bscr], writes=[bscr])
                    S.op("pool", lambda e: e.tensor_scalar(out=ss[:, 0:4], in0=ss[:, 0:4], scalar1=1.0 / 128, scalar2=EPS, op0=ALU.mult, op1=ALU.add), reads=[bscr], writes=[bscr])
                    S.op("pool", lambda e: e.tensor_tensor(out=ss[:, 0:4], in0=ss[:, 0:4], in1=mhalf[:, 0:4], op=ALU.pow), reads=[bscr, bC], writes=[bscr])
                    S.op("dve", lambda e: e.tensor_tensor(out=tn[:, 0:512].rearrange("p (h d) -> p h d", d=128), in0=bv.rearrange("p (h d) -> p h d", d=128),
                                                          in1=vw(ss[:], [[1, 4], [0, 128]]), op=ALU.mult), reads=[buv, bscr], writes=[bscr])
                    S.op("pool", lambda e: e.tensor_tensor(out=vnb[:].rearrange("p g d -> p (g d)"), in0=tn[:, 0:512], in1=SG[:], op=ALU.mult),
                         reads=[bscr, bC], writes=[bvnb])
                    for g in range(4):
                        S.op("pe", lambda e, g=g: e.matmul(pG[:, g * 128:(g + 1) * 128], lhsT=WST[:, g, :], rhs=vnb[:, g, :], start=True, stop=True),
                             reads=[bC, bvnb], writes=[bpG])
                    for g in range(4):
                        S.op("dve", lambda e, g=g: e.scalar_tensor_tensor(
                            out=tn[:, 1024 + g * 128:1024 + (g + 1) * 128], in0=pG[:, g * 128:(g + 1) * 128], scalar=BST[:, g:g + 1],
                            in1=uv[:, g * 128:(g + 1) * 128], op0=ALU.add, op1=ALU.mult), reads=[bpG, bC, buv], writes=[bscr])
                    S.op("pool", lambda e, r=r: e.tensor_tensor(out=mixtok[:, 1536:2048], in0=tn[:, 1024:1536], in1=r[:, 1536:2048], op=ALU.mult),
                         reads=[bscr, br], writes=[bmix])
                    for hg in range(4):
                        p = pT[hg % 2]
                        for hh in range(4):
                            k_ = hg * 4 + hh
                            S.op("pe", lambda e, p=p, hh=hh, k_=k_: e.transpose(out=p[:, hh, :], in_=mixtok[:, k_ * 128:(k_ + 1) * 128], identity=identb[:]),
                                 reads=[bmix, bC], writes=[bpT[hg % 2]])
                        S.op("dve", lambda e, p=p, hg=hg, t=t: e.tensor_copy(out=AT[:, hg * 4:(hg + 1) * 4, t * 128:(t + 1) * 128], in_=p[:]),
                             reads=[bpT[hg % 2]], writes=[bAT[hg * 4 + i_] for i_ in range(4)])

                NCH = len(chunks)
                qprep_tile(0, 0)
                qprep_tile(0, 1)
                units = [(ci, h) for ci in range(NCH) for h in range(12)]
                for f in qk_ops(*units[0]):
                    f()
                for ui, (ci, h) in enumerate(units):
                    pv = pv_ops(ci, h)
                    qk = qk_ops(*units[ui + 1]) if ui + 1 < len(units) else []
                    nqk, npv = len(qk), len(pv)
                    a = b = 0
                    while a < nqk or b < npv:
                        if a < nqk and a * npv <= b * nqk:
                            qk[a]()
                            a += 1
                        else:
                            pv[b]()
                            b += 1
                    if ci > 0 and h == 0:
                        gate_tile(ci - 1, 0)
                    if ci > 0 and h == 3:
                        gate_tile(ci - 1, 1)
                    if ci + 1 < NCH and h == 5:
                        qprep_tile(ci + 1, 0)
                    if ci + 1 < NCH and h == 8:
                        qprep_tile(ci + 1, 1)
                gate_tile(NCH - 1, 0)
                gate_tile(NCH - 1, 1)
                S.barrier()

        def out_proj(l, Wout, last):
            AT = ATS["AT"]
            with ExitStack() as ph:
                wo = [sb(ph, "wo%d" % i, [128, KT, 512], BF16) for i in range(2)]
                xa = [sb(ph, "ox%d" % i, [128, T], F32) for i in range(2)]
                ost = [sb(ph, "ost%d" % i, [128, 4, 128], F32) for i in range(2)]
                pp = [ps(ph, "ops%d" % i, [128, 512]) for i in range(5)]
                ptr = [ps(ph, "optr%d" % i, [128, 4, 128]) for i in range(2)]
                bw = [Buf(), Buf()]
                bx = [Buf(), Buf()]
                bost = [Buf(), Buf()]
                bpp = [Buf() for _ in range(5)]
                bptr = [Buf(), Buf()]
                bout = Buf()
                oi = 0
                do_stats = (not last) and ("stats" not in skip)
                if do_stats:
                    acc = sb(ph, "oacc", [128, T], F32)
                    sqt = sb(ph, "osqt", [128, T], F32)
                    onesf = sb(ph, "onesf", [128, 128], F32)
                    bacc, bsqt, bof = Buf(), Buf(), Buf()
                    S.op("pool", lambda e: e.memset(onesf[:], 1.0), writes=[bof])
                job = ModsJob(l + 1, ph) if (l + 1 < nlayers and "mods" not in skip) else None
                S.dma("poolq", wo[0][:], Wout[:, 0:512].rearrange("(kt p) n -> p kt n", p=128), writes=[bw[0]])
                if job:
                    job.dma(0)
                    job.dma(1)
                for mg in range(4):
                    w = wo[mg % 2]
                    if mg + 1 < 4:
                        S.dma("poolq", wo[(mg + 1) % 2][:], Wout[:, (mg + 1) * 512:(mg + 2) * 512].rearrange("(kt p) n -> p kt n", p=128), writes=[bw[(mg + 1) % 2]])
                    for m in range(4):
                        mi = mg * 4 + m
                        if job:
                            job.dma(2 * mi + 2)
                            job.dma(2 * mi + 3)
                        x_ = xa[mi % 2]
                        bx_ = bx[mi % 2]
                        S.dma("sp", x_[:], xT[mi * 128:(mi + 1) * 128, :], reads=[bxT[mi]], writes=[bx_])
                        for kt in range(KT):
                            for c, (c0, cl) in enumerate(CH5):
                                S.op("pe", lambda e, c=c, c0=c0, cl=cl, kt=kt, w=w, m=m: e.matmul(
                                    pp[c][:, 0:cl], lhsT=w[:, kt, m * 128:(m + 1) * 128], rhs=AT[:, kt, c0:c0 + cl],
                                    start=(kt == 0), stop=(kt == KT - 1)), reads=[bw[mg % 2], bAT[kt]], writes=[bpp[c]])
                        for c, (c0, cl) in enumerate(CH5):
                            segs = [(0, CTXL, 1), (CTXL, 512, 0)] if c == 0 else [(0, cl, 0)]
                            for (a0, a1, r) in segs:
                                S.op("dve", lambda e, c=c, c0=c0, a0=a0, a1=a1, r=r, x_=x_, mi=mi: e.scalar_tensor_tensor(
                                    out=x_[:, c0 + a0:c0 + a1], in0=pp[c][:, a0:a1], scalar=MODS[:, l, 32 + mi, r:r + 1],
                                    in1=x_[:, c0 + a0:c0 + a1], op0=ALU.mult, op1=ALU.add), reads=[bpp[c], bx_, bMODS], writes=[bx_])
                        if job:
                            job.compute(2 * mi)
                            job.compute(2 * mi + 1)
                        if do_stats:
                            if mi == 0:
                                S.op("act", lambda e, x_=x_: e.activation(out=acc[:], in_=x_[:], func=AF.Square), reads=[bx_], writes=[bacc])
                            else:
                                S.op("act", lambda e, x_=x_: e.activation(out=sqt[:], in_=x_[:], func=AF.Square), reads=[bx_], writes=[bsqt])
                                S.op("pool", lambda e: e.tensor_tensor(out=acc[:], in0=acc[:], in1=sqt[:], op=ALU.add), reads=[bacc, bsqt], writes=[bacc])
                        if not last:
                            S.dma("sp", xT[mi * 128:(mi + 1) * 128, :], x_[:], reads=[bx_], writes=[bxT[mi]])
                        else:
                            for tg in range(4):
                                p = ptr[oi % 2]
                                o_ = ost[oi % 2]
                                bp_, bo_ = bptr[oi % 2], bost[oi % 2]
                                oi += 1
                                for i in range(4):
                                    tt = tg * 4 + i
                                    S.op("pe", lambda e, p=p, i=i, tt=tt, x_=x_: e.transpose(
                                        out=p[:, i, :], in_=x_[:, CTXL + tt * 128:CTXL + (tt + 1) * 128], identity=identf[:]),
                                        reads=[bx_, bC], writes=[bp_])
                                S.op("act", lambda e, p=p, o_=o_: e.copy(out=o_[:], in_=p[:]), reads=[bp_], writes=[bo_])
                                S.dma("sp", out.rearrange("(t p) f -> p t f", p=128)[:, tg * 4:(tg + 1) * 4, mi * 128:(mi + 1) * 128], o_[:],
                                      reads=[bo_], writes=[bout])
                if do_stats:
                    for c, (c0, cl) in enumerate(CH5):
                        S.op("pe", lambda e, c=c, c0=c0, cl=cl: e.matmul(pp[c][:, 0:cl], lhsT=onesf[:], rhs=acc[:, c0:c0 + cl], start=True, stop=True),
                             reads=[bof, bacc], writes=[bpp[c]])
                        S.op("act", lambda e, c=c, c0=c0, cl=cl: e.activation(out=sqt[:, c0:c0 + cl], in_=pp[c][:, 0:cl], func=AF.Sqrt, bias=epsT[:], scale=1.0 / D),
                             reads=[bpp[c], bC], writes=[bsqt])
                    S.op("dve", lambda e: e.reciprocal(out=sqt[0:1, :], in_=sqt[0:1, :]), reads=[bsqt], writes=[bsqt])
                    S.dma("sp", RSTDd[:, :], sqt[0:1, :], reads=[bsqt], writes=[bRd])
                    have_rstd[0] = True
                S.barrier()

        def gemm_ch(W, ncols, Z):
            AT = ATS["AT"]
            with ExitStack() as ph:
                wc = [sb(ph, "cw%d" % i, [128, KT, 512], BF16) for i in range(2)]
                st = [sb(ph, "cst%d" % i, [128, T], F32) for i in range(2)]
                pp = [ps(ph, "cps%d" % i, [128, 512]) for i in range(5)]
                bw = [Buf(), Buf()]
                bst = [Buf(), Buf()]
                bpp = [Buf() for _ in range(5)]
                bz = Buf()
                for mg in range(ncols // 512):
                    w = wc[mg % 2]
                    S.dma("poolq", w[:], W[:, mg * 512:(mg + 1) * 512].rearrange("(kt p) n -> p kt n", p=128), writes=[bw[mg % 2]])
                    for m in range(4):
                        mi = mg * 4 + m
                        s_ = st[mi % 2]
                        for kt in range(KT):
                            for c, (c0, cl) in enumerate(CH5):
                                S.op("pe", lambda e, c=c, c0=c0, cl=cl, kt=kt, w=w, m=m: e.matmul(
                                    pp[c][:, 0:cl], lhsT=w[:, kt, m * 128:(m + 1) * 128], rhs=AT[:, kt, c0:c0 + cl],
                                    start=(kt == 0), stop=(kt == KT - 1)), reads=[bw[mg % 2], bAT[kt]], writes=[bpp[c]])
                        for c, (c0, cl) in enumerate(CH5):
                            if c % 2 == 0:
                                S.op("dve", lambda e, c=c, c0=c0, cl=cl, s_=s_: e.tensor_copy(out=s_[:, c0:c0 + cl], in_=pp[c][:, 0:cl]), reads=[bpp[c]], writes=[bst[mi % 2]])
                            else:
                                S.op("act", lambda e, c=c, c0=c0, cl=cl, s_=s_: e.copy(out=s_[:, c0:c0 + cl], in_=pp[c][:, 0:cl]), reads=[bpp[c]], writes=[bst[mi % 2]])
                        S.dma("sp", Z[mi * 128:(mi + 1) * 128, :], s_[:], reads=[bst[mi % 2]], writes=[bz])
                S.barrier()

        def s5_phase(j):
            GB = 8
            NC8 = 288
            bzu = Buf()
            with ExitStack() as ph0:
                ut = [sb(ph0, "ut%d" % i, [128, T], F32) for i in range(2)]
                ub = [sb(ph0, "ub%d" % i, [128, 8, NC8], BF16) for i in range(2)]
                but = [Buf(), Buf()]
                bub = [Buf(), Buf()]
                for m in range(8):
                    i = m % 2
                    S.dma("sp", ut[i][:], ZO[m * 128:(m + 1) * 128, :], writes=[but[i]])
                    S.op("dve" if i == 0 else "pool", lambda e, i=i: e.tensor_copy(out=ub[i][:], in_=ut[i][:].rearrange("p (c s) -> p s c", s=8)),
                         reads=[but[i]], writes=[bub[i]])
                    S.dma("sp", ZU[m * 128:(m + 1) * 128, :], ub[i][:].rearrange("p s c -> p (s c)"), reads=[bub[i]], writes=[bzu])

                S.barrier()
            with ExitStack() as ph:
                phs = ExitStack()
                TT = lambda en, o, a, b, op, rd, wr: S.op(en, lambda e: e.tensor_tensor(out=o, in0=a, in1=b, op=op), reads=rd, writes=wr)
                PW = sb(ph, "PW", [128, 2, 9, 64], F32)
                PN = sb(ph, "PN", [128, 2, 8, 64], F32)
                SELW = sb(ph, "SELW", [128, 2, 8, 64], F32)
                SELC = sb(ph, "SELC", [128, 2, 8, 64], F32)
                SELR = sb(ph, "SELR", [128, 2, 8, 64], F32)
                BB = sb(ph, "BB", [128, 2, 64, 16], F32)
                CC = sb(ph, "CC", [128, 2, 64, 16], F32)
                A1 = sb(ph, "A1", [128, 2, 64], F32)
                A2 = sb(ph, "A2", [128, 2, 64], F32)
                MASK = sb(ph, "MASK", [128, 2, 128], F32)
                JS = 8
                NS = NC8 // JS
                AJ1 = sb(ph, "AJ1", [128, 2, 64], F32)
                AJ2 = sb(ph, "AJ2", [128, 2, 64], F32)
                J2 = 6
                NS2 = NS // J2
                AK1 = sb(ph, "AK1", [128, 2, 64], F32)
                AK2 = sb(ph, "AK2", [128, 2, 64], F32)
                pq = [ps(ph, "s5p%d" % i, [128, 512]) for i in range(4)]
                px = [ps(ph, "s5x%d" % i, [128, 512]) for i in range(2)]
                py = [ps(ph, "s5y%d" % i, [128, 512]) for i in range(2)]
                bS = Buf()
                bpq = [Buf() for _ in range(4)]
                bpx = [Buf(), Buf()]
                bpy = [Buf(), Buf()]
                L0 = sb(phs, "L0", [64, 2, 2, 64], F32)
                LR = sb(phs, "LR", [128, 64], F32)
                LI = sb(phs, "LI", [128, 64], F32)
                DT = sb(phs, "DT", [128, 64], F32)
                TM = sb(phs, "TM", [128, 24, 64], F32)
                BB0 = sb(phs, "BB0", [128, 2, 64, 16], F32)
                Cn = sb(phs, "Cn", [128, 2, 8, 2, 64], F32)
                E8 = sb(phs, "E8", [8, 128], F32)
                G8f = sb(phs, "G8f", [8, 128], F32)
                G8b = sb(phs, "G8b", [8, 128], F32)
                w1t = sb(phs, "s5w1", [128, 1024], F32)
                w2t = sb(phs, "s5w2", [128, 1024], F32)

                for ri, nm in enumerate(("s5_lam_re", "s5_lam_im")):
                    S.dma("sp", L0[:, ri, :, :], od[nm][j].rearrange("d g n -> g d n"), writes=[bS])
                for d in range(2):
                    ld = od["s5_log_dt"][j, d:d + 1, :]
                    S.dma("sp", DT[64 * d:64 * d + 64, :], bass.AP(tensor=ld.tensor, offset=ld.offset, ap=[[0, 64], [1, 64]]), writes=[bS])
                    for ri, nm in enumerate(("s5_b_re", "s5_b_im")):
                        S.dma("sp", BB0[64 * d:64 * d + 64, ri, :, :], od[nm][j, d].rearrange("g n q -> n g q"), writes=[bS])
                    for ri, nm in enumerate(("s5_c_re", "s5_c_im")):
                        S.dma("sp", Cn[:, ri, :, d, :], od[nm][j, d].rearrange("(gb g8) p n -> (g8 p) gb n", g8=8), writes=[bS])
                for ri, dst in enumerate((LR, LI)):
                    S.op("pe", lambda e, ri=ri: e.transpose(out=pq[0][:, ri * 64:(ri + 1) * 64], in_=L0[:, ri, :, :].rearrange("g d n -> g (d n)"), identity=identf[0:64, 0:64]),
                         reads=[bS, bC], writes=[bpq[0]])
                S.op("dve", lambda e: e.tensor_copy(out=LR[:], in_=pq[0][:, 0:64]), reads=[bpq[0]], writes=[bS])
                S.op("dve", lambda e: e.tensor_copy(out=LI[:], in_=pq[0][:, 64:128]), reads=[bpq[0]], writes=[bS])
                pi_ = 1
                for ri in range(2):
                    for gb in range(8):
                        p_ = pq[pi_ % 4]
                        bp_ = bpq[pi_ % 4]
                        pi_ += 1
                        S.op("pe", lambda e, p_=p_, ri=ri, gb=gb: e.transpose(out=p_[:, 0:128], in_=Cn[:, ri, gb, :, :].rearrange("p d n -> p (d n)"), identity=identf[:]),
                             reads=[bS, bC], writes=[bp_])
                        S.op("dve", lambda e, p_=p_, ri=ri, gb=gb: e.tensor_copy(out=CC[:, ri, gb * 8:(gb + 1) * 8, :].rearrange("p g q -> p (g q)"), in_=p_[:, 0:128]),
                             reads=[bp_], writes=[bS])

                tm = lambda k: TM[:, k, :]
                V = lambda o, a, b, op: TT("dve", o, a, b, op, [bS], [bS])
                S.op("act", lambda e: e.activation(out=DT[:], in_=DT[:], func=AF.Exp), reads=[bS], writes=[bS])
                V(tm(0), LR[:], DT[:], ALU.mult)
                V(tm(1), LI[:], DT[:], ALU.mult)
                S.op("act", lambda e: e.activation(out=tm(2), in_=tm(0), func=AF.Exp), reads=[bS], writes=[bS])
                S.op("act", lambda e: e.activation(out=tm(3), in_=tm(1), func=AF.Sin, scale=1.0 / 16), reads=[bS], writes=[bS])
                S.op("act", lambda e: e.activation(out=tm(5), in_=tm(1), func=AF.Sin, scale=1.0 / 8), reads=[bS], writes=[bS])
                V(tm(4), tm(3), tm(3), ALU.mult)
                S.op("dve", lambda e: e.tensor_scalar(out=tm(4), in0=tm(4), scalar1=-2.0, scalar2=1.0, op0=ALU.mult, op1=ALU.add), reads=[bS], writes=[bS])
                for _ in range(3):
                    V(tm(6), tm(4), tm(4), ALU.mult)
                    V(tm(7), tm(5), tm(5), ALU.mult)
                    V(tm(8), tm(4), tm(5), ALU.mult)
                    V(tm(4), tm(6), tm(7), ALU.subtract)
                    S.op("dve", lambda e: e.tensor_scalar(out=tm(5), in0=tm(8), scalar1=2.0, scalar2=None, op0=ALU.mult), reads=[bS], writes=[bS])
                ar, ai = PW[:, 0, 1, :], PW[:, 1, 1, :]
                V(ar, tm(2), tm(4), ALU.mult)
                V(ai, tm(2), tm(5), ALU.mult)
                S.op("dve", lambda e: e.memset(PW[:, 0, 0, :], 1.0), reads=[], writes=[bS])
                S.op("dve", lambda e: e.memset(PW[:, 1, 0, :], 0.0), reads=[], writes=[bS])
                S.op("dve", lambda e: e.memset(PN[:, 0, 0, :], 1.0), reads=[], writes=[bS])
                S.op("dve", lambda e: e.memset(PN[:, 1, 0, :], 0.0), reads=[], writes=[bS])

                def cmul(or_, oi_, xr, xi, yr, yi):
                    V(tm(9), xr, yr, ALU.mult)
                    V(tm(10), xi, yi, ALU.mult)
                    V(tm(11), xr, yi, ALU.mult)
                    V(tm(12), xi, yr, ALU.mult)
                    V(or_, tm(9), tm(10), ALU.subtract)
                    V(oi_, tm(11), tm(12), ALU.add)
                for k in range(2, 9):
                    cmul(PW[:, 0, k, :], PW[:, 1, k, :], PW[:, 0, k - 1, :], PW[:, 1, k - 1, :], ar, ai)
                V(tm(13), tm(2), tm(2), ALU.mult)
                S.op("dve", lambda e: e.reciprocal(out=tm(13), in_=tm(13)), reads=[bS], writes=[bS])
                V(PN[:, 0, 1, :], ar, tm(13), ALU.mult)
                V(PN[:, 1, 1, :], ai, tm(13), ALU.mult)
                S.op("dve", lambda e: e.tensor_scalar(out=PN[:, 1, 1, :], in0=PN[:, 1, 1, :], scalar1=-1.0, scalar2=None, op0=ALU.mult), reads=[bS], writes=[bS])
                for k in range(2, 8):
                    cmul(PN[:, 0, k, :], PN[:, 1, k, :], PN[:, 0, k - 1, :], PN[:, 1, k - 1, :], PN[:, 0, 1, :], PN[:, 1, 1, :])
                V(tm(14), LR[:], LR[:], ALU.mult)
                V(tm(15), LI[:], LI[:], ALU.mult)
                V(tm(14), tm(14), tm(15), ALU.add)
                S.op("dve", lambda e: e.reciprocal(out=tm(14), in_=tm(14)), reads=[bS], writes=[bS])
                S.op("dve", lambda e: e.tensor_scalar(out=tm(15), in0=ar, scalar1=-1.0, scalar2=None, op0=ALU.add), reads=[bS], writes=[bS])
                V(tm(16), tm(15), LR[:], ALU.mult)
                V(tm(17), ai, LI[:], ALU.mult)
                V(tm(16), tm(16), tm(17), ALU.add)
                V(tm(18), ai, LR[:], ALU.mult)
                V(tm(19), tm(15), LI[:], ALU.mult)
                V(tm(18), tm(18), tm(19), ALU.subtract)
                V(tm(16), tm(16), tm(14), ALU.mult)
                V(tm(18), tm(18), tm(14), ALU.mult)
                fr = vw(tm(16), [[1, 64], [0, 16]])
                fi = vw(tm(18), [[1, 64], [0, 16]])
                b0r, b0i = BB0[:, 0, :, :], BB0[:, 1, :, :]
                w1 = w1t[:].rearrange("p (g q) -> p g q", q=16)
                w2 = w2t[:].rearrange("p (g q) -> p g q", q=16)
                V(w1, b0r, fr, ALU.mult)
                V(w2, b0i, fi, ALU.mult)
                V(BB[:, 0, :, :], w1, w2, ALU.subtract)
                V(w1, b0r, fi, ALU.mult)
                V(w2, b0i, fr, ALU.mult)
                V(BB[:, 1, :, :], w1, w2, ALU.add)
                for ri in range(2):
                    for s_ in range(8):
                        for (lo, hi, fwd) in ((0, 64, True), (64, 128, False)):
                            kw_ = 7 - s_ if fwd else s_
                            kc_ = 7 - s_ if fwd else s_
                            kr_ = s_ + 1 if fwd else 8 - s_
                            S.op("pool", lambda e, lo=lo, hi=hi, ri=ri, s_=s_, kw_=kw_: e.tensor_copy(out=SELW[lo:hi, ri, s_, :], in_=PW[lo:hi, ri, kw_, :]), reads=[bS], writes=[bS])
                            S.op("dve", lambda e, lo=lo, hi=hi, ri=ri, s_=s_, kc_=kc_: e.tensor_copy(out=SELC[lo:hi, ri, s_, :], in_=PN[lo:hi, ri, kc_, :]), reads=[bS], writes=[bS])
                            S.op("act", lambda e, lo=lo, hi=hi, ri=ri, s_=s_, kr_=kr_: e.copy(out=SELR[lo:hi, ri, s_, :], in_=PW[lo:hi, ri, kr_, :]), reads=[bS], writes=[bS])
                for tl_ in (SELC, SELR):
                    S.op("pool", lambda e, tl_=tl_: e.tensor_scalar(out=tl_[:, 1, :, :], in0=tl_[:, 1, :, :], scalar1=-1.0, scalar2=None, op0=ALU.mult), reads=[bS], writes=[bS])
                S.op("pool", lambda e: e.tensor_scalar(out=CC[:, 1, :, :], in0=CC[:, 1, :, :], scalar1=-1.0, scalar2=None, op0=ALU.mult), reads=[bS], writes=[bS])
                S.op("dve", lambda e: e.tensor_copy(out=A1[:, 0, :], in_=PW[:, 0, 8, :]), reads=[bS], writes=[bS])
                S.op("dve", lambda e: e.tensor_copy(out=A1[:, 1, :], in_=PW[:, 0, 8, :]), reads=[bS], writes=[bS])
                S.op("dve", lambda e: e.tensor_scalar(out=A2[:, 0, :], in0=PW[:, 1, 8, :], scalar1=-1.0, scalar2=None, op0=ALU.mult), reads=[bS], writes=[bS])
                S.op("dve", lambda e: e.tensor_copy(out=A2[:, 1, :], in_=PW[:, 1, 8, :]), reads=[bS], writes=[bS])
                S.op("dve", lambda e: e.tensor_copy(out=tm(20), in_=PW[:, 0, 8, :]), reads=[bS], writes=[bS])
                S.op("dve", lambda e: e.tensor_copy(out=tm(21), in_=PW[:, 1, 8, :]), reads=[bS], writes=[bS])
                for it_ in range(3):
                    V(tm(22), tm(20), tm(20), ALU.mult)
                    V(tm(23), tm(21), tm(21), ALU.mult)
                    V(tm(19), tm(20), tm(21), ALU.mult)
                    V(tm(20), tm(22), tm(23), ALU.subtract)
                    S.op("dve", lambda e: e.tensor_scalar(out=tm(21), in0=tm(19), scalar1=2.0, scalar2=None, op0=ALU.mult), reads=[bS], writes=[bS])
                S.op("dve", lambda e: e.tensor_copy(out=AJ1[:, 0, :], in_=tm(20)), reads=[bS], writes=[bS])
                S.op("dve", lambda e: e.tensor_copy(out=AJ1[:, 1, :], in_=tm(20)), reads=[bS], writes=[bS])
                S.op("dve", lambda e: e.tensor_scalar(out=AJ2[:, 0, :], in0=tm(21), scalar1=-1.0, scalar2=None, op0=ALU.mult), reads=[bS], writes=[bS])
                S.op("dve", lambda e: e.tensor_copy(out=AJ2[:, 1, :], in_=tm(21)), reads=[bS], writes=[bS])
                cmul(tm(14), tm(15), tm(20), tm(21), tm(20), tm(21))
                cmul(tm(16), tm(17), tm(14), tm(15), tm(14), tm(15))
                cmul(tm(18), tm(19), tm(16), tm(17), tm(14), tm(15))
                S.op("dve", lambda e: e.tensor_copy(out=AK1[:, 0, :], in_=tm(18)), reads=[bS], writes=[bS])
                S.op("dve", lambda e: e.tensor_copy(out=AK1[:, 1, :], in_=tm(18)), reads=[bS], writes=[bS])
                S.op("dve", lambda e: e.tensor_scalar(out=AK2[:, 0, :], in0=tm(19), scalar1=-1.0, scalar2=None, op0=ALU.mult), reads=[bS], writes=[bS])
                S.op("dve", lambda e: e.tensor_copy(out=AK2[:, 1, :], in_=tm(19)), reads=[bS], writes=[bS])
                for tl_, fillcmp in ((E8, None), (G8f, None), (G8b, None)):
                    S.op("pool", lambda e, tl_=tl_: e.memset(tl_[:], 1.0), reads=[], writes=[bS])
                S.op("pool", lambda e: e.affine_select(out=E8[:], in_=E8[:], pattern=[[1, 128]], compare_op=ALU.is_ge, fill=0.0, base=0, channel_multiplier=-16), reads=[bS], writes=[bS])
                S.op("pool", lambda e: e.affine_select(out=E8[:], in_=E8[:], pattern=[[-1, 128]], compare_op=ALU.is_ge, fill=0.0, base=15, channel_multiplier=16), reads=[bS], writes=[bS])
                S.op("pool", lambda e: e.affine_select(out=G8f[:], in_=G8f[:], pattern=[[1, 128]], compare_op=ALU.is_ge, fill=0.0, base=0, channel_multiplier=-16), reads=[bS], writes=[bS])
                S.op("pool", lambda e: e.affine_select(out=G8b[:], in_=G8b[:], pattern=[[-1, 128]], compare_op=ALU.is_ge, fill=0.0, base=15, channel_multiplier=16), reads=[bS], writes=[bS])
                for d, G_ in enumerate((G8f, G8b)):
                    S.op("pe", lambda e, d=d, G_=G_: e.matmul(pq[d][:, 0:128], lhsT=E8[:], rhs=G_[:], start=True, stop=True), reads=[bS], writes=[bpq[d]])
                    S.op("dve", lambda e, d=d: e.tensor_copy(out=MASK[:, d, :], in_=pq[d][:, 0:128]), reads=[bpq[d]], writes=[bS])

                def prod_tab(dstr, dsti, SEL, SRC, g0, neg_im, bdst):
                    def sv(ri):
                        a = SEL[:, ri, :, :]
                        return vw(a, [[1, GB], [64, 8], [0, 16]], off=g0)
                    def bv(ri):
                        a = SRC[:, ri, :, :]
                        return vw(a, [[16, GB], [0, 8], [1, 16]], off=g0 * 16)
                    o4 = lambda a: a.rearrange("p g (s q) -> p g s q", q=16)
                    T1, T2 = o4(t1[:]), o4(t2[:])
                    TT("dve", T1, sv(0), bv(0), ALU.mult, [bS], [bt1])
                    TT("pool", T2, sv(1), bv(1), ALU.mult, [bS], [bt2])
                    TT("dve", o4(dstr), T1, T2, ALU.subtract, [bt1, bt2], [bdst])
                    TT("dve", T1, sv(0), bv(1), ALU.mult, [bS, bdst], [bt1])
                    TT("pool", T2, sv(1), bv(0), ALU.mult, [bS, bdst], [bt2])
                    if neg_im:
                        S.op("dve", lambda e: e.scalar_tensor_tensor(out=dsti, in0=t1[:], scalar=-1.0, in1=t2[:], op0=ALU.mult, op1=ALU.subtract),
                             reads=[bt1, bt2], writes=[bdst])
                    else:
                        TT("dve", o4(dsti), T1, T2, ALU.add, [bt1, bt2], [bdst])

                S.barrier()
                phs.close()
                WINtab = sb(ph, "WINtab", [128, 2, GB, 128], F32)
                CT7tab = sb(ph, "CT7tab", [128, 2, GB, 128], F32)
                ROUTtab = sb(ph, "ROUTtab", [128, 2, GB, 128], BF16)
                WIND = [sb(ph, "WIND%d" % d, [128, 2, GB, 128], F32) for d in range(2)]
                ROUTDs = [[sb(ph, "ROUTD%d_%d" % (d, i), [128, 2, GB, 128], BF16) for d in range(2)] for i in range(2)]
                t1 = sb(ph, "s5t1", [128, GB, 128], F32)
                t2 = sb(ph, "s5t2", [128, GB, 128], F32)
                WINTs = [sb(ph, "WINT%d" % i, [128, GB, 2, 128], BF16) for i in range(2)]
                TOEPs = [sb(ph, "TOEP%d" % i, [128, GB, 2, 128], BF16) for i in range(2)]
                U8bs = [sb(ph, "U8b%d" % i, [128, GB, NC8], BF16) for i in range(2)]
                X = sb(ph, "X", [128, 2, GB, NC8], F32)
                Hs = sb(ph, "Hs", [128, 2, GB, NC8], BF16)
                Hb = sb(ph, "Hb", [128, 2, GB, NC8], BF16)
                Hc = [sb(ph, "Hc%d" % i, [128, 2, GB], F32) for i in range(2)]
                XA = [sb(ph, "XA%d" % i, [128, 2, GB, NS], F32) for i in range(3)]
                PB = sb(ph, "PB", [128, 2, GB, NS], F32)
                QB = sb(ph, "QB", [128, 2, GB, NS], F32)
                GH = sb(ph, "GH", [128, 2, GB, NS], F32)
                bXA = [Buf(), Buf(), Buf()]
                bGH = Buf()
                XB = [sb(ph, "XB%d" % i, [128, 2, GB, NS2], F32) for i in range(2)]
                PB2 = sb(ph, "PB2", [128, 2, GB, NS2], F32)
                QB2 = sb(ph, "QB2", [128, 2, GB, NS2], F32)
                G2 = sb(ph, "G2", [128, 2, GB, NS2], F32)
                bXB = [Buf(), Buf()]
                bG2 = Buf()
                Pq = sb(ph, "Pq", [128, 2, GB], F32)
                Qq = sb(ph, "Qq", [128, 2, GB], F32)
                bWt, bCt, bRt, bt1, bt2, bWINT, bTOEP, bU8, bX, bHs, bHb, bY, bzy = (Buf() for _ in range(13))
                bHc = [Buf(), Buf()]
                bPq, bQq = Buf(), Buf()

                bWD = Buf()
                bRDs, bWINTs, bTOEPs, bU8s = [Buf(), Buf()], [Buf(), Buf()], [Buf(), Buf()], [Buf(), Buf()]
                for d in range(2):
                    S.op("pool", lambda e, d=d: e.memset(WIND[d][:], 0.0), reads=[], writes=[bWD])
                    for i_ in range(2):
                        S.op("pool", lambda e, d=d, i_=i_: e.memset(ROUTDs[i_][d][:], 0.0), reads=[], writes=[bRDs[i_]])
                S.op("pool", lambda e: e.memset(Hb[:], 0.0), reads=[], writes=[bHb])
                Yv = X[:, 0, :, :]

                def stageA(gbi):
                    g0 = gbi * GB
                    par = gbi % 2
                    prod_tab(WINtab[:, 0, :, :], WINtab[:, 1, :, :], SELW, BB, g0, False, bWt)
                    prod_tab(CT7tab[:, 0, :, :], CT7tab[:, 1, :, :], SELC, CC, g0, False, bCt)
                    prod_tab(ROUTtab[:, 0, :, :], ROUTtab[:, 1, :, :], SELR, CC, g0, False, bRt)
                    for d in range(2):
                        lo, hi = 64 * d, 64 * d + 64
                        S.op("act", lambda e, d=d, lo=lo, hi=hi: e.copy(out=WIND[d][lo:hi, :, :, :], in_=WINtab[lo:hi, :, :, :]), reads=[bWt], writes=[bWD])
                        S.op("act", lambda e, d=d, lo=lo, hi=hi: e.copy(out=ROUTDs[par][d][lo:hi, :, :, :], in_=ROUTtab[lo:hi, :, :, :]), reads=[bRt], writes=[bRDs[par]])
                    for s_ in range(8):
                        src = ZU[g0 * 16:(g0 + GB) * 16, s_ * NC8:(s_ + 1) * NC8].rearrange("(g q) c -> q g c", q=16)
                        S.dma("sp", U8bs[par][16 * s_:16 * s_ + 16, :, :], src, reads=[bzu], writes=[bU8s[par]])
                    for g in range(GB):
                        p_ = pq[g % 2]
                        bp_ = bpq[g % 2]
                        for ri in range(2):
                            S.op("pe", lambda e, p_=p_, ri=ri, g=g: e.transpose(out=p_[:, ri * 128:(ri + 1) * 128], in_=WINtab[:, ri, g, :], identity=identf[:]),
                                 reads=[bWt, bC], writes=[bp_])
                        S.op("act", lambda e, p_=p_, g=g: e.copy(out=WINTs[par][:, g, :, :].rearrange("p r n -> p (r n)"), in_=p_[:, 0:256]), reads=[bp_], writes=[bWINTs[par]])
                        p2 = pq[2 + g % 2]
                        bp2 = bpq[2 + g % 2]
                        for d in range(2):
                            lo, hi = 64 * d, 64 * d + 64
                            S.op("pe", lambda e, p2=p2, d=d, g=g: e.matmul(p2[:, d * 128:(d + 1) * 128], lhsT=WIND[d][:, 0, g, :], rhs=CT7tab[:, 0, g, :], start=True, stop=False),
                                 reads=[bWD, bCt], writes=[bp2])
                            S.op("pe", lambda e, p2=p2, d=d, g=g: e.matmul(p2[:, d * 128:(d + 1) * 128], lhsT=WIND[d][:, 1, g, :], rhs=CT7tab[:, 1, g, :], start=False, stop=True),
                                 reads=[bWD, bCt], writes=[bp2])
                        S.op("dve", lambda e, p2=p2, g=g: e.tensor_tensor(out=TOEPs[par][:, g, :, :].rearrange("p d n -> p (d n)"), in0=p2[:, 0:256], in1=MASK[:].rearrange("p d n -> p (d n)"), op=ALU.mult),
                             reads=[bp2, bS], writes=[bTOEPs[par]])

                def stageB(gbi):
                    g0 = gbi * GB
                    par = gbi % 2
                    xi_ = 0
                    for g in range(GB):
                        for ri in range(2):
                            p_ = px[xi_ % 2]
                            bp_ = bpx[xi_ % 2]
                            xi_ += 1
                            S.op("pe", lambda e, p_=p_, g=g, ri=ri: e.matmul(p_[:, 0:NC8], lhsT=WINTs[par][:, g, ri, :], rhs=U8bs[par][:, g, :], start=True, stop=True),
                                 reads=[bWINTs[par], bU8s[par]], writes=[bp_])
                            S.op("act", lambda e, p_=p_, g=g, ri=ri: e.copy(out=vw(X[0:64, ri, g, 0:1], [[1, NS], [NS, JS]]),
                                                                           in_=vw(p_[0:64, 0:1], [[JS, NS], [1, JS]])), reads=[bp_], writes=[bX])
                            S.op("dve", lambda e, p_=p_, g=g, ri=ri: e.tensor_copy(out=vw(X[64:128, ri, g, 0:1], [[1, 4], [NS, JS]]),
                                                                                  in_=vw(p_[64:128, 0:1], [[-JS, 4], [-1, JS]], off=31)), reads=[bp_], writes=[bX])
                            S.op("dve", lambda e, p_=p_, g=g, ri=ri: e.tensor_copy(out=vw(X[64:128, ri, g, 0:1], [[1, 32], [NS, JS]], off=4),
                                                                                  in_=vw(p_[64:128, 0:1], [[-JS, 32], [-1, JS]], off=287)), reads=[bp_], writes=[bX])
                    a1b = vw(A1[:, 0, 0:1], [[64, 2], [1, GB], [0, NS]], off=g0)
                    a2b = vw(A2[:, 0, 0:1], [[64, 2], [1, GB], [0, NS]], off=g0)
                    aJ1b = vw(AJ1[:, 0, 0:1], [[64, 2], [1, GB], [0, NS2]], off=g0)
                    aJ2b = vw(AJ2[:, 0, 0:1], [[64, 2], [1, GB], [0, NS2]], off=g0)
                    aK1 = AK1[:, :, g0:g0 + GB]
                    aK2 = AK2[:, :, g0:g0 + GB]
                    x8v = lambda jj: vw(X[:, 0, 0, 0:1], [[GB * NC8, 2], [NC8, GB], [1, NS]], off=jj * NS)
                    hsv = lambda jj: vw(Hs[:, 0, 0, 0:1], [[GB * NC8, 2], [NC8, GB], [JS, NS]], off=jj)
                    swp = lambda t_, n_: vw(t_[:, 0, 0, 0:1], [[-GB * n_, 2], [n_, GB], [1, n_]], off=GB * n_)
                    lv2 = lambda t_, j2: vw(t_[:, 0, 0, 0:1], [[GB * NS, 2], [NS, GB], [J2, NS2]], off=j2)

                    def cmad(dst, bdst, src, bsrc, xin, bxin, m1, m2, P_, Q_, n_):
                        S.op("dve", lambda e: e.tensor_tensor(out=P_[:], in0=src[:], in1=m1, op=ALU.mult), reads=[bsrc, bS], writes=[bPq])
                        S.op("pool", lambda e: e.tensor_tensor(out=Q_[:], in0=swp(src, n_), in1=m2, op=ALU.mult), reads=[bsrc, bS], writes=[bQq])
                        S.op("dve", lambda e: e.tensor_tensor(out=P_[:], in0=P_[:], in1=xin, op=ALU.add), reads=[bPq, bxin], writes=[bPq])
                        S.op("dve", lambda e: e.tensor_tensor(out=dst[:], in0=P_[:], in1=Q_[:], op=ALU.add), reads=[bPq, bQq], writes=[bdst])

                    S.op("dve", lambda e: e.tensor_copy(out=XA[0][:], in_=x8v(0)), reads=[bX], writes=[bXA[0]])
                    for jj in range(1, JS):
                        cmad(XA[jj % 2], bXA[jj % 2], XA[(jj - 1) % 2], bXA[(jj - 1) % 2], x8v(jj), bX, a1b, a2b, PB, QB, NS)
                    XS1, bXS1 = XA[(JS - 1) % 2], bXA[(JS - 1) % 2]
                    S.op("dve", lambda e: e.tensor_copy(out=XB[0][:], in_=lv2(XS1, 0)), reads=[bXS1], writes=[bXB[0]])
                    for j2 in range(1, J2):
                        cmad(XB[j2 % 2], bXB[j2 % 2], XB[(j2 - 1) % 2], bXB[(j2 - 1) % 2], lv2(XS1, j2), bXS1, aJ1b, aJ2b, PB2, QB2, NS2)
                    XS2, bXS2 = XB[(J2 - 1) % 2], bXB[(J2 - 1) % 2]
                    S.op("pool", lambda e: e.memset(G2[:], 0.0), reads=[], writes=[bG2])
                    for C in range(NS2 - 1):
                        cur = G2[:, :, :, C]
                        csw = vw(G2[:, 0, 0, 0:1], [[-GB * NS2, 2], [NS2, GB]], off=GB * NS2 + C)
                        S.op("dve", lambda e, cur=cur: e.tensor_tensor(out=Pq[:], in0=cur, in1=aK1, op=ALU.mult), reads=[bG2, bS], writes=[bPq])
                        S.op("dve", lambda e, csw=csw: e.tensor_tensor(out=Qq[:], in0=csw, in1=aK2, op=ALU.mult), reads=[bG2, bS], writes=[bQq])
                        S.op("dve", lambda e: e.tensor_tensor(out=Pq[:], in0=Pq[:], in1=Qq[:], op=ALU.add), reads=[bPq, bQq], writes=[bPq])
                        S.op("dve", lambda e, C=C: e.tensor_tensor(out=G2[:, :, :, C + 1], in0=Pq[:], in1=XS2[:, :, :, C], op=ALU.add), reads=[bPq, bXS2], writes=[bG2])
                    S.op("dve", lambda e: e.tensor_copy(out=lv2(GH, 0), in_=G2[:]), reads=[bG2], writes=[bGH])
                    gsrc, bgsrc = G2, bG2
                    for j2 in range(J2 - 1):
                        gd_, bgd_ = XB[j2 % 2], bXB[j2 % 2]
                        cmad(gd_, bgd_, gsrc, bgsrc, lv2(XS1, j2), bXS1, aJ1b, aJ2b, PB2, QB2, NS2)
                        S.op("act", lambda e, gd_=gd_, j2=j2: e.copy(out=lv2(GH, j2 + 1), in_=gd_[:]), reads=[bgd_], writes=[bGH])
                        gsrc, bgsrc = gd_, bgd_
                    S.op("act", lambda e: e.copy(out=hsv(0), in_=GH[:]), reads=[bGH], writes=[bHs])
                    hsrc, bhsrc = GH, bGH
                    for jj in range(JS - 1):
                        hd, bhd = XA[jj % 3], bXA[jj % 3]
                        cmad(hd, bhd, hsrc, bhsrc, x8v(jj), bX, a1b, a2b, PB, QB, NS)
                        S.op("act", lambda e, hd=hd, jj=jj: e.copy(out=hsv(jj + 1), in_=hd[:]), reads=[bhd], writes=[bHs])
                        hsrc, bhsrc = hd, bhd
                    S.op("dve", lambda e: e.tensor_copy(out=Hb[64:128, :, :, 0:32].rearrange("p r g c -> p (r g) c"),
                                                         in_=vw(Hs[64:128, 0, 0, 0:1], [[NC8, 2 * GB], [-1, 32]], off=31)), reads=[bHs], writes=[bHb])
                    S.op("dve", lambda e: e.tensor_copy(out=Hb[64:128, :, :, 32:NC8].rearrange("p r g c -> p (r g) c"),
                                                         in_=vw(Hs[64:128, 0, 0, 0:1], [[NC8, 2 * GB], [-1, 256]], off=NC8 - 1)), reads=[bHs], writes=[bHb])
                    for g in range(GB):
                        p_ = py[g % 2]
                        bp_ = bpy[g % 2]
                        for d in range(2):
                            S.op("pe", lambda e, p_=p_, g=g, d=d: e.matmul(p_[:, 0:NC8], lhsT=TOEPs[par][:, g, d, :], rhs=U8bs[par][:, g, :], start=(d == 0), stop=False),
                                 reads=[bTOEPs[par], bU8s[par]], writes=[bp_])
                        for d in range(2):
                            Hsrc = Hs if d == 0 else Hb
                            bH = bHs if d == 0 else bHb
                            for ri in range(2):
                                S.op("pe", lambda e, p_=p_, g=g, d=d, ri=ri, Hsrc=Hsrc: e.matmul(p_[:, 0:NC8], lhsT=ROUTDs[par][d][:, ri, g, :], rhs=Hsrc[:, ri, g, :], start=False, stop=(d == 1 and ri == 1)),
                                     reads=[bRDs[par], bH], writes=[bp_])
                        if g % 2 == 0:
                            S.op("act", lambda e, g=g, p_=p_: e.copy(out=Yv[:, g, :], in_=p_[:, 0:NC8]), reads=[bp_], writes=[bX])
                        else:
                            S.op("dve", lambda e, g=g, p_=p_: e.tensor_copy(out=Yv[:, g, :], in_=p_[:, 0:NC8]), reads=[bp_], writes=[bX])
                    for t_ in range(8):
                        dst = ZY[g0 * 16:(g0 + GB) * 16, t_ * NC8:(t_ + 1) * NC8].rearrange("(g p) c -> p g c", p=16)
                        S.dma("sp", dst, Yv[16 * t_:16 * t_ + 16, :, :], reads=[bX], writes=[bzy])

                NB = 64 // GB
                stageA(0)
                for gbi in range(NB):
                    S.begin_capture()
                    stageB(gbi)
                    lb = S.end_capture()
                    la = []
                    if gbi + 1 < NB:
                        S.begin_capture()
                        stageA(gbi + 1)
                        la = S.end_capture()
                    S.replay(lb, la)
                S.barrier()

        def s5_finish(j):
            AT = ATS["AT"]
            with ExitStack() as ph:
                YT = sb(ph, "YT", [128, 8, T], BF16)
                yz = [sb(ph, "yz%d" % i, [128, T], F32) for i in range(2)]
                uu = [sb(ph, "uu%d" % i, [128, T], F32) for i in range(2)]
                gw = [sb(ph, "gw%d" % i, [128, 8, 128], BF16) for i in range(2)]
                DS = sb(ph, "DS", [128, 8], F32)
                GBI = sb(ph, "GBI", [128, 8], F32)
                pp = [ps(ph, "fps%d" % i, [128, 512]) for i in range(5)]
                byz = [Buf(), Buf()]
                buu = [Buf(), Buf()]
                bgw = [Buf(), Buf()]
                bpp = [Buf() for _ in range(5)]
                bYT = [Buf() for _ in range(8)]
                bK = Buf()
                load_T(DS[:], od["s5_d"][j].rearrange("(m p) -> m p", p=128), 8, pp[0], bpp[0], bK)
                load_T(GBI[:], od["s5_glu_b"][j].rearrange("(m p) -> m p", p=128), 8, pp[0], bpp[0], bK)
                for m in range(8):
                    i = m % 2
                    S.dma("sp", yz[i][:], ZY[m * 128:(m + 1) * 128, :], writes=[byz[i]])
                    S.dma("sp", uu[i][:], ZO[m * 128:(m + 1) * 128, :], writes=[buu[i]])
                    S.op("dve", lambda e, i=i, m=m: e.scalar_tensor_tensor(
                        out=uu[i][:].rearrange("p (c s) -> p c s", s=8), in0=uu[i][:].rearrange("p (c s) -> p c s", s=8), scalar=DS[:, m:m + 1],
                        in1=yz[i][:].rearrange("p (s c) -> p c s", s=8), op0=ALU.mult, op1=ALU.add), reads=[buu[i], byz[i], bK], writes=[buu[i]])
                    S.op("pool", lambda e, i=i: e.tensor_tensor(out=yz[i][:], in0=uu[i][:], in1=uu[i][:], op=ALU.mult), reads=[buu[i], byz[i]], writes=[byz[i]])
                    S.op("dve", lambda e, i=i: e.tensor_scalar(out=yz[i][:], in0=yz[i][:], scalar1=0.044715, scalar2=1.0, op0=ALU.mult, op1=ALU.add), reads=[byz[i]], writes=[byz[i]])
                    S.op("pool", lambda e, i=i: e.tensor_tensor(out=yz[i][:], in0=yz[i][:], in1=uu[i][:], op=ALU.mult), reads=[buu[i], byz[i]], writes=[byz[i]])
                    S.op("act", lambda e, i=i: e.activation(out=yz[i][:], in_=yz[i][:], func=AF.Sigmoid, scale=2.0 * math.sqrt(2.0 / math.pi)), reads=[byz[i]], writes=[byz[i]])
                    S.op("dve", lambda e, i=i, m=m: e.tensor_tensor(out=YT[:, m, :], in0=yz[i][:], in1=uu[i][:], op=ALU.mult), reads=[buu[i], byz[i]], writes=[bYT[m]])
                for mo in range(8):
                    i = mo % 2
                    S.dma("poolq", gw[i][:], od["s5_glu_w"][j][:, mo * 128:(mo + 1) * 128].rearrange("(kt p) n -> p kt n", p=128), writes=[bgw[i]])
                    S.dma("sp", uu[i][:], ZO[2048 + mo * 128:2048 + (mo + 1) * 128, :], writes=[buu[i]])
                    S.op("act", lambda e, i=i: e.activation(out=uu[i][:], in_=uu[i][:], func=AF.Silu), reads=[buu[i]], writes=[buu[i]])
                    for kt in range(8):
                        for c, (c0, cl) in enumerate(CH5):
                            S.op("pe", lambda e, c=c, c0=c0, cl=cl, kt=kt, i=i: e.matmul(pp[c][:, 0:cl], lhsT=gw[i][:, kt, :], rhs=YT[:, kt, c0:c0 + cl],
                                                                                       start=(kt == 0), stop=(kt == 7)), reads=[bgw[i], bYT[kt]], writes=[bpp[c]])
                    for c, (c0, cl) in enumerate(CH5):
                        S.op("act", lambda e, c=c, c0=c0, cl=cl, i=i, mo=mo: e.activation(out=yz[i][:, c0:c0 + cl], in_=pp[c][:, 0:cl], func=AF.Sigmoid, bias=GBI[:, mo:mo + 1], scale=1.0),
                             reads=[bpp[c], bK], writes=[byz[i]])
                    S.op("dve", lambda e, i=i, mo=mo: e.tensor_tensor(out=yz[i][:], in0=yz[i][:], in1=YT[:, mo, :], op=ALU.mult), reads=[byz[i], bYT[mo]], writes=[byz[i]])
                    S.op("pool", lambda e, i=i, mo=mo: e.tensor_tensor(out=AT[:, mo, :], in0=yz[i][:], in1=uu[i][:], op=ALU.mult), reads=[byz[i], buu[i]], writes=[bAT[mo]])
                S.barrier()

        def lru_phase(j):
            AT = ATS["AT"]
            with ExitStack() as ph:
                xd = sb(ph, "lxd", [128, T], F32)
                xc = sb(ph, "lxc", [128, T], F32)
                xcb = sb(ph, "lxcb", [128, T], BF16)
                rrd = [sb(ph, "lrr%d" % d, [128, T], F32) for d in range(2)]
                iid = [sb(ph, "lii%d" % d, [128, T], F32) for d in range(2)]
                aad = [sb(ph, "laa%d" % d, [128, T], F32) for d in range(2)]
                brrd, biid, baad, bWd = [Buf(), Buf()], [Buf(), Buf()], [Buf(), Buf()], [Buf(), Buf()]
                hh = [sb(ph, "lhh%d" % d, [128, T], F32) for d in range(2)]
                gd = sb(ph, "lgd", [128, T], F32)
                WA = [sb(ph, "lwa%d" % d, [128, 128], BF16) for d in range(2)]
                WX = [sb(ph, "lwx%d" % d, [128, 128], BF16) for d in range(2)]
                CW = sb(ph, "lcw", [128, 8, 4], F32)
                CB = sb(ph, "lcb", [128, 8], F32)
                LM = sb(ph, "llm", [128, 2, 8], F32)
                BA = sb(ph, "lba", [128, 2, 8], F32)
                BX = sb(ph, "lbx", [128, 2, 8], F32)
                pp = [ps(ph, "lps%d" % i, [128, 512]) for i in range(4)]
                bK, bxd, bxc, bxcb, brr, bii, baa, bbb, bgd, bW = (Buf() for _ in range(10))
                bhh = [Buf(), Buf()]
                bpp = [Buf() for _ in range(4)]
                for k in range(4):
                    load_T(CW[:, :, k], od["lru_conv_w"][j, k, :].rearrange("(h p) -> h p", p=128), 8, pp[0], bpp[0], bK)
                load_T(CB[:], od["lru_conv_b"][j].rearrange("(h p) -> h p", p=128), 8, pp[0], bpp[0], bK)
                for d in range(2):
                    for tl_, nm in ((LM, "lru_lam"), (BA, "lru_ba"), (BX, "lru_bx")):
                        load_T(tl_[:, d, :], od[nm][j, d, :].rearrange("(h p) -> h p", p=128), 8, pp[0], bpp[0], bK)
                S.op("act", lambda e: e.activation(out=LM[:], in_=LM[:], func=AF.Exp, scale=-1.0), reads=[bK], writes=[bK])
                S.op("act", lambda e: e.activation(out=LM[:], in_=LM[:], func=AF.Ln, bias=1.0, scale=1.0), reads=[bK], writes=[bK])
                S.op("dve", lambda e: e.tensor_scalar(out=LM[:], in0=LM[:], scalar1=-8.0, scalar2=None, op0=ALU.mult), reads=[bK], writes=[bK])
                pi_ = 0
                for h in range(8):
                    S.dma("sp", xd[:], ZO[1024 + h * 128:1024 + (h + 1) * 128, :], writes=[bxd])
                    S.dma("sp", gd[:], ZO[3072 + h * 128:3072 + (h + 1) * 128, :], writes=[bgd])
                    S.op("act", lambda e: e.activation(out=gd[:], in_=gd[:], func=AF.Silu), reads=[bgd], writes=[bgd])
                    for (a0, a1) in ((0, CTXL), (CTXL, T)):
                        S.op("dve", lambda e, a0=a0, a1=a1, h=h: e.tensor_scalar(out=xc[:, a0:a1], in0=xd[:, a0:a1], scalar1=CW[:, h, 1:2], scalar2=CB[:, h:h + 1], op0=ALU.mult, op1=ALU.add),
                             reads=[bxd, bK], writes=[bxc])
                        for (kk, so, do, n_) in ((0, 0, 1, a1 - a0 - 1), (2, 1, 0, a1 - a0 - 1), (3, 2, 0, a1 - a0 - 2)):
                            S.op("dve", lambda e, a0=a0, kk=kk, so=so, do=do, n_=n_, h=h: e.scalar_tensor_tensor(
                                out=xc[:, a0 + do:a0 + do + n_], in0=xd[:, a0 + so:a0 + so + n_], scalar=CW[:, h, kk:kk + 1], in1=xc[:, a0 + do:a0 + do + n_],
                                op0=ALU.mult, op1=ALU.add), reads=[bxd, bxc, bK], writes=[bxc])
                    S.op("act", lambda e: e.copy(out=xcb[:], in_=xc[:]), reads=[bxc], writes=[bxcb])
                    for d in range(2):
                        S.dma("poolq", WA[d][:], od["lru_wa"][j, d, h], writes=[bWd[d]])
                        S.dma("poolq", WX[d][:], od["lru_wx"][j, d, h], writes=[bWd[d]])
                    for d in range(2):
                        for (Wt, Bt, dst, bdst) in ((WA[d], BA, rrd[d], brrd[d]), (WX[d], BX, iid[d], biid[d])):
                            for c, (c0, cl) in enumerate(CH5):
                                p_ = pp[pi_ % 4]
                                bp_ = bpp[pi_ % 4]
                                pi_ += 1
                                S.op("pe", lambda e, p_=p_, Wt=Wt, c0=c0, cl=cl: e.matmul(p_[:, 0:cl], lhsT=Wt[:], rhs=xcb[:, c0:c0 + cl], start=True, stop=True),
                                     reads=[bWd[d], bxcb], writes=[bp_])
                                S.op("act", lambda e, p_=p_, Bt=Bt, dst=dst, c0=c0, cl=cl, d=d, h=h: e.activation(out=dst[:, c0:c0 + cl], in_=p_[:, 0:cl], func=AF.Sigmoid, bias=Bt[:, d, h:h + 1], scale=1.0),
                                     reads=[bp_, bK], writes=[bdst])
                    for d in range(2):
                        S.op("act", lambda e, d=d, h=h: e.activation(out=aad[d][:], in_=rrd[d][:], func=AF.Exp, scale=LM[:, d, h:h + 1]), reads=[brrd[d], bK], writes=[baad[d]])
                    for d in range(2):
                        S.op("dve", lambda e, d=d: e.tensor_tensor(out=rrd[d][:], in0=aad[d][:], in1=aad[d][:], op=ALU.mult), reads=[baad[d]], writes=[brrd[d]])
                        S.op("pool", lambda e, d=d: e.tensor_tensor(out=iid[d][:], in0=iid[d][:], in1=xc[:], op=ALU.mult), reads=[biid[d], bxc], writes=[biid[d]])
                    for d in range(2):
                        S.op("act", lambda e, d=d: e.activation(out=rrd[d][:], in_=rrd[d][:], func=AF.Sqrt, bias=1.0, scale=-1.0), reads=[brrd[d]], writes=[brrd[d]])
                    for d in range(2):
                        S.op("dve", lambda e, d=d: e.tensor_tensor(out=iid[d][:], in0=iid[d][:], in1=rrd[d][:], op=ALU.mult), reads=[biid[d], brrd[d]], writes=[biid[d]])
                        aa, bb = aad[d], iid[d]
                        if d == 0:
                            S.op("dve", lambda e, aa=aa, bb=bb: e.tensor_tensor_scan(out=hh[0][:], data0=aa[:], data1=bb[:], initial=0.0, op0=ALU.mult, op1=ALU.add),
                                 reads=[baad[d], biid[d]], writes=[bhh[0]])
                        else:
                            S.op("dve", lambda e, aa=aa, bb=bb: e.tensor_tensor_scan(out=rev(hh[1][:, 0:CTXL], CTXL), data0=rev(aa[:, 0:CTXL], CTXL), data1=rev(bb[:, 0:CTXL], CTXL),
                                                                                     initial=0.0, op0=ALU.mult, op1=ALU.add), reads=[baad[d], biid[d]], writes=[bhh[1]])
                            S.op("dve", lambda e, aa=aa, bb=bb: e.tensor_tensor_scan(out=rev(hh[1][:, CTXL:T], T - CTXL), data0=rev(aa[:, CTXL:T], T - CTXL), data1=rev(bb[:, CTXL:T], T - CTXL),
                                                                                     initial=hh[1][:, 0:1], op0=ALU.mult, op1=ALU.add), reads=[baad[d], biid[d], bhh[1]], writes=[bhh[1]])
                    S.op("dve", lambda e: e.tensor_tensor(out=hh[0][:], in0=hh[0][:], in1=hh[1][:], op=ALU.add), reads=[bhh[0], bhh[1]], writes=[bhh[0]])
                    S.op("pool", lambda e, h=h: e.tensor_tensor(out=AT[:, 8 + h, :], in0=hh[0][:], in1=gd[:], op=ALU.mult), reads=[bhh[0], bgd], writes=[bAT[8 + h]])
                S.barrier()

        for l in range(nlayers):
            j = l // 2
            last = (l == nlayers - 1)
            at_alloc()
            if "norm" not in skip:
                phase_norm(l)
            if l % 2 == 0:
                if "gemm_tok" not in skip:
                    gemm_tok(l, ev_w_in[j], 1024, ZE)
                if "evmix" not in skip:
                    even_mixers(l, j)
                if "outp" not in skip:
                    out_proj(l, ev_w_out[j], last)
            else:
                if "gemm_ch" not in skip:
                    gemm_ch(od["od_w_in"][j], ODD_IN, ZO)
                at_free()
                if "s5" not in skip:
                    s5_phase(j)
                at_alloc()
                if "s5f" not in skip:
                    s5_finish(j)
                if "lru" not in skip:
                    lru_phase(j)
                if "outp" not in skip:
                    out_proj(l, od["od_w_out"][j], last)
            at_free()
        S.barrier()
        print("ninst", S.ninst, "nwaits", S.nwaits)
    return nc


def rope_tables():
    rows = 2048 // 64
    row = np.repeat(np.arange(rows), 64).astype(np.float32)
    col = np.tile(np.arange(64), rows).astype(np.float32)
    inv = (10000.0 ** (-np.arange(0, 64, 2, dtype=np.float32) / 64)).astype(np.float32)
    ang = np.concatenate([row[:, None] * inv, col[:, None] * inv], axis=-1).astype(np.float32)
    cos = np.concatenate([np.ones((CTXL, 64), np.float32), np.cos(ang).astype(np.float32)], 0)
    sin = np.concatenate([np.zeros((CTXL, 64), np.float32), np.sin(ang).astype(np.float32)], 0)
    return np.ascontiguousarray(cos), np.ascontiguousarray(sin)


_WNAMES = ["ada_w", "ada_b", "norm_g", "ev_w_in", "ev_w_out", "ev_q_g", "ev_k_g", "ev_sgu_g", "ev_ws", "ev_bs",
           "od_w_in", "od_w_out", "s5_lam_re", "s5_lam_im", "s5_log_dt", "s5_b_re", "s5_b_im", "s5_c_re", "s5_c_im",
           "s5_d", "s5_glu_w", "s5_glu_b", "lru_conv_w", "lru_conv_b", "lru_lam", "lru_wa", "lru_ba", "lru_wx", "lru_bx"]


def make_in_maps(inputs, ncores=8):
    cos, sin = rope_tables()
    f = lambda a: np.ascontiguousarray(np.asarray(a, dtype=np.float32))
    shared = {k: f(inputs[k]) for k in _WNAMES}
    maps = []
    for core in range(ncores):
        b = core % 4
        m = dict(shared)
        m["xin"] = np.ascontiguousarray(np.concatenate([f(inputs["ctx"][b]), f(inputs["x"][b])], axis=0))
        m["cond"] = np.ascontiguousarray(np.stack([f(inputs["c"][b]), f(inputs["c_ctx"])], axis=0))
        m["rope_cos"] = cos
        m["rope_sin"] = sin
        maps.append(m)
    return maps


def kernel(**inputs):
    nc = build()
    maps = make_in_maps(inputs)
    res = run_bass_kernel_spmd(nc, maps, core_ids=list(range(8)))
    return np.stack([np.asarray(res.results[b]["out"], dtype=np.float32) for b in range(4)], axis=0)
```

```python
import math
import numpy as np
from contextlib import ExitStack
import concourse.bass as bass
import concourse.mybir as mybir
from concourse.bass_utils import run_bass_kernel_spmd

F32 = mybir.dt.float32
BF16 = mybir.dt.bfloat16
ALU = mybir.AluOpType
AF = mybir.ActivationFunctionType
AX = mybir.AxisListType

T = 2304
NT = 18
D = 2048
KT = 16
CTXL = 256
EPS = 1e-6
CH5 = [(0, 512), (512, 512), (1024, 512), (1536, 512), (2048, 256)]
EVEN_IN = 5632
ODD_IN = 4096


class Buf:
    __slots__ = ("name", "w", "rs")

    def __init__(self, name=""):
        self.name = name
        self.w = None
        self.rs = []


class _Eng:
    def __init__(self, name, eng, sems, inc):
        self.name = name
        self.eng = eng
        self.sems = sems
        self.cnt = [0] * len(sems)
        self.inc = inc
        self.n = 0


class Sched:
    def __init__(self, nc, es, n_dma_sems=8, same_engine_sync=True):
        self.nc = nc
        self.same = same_engine_sync
        self.E = {}
        for name, e in (("pe", nc.tensor), ("dve", nc.vector), ("act", nc.scalar), ("pool", nc.gpsimd)):
            s = es.enter_context(nc.semaphore("s_" + name))
            self.E[name] = _Eng(name, e, [s], 1)
        self.Q = {}
        for qname, e in (("sp", nc.sync), ("actq", nc.scalar), ("poolq", nc.gpsimd)):
            sems = [es.enter_context(nc.semaphore("d_%s%d" % (qname, i))) for i in range(n_dma_sems)]
            self.Q[qname] = _Eng(qname, e, sems, 16)
        self.stream = {"pe": "pe", "dve": "dve", "act": "act", "pool": "pool",
                       "sp": "sp", "actq": "act", "poolq": "pool"}
        self.waited = {s: {} for s in ("pe", "dve", "act", "pool", "sp")}
        self.streng = {"pe": nc.tensor, "dve": nc.vector, "act": nc.scalar, "pool": nc.gpsimd, "sp": nc.sync}
        self.nwaits = 0
        self.ninst = 0
        self._cap = None

    def begin_capture(self):
        self._cap = []

    def end_capture(self):
        c = self._cap
        self._cap = None
        return c

    def replay(self, *lists):
        pos = [0] * len(lists)
        tot = [max(1, len(l)) for l in lists]
        while True:
            best, bi = None, -1
            for i, l in enumerate(lists):
                if pos[i] < len(l):
                    frac = pos[i] / tot[i]
                    if best is None or frac < best:
                        best, bi = frac, i
            if bi < 0:
                break
            kind, a, kw = lists[bi][pos[bi]]
            pos[bi] += 1
            if kind == "op":
                self.op(*a)
            else:
                self.dma(*a, **kw)

    def _wait(self, stream, tok):
        sem, val, _ = tok
        w = self.waited[stream]
        k = id(sem)
        if w.get(k, 0) >= val:
            return
        w[k] = val
        self.streng[stream].wait_ge(sem, val)
        self.nwaits += 1

    @staticmethod
    def _toks(reads, writes):
        toks = []
        for b in reads:
            if b.w is not None:
                toks.append(b.w)
        for b in writes:
            if b.w is not None:
                toks.append(b.w)
            toks.extend(b.rs)
        return toks

    @staticmethod
    def _commit(tok, reads, writes):
        for b in reads:
            b.rs.append(tok)
        for b in writes:
            b.w = tok
            b.rs = []

    def op(self, en, fn, reads=(), writes=()):
        if self._cap is not None:
            self._cap.append(("op", (en, fn, list(reads), list(writes)), None))
            return None
        E = self.E[en]
        stream = self.stream[en]
        skip_same = (en == "pe") or (not self.same)
        for t in self._toks(reads, writes):
            if skip_same and t[2] == en:
                continue
            self._wait(stream, t)
        ins = fn(E.eng)
        E.cnt[0] += 1
        ins.then_inc(E.sems[0], 1)
        tok = (E.sems[0], E.cnt[0], en)
        self._commit(tok, reads, writes)
        self.ninst += 1
        return tok

    def dma(self, qn, out, in_, reads=(), writes=(), **kw):
        if self._cap is not None:
            self._cap.append(("dma", (qn, out, in_, list(reads), list(writes)), kw))
            return None
        Q = self.Q[qn]
        stream = self.stream[qn]
        for t in self._toks(reads, writes):
            self._wait(stream, t)
        j = Q.n % len(Q.sems)
        Q.n += 1
        sem = Q.sems[j]
        if Q.cnt[j] > 0:
            self._wait(stream, (sem, Q.cnt[j], qn))
        ins = Q.eng.dma_start(out=out, in_=in_, **kw)
        Q.cnt[j] += 16
        ins.then_inc(sem, 16)
        tok = (sem, Q.cnt[j], qn)
        self._commit(tok, reads, writes)
        self.ninst += 1
        return tok

    def barrier(self):
        toks = []
        for en, E in self.E.items():
            if E.cnt[0] > 0:
                toks.append((E.sems[0], E.cnt[0], en))
        for qn, Q in self.Q.items():
            for j, sem in enumerate(Q.sems):
                if Q.cnt[j] > 0:
                    toks.append((sem, Q.cnt[j], qn))
        for stream in self.waited:
            for t in toks:
                self._wait(stream, t)


def vw(ap, dims, off=0):
    return bass.AP(tensor=ap.tensor, offset=ap.offset + off, ap=[list(ap.ap[0])] + [list(d) for d in dims])


class Ctx:
    pass


def build(nlayers=4, debug=(), skip=()):
    nc = bass.Bass("TRN2", target_bir_lowering=False)
    K = Ctx()
    K.nc = nc

    def din(name, shape):
        return nc.dram_tensor(name, list(shape), F32, kind="ExternalInput").ap()

    def dscr(name, shape, dt=F32):
        kind = "ExternalOutput" if name in debug else "Internal"
        return nc.dram_tensor(name, list(shape), dt, kind=kind).ap()

    xin = din("xin", [T, D])
    cond = din("cond", [2, D])
    ada_w = din("ada_w", [4, D, 3 * D])
    ada_b = din("ada_b", [4, 3 * D])
    norm_g = din("norm_g", [4, D])
    ev_w_in = din("ev_w_in", [2, D, EVEN_IN])
    ev_w_out = din("ev_w_out", [2, D, D])
    ev_q_g = din("ev_q_g", [2, 128])
    ev_k_g = din("ev_k_g", [2, 128])
    ev_sgu_g = din("ev_sgu_g", [2, 512])
    ev_ws = din("ev_ws", [2, 4, 128, 128])
    ev_bs = din("ev_bs", [2, 4, 128])
    rope_cos = din("rope_cos", [T, 64])
    rope_sin = din("rope_sin", [T, 64])
    od = {}
    for nm, shp in (("od_w_in", [2, D, ODD_IN]), ("od_w_out", [2, D, D]),
                    ("s5_lam_re", [2, 2, 64, 64]), ("s5_lam_im", [2, 2, 64, 64]), ("s5_log_dt", [2, 2, 64]),
                    ("s5_b_re", [2, 2, 64, 64, 16]), ("s5_b_im", [2, 2, 64, 64, 16]),
                    ("s5_c_re", [2, 2, 64, 16, 64]), ("s5_c_im", [2, 2, 64, 16, 64]),
                    ("s5_d", [2, 1024]), ("s5_glu_w", [2, 1024, 1024]), ("s5_glu_b", [2, 1024]),
                    ("lru_conv_w", [2, 4, 1024]), ("lru_conv_b", [2, 1024]), ("lru_lam", [2, 2, 1024]),
                    ("lru_wa", [2, 2, 8, 128, 128]), ("lru_ba", [2, 2, 1024]),
                    ("lru_wx", [2, 2, 8, 128, 128]), ("lru_bx", [2, 2, 1024])):
        od[nm] = din(nm, shp)
    out = nc.dram_tensor("out", [2048, D], F32, kind="ExternalOutput").ap()

    xT = dscr("xT", [D, T])
    ZE = dscr("ZE", [T, EVEN_IN])

    with ExitStack() as es:
        S = Sched(nc, es)
        K.S = S

        uid = [0]

        def sb(stack, name, shape, dt):
            uid[0] += 1
            return stack.enter_context(nc.sbuf_tensor("%s_%d" % (name, uid[0]), list(shape), dt))

        def ps(stack, name, shape, dt=F32):
            uid[0] += 1
            return stack.enter_context(nc.psum_tensor("%s_%d" % (name, uid[0]), list(shape), dt))

        identf = sb(es, "identf", [128, 128], F32)
        identb = sb(es, "identb", [128, 128], BF16)
        onesb = sb(es, "onesb", [128, 128], BF16)
        epsT = sb(es, "epsT", [128, 1], F32)
        mhalf = sb(es, "mhalf", [128, 16], F32)
        MODS = sb(es, "MODS", [128, 4, 48, 2], F32)
        GS = sb(es, "GS", [128, 4, 16, 2], F32)
        ATS = {"st": None, "AT": None}
        bC = Buf("const")
        bMODS = Buf("MODS")
        bAT = [Buf("AT%d" % k) for k in range(KT)]

        S.op("pool", lambda e: e.memset(identf[:], 0.0), writes=[bC])
        S.op("pool", lambda e: e.affine_select(out=identf[:], in_=identf[:], pattern=[[-1, 128]],
                                               compare_op=ALU.not_equal, fill=1.0, base=0, channel_multiplier=1),
             reads=[bC], writes=[bC])
        S.op("dve", lambda e: e.tensor_copy(out=identb[:], in_=identf[:]), reads=[bC], writes=[bC])
        S.op("dve", lambda e: e.memset(onesb[:], 1.0), writes=[bC])
        S.op("dve", lambda e: e.memset(epsT[:], EPS), writes=[bC])
        S.op("dve", lambda e: e.memset(mhalf[:], -0.5), writes=[bC])

        tstage = sb(es, "tstage", [32, 128], F32)
        btst = Buf("tstage")

        def load_T(dst, src_rows, n, ptile, bptile, bdst):
            S.dma("sp", tstage[0:n, :], src_rows, writes=[btst])
            S.op("pe", lambda e: e.transpose(out=ptile[:, 0:n], in_=tstage[0:n, :], identity=identf[0:n, 0:n]), reads=[btst, bC], writes=[bptile])
            S.op("dve", lambda e: e.tensor_copy(out=dst, in_=ptile[:, 0:n]), reads=[bptile], writes=[bdst])

        scTb = sb(es, "scTb", [128, KT, 2], BF16)
        ngT = sb(es, "ngT", [128, 4, KT], F32)
        ones2 = sb(es, "ones2", [1, 2], F32)
        bcond, bng = Buf("cond"), Buf("ng")
        with ExitStack() as ph:
            condT = sb(ph, "condT", [128, KT, 2], F32)
            scT = sb(ph, "scT", [128, KT, 2], F32)
            pm0 = ps(ph, "pm0", [128, 512])
            bpm0 = Buf()
            for kt in range(KT):
                load_T(condT[:, kt, :], cond[:, kt * 128:(kt + 1) * 128], 2, pm0, bpm0, bcond)
                load_T(ngT[:, :, kt], norm_g[:, kt * 128:(kt + 1) * 128], 4, pm0, bpm0, bng)
            S.op("act", lambda e: e.activation(out=scT[:], in_=condT[:], func=AF.Silu), reads=[bcond], writes=[bcond])
            S.op("dve", lambda e: e.memset(ones2[:], 1.0), writes=[bcond])
            S.op("dve", lambda e: e.tensor_copy(out=scTb[:], in_=scT[:]), reads=[bcond], writes=[bcond])
            S.barrier()

        class ModsJob:
            NCHK = 24

            def __init__(self, l, stack):
                self.l = l
                self.wbuf = [sb(stack, "adaw%d" % i, [128, KT, 256], BF16) for i in range(4)]
                self.brow = sb(stack, "brow", [1, 3 * D], F32)
                self.pm = ps(stack, "pm", [128, 512])
                self.bw = [Buf() for _ in range(4)]
                self.bbrow, self.bpm = Buf(), Buf()
                S.dma("sp", self.brow[:], ada_b[l:l + 1, :], writes=[self.bbrow])

            def dma(self, c):
                if c >= self.NCHK:
                    return
                S.dma("poolq", self.wbuf[c % 4][:], ada_w[self.l, :, c * 256:(c + 1) * 256].rearrange("(kt p) n -> p kt n", p=128),
                      writes=[self.bw[c % 4]])

            def compute(self, c):
                if c >= self.NCHK:
                    return
                l, wb_, bwb, pm, brow = self.l, self.wbuf[c % 4], self.bw[c % 4], self.pm, self.brow
                for mt in range(2):
                    m = c * 2 + mt
                    for kt in range(KT):
                        S.op("pe", lambda e, kt=kt, mt=mt: e.matmul(pm[:, 2 * mt:2 * mt + 2], lhsT=wb_[:, kt, mt * 128:(mt + 1) * 128], rhs=scTb[:, kt, :],
                                                                  start=(kt == 0), stop=False), reads=[bwb, bcond], writes=[self.bpm])
                    S.op("pe", lambda e, mt=mt, m=m: e.matmul(pm[:, 2 * mt:2 * mt + 2], lhsT=brow[0:1, m * 128:(m + 1) * 128], rhs=ones2[0:1, :],
                                                             start=False, stop=True), reads=[self.bbrow, bcond], writes=[self.bpm])
                S.op("dve", lambda e: e.tensor_copy(out=MODS[:, l, 2 * c:2 * c + 2, :], in_=pm[:, 0:4].rearrange("p (m r) -> p m r", r=2)),
                     reads=[self.bpm], writes=[bMODS])
                if c == self.NCHK - 1:
                    for r in range(2):
                        S.op("dve", lambda e, r=r: e.scalar_tensor_tensor(out=GS[:, l, :, r], in0=MODS[:, l, 16:32, r], scalar=1.0, in1=ngT[:, l, :],
                                                                           op0=ALU.add, op1=ALU.mult), reads=[bMODS, bng], writes=[bMODS])

        mods0_stack = ExitStack()
        lmods0 = []
        if nlayers > 0:
            job0 = ModsJob(0, mods0_stack)
            S.begin_capture()
            job0.dma(0)
            job0.dma(1)
            for c in range(ModsJob.NCHK):
                job0.dma(c + 2)
                job0.compute(c)
            lmods0 = S.end_capture()

        bxT = [Buf("xT%d" % k) for k in range(KT)]
        with ExitStack() as ph:
            xa = [sb(ph, "t0a%d" % i, [128, NT, 128], F32) for i in range(2)]
            xb = [sb(ph, "t0b%d" % i, [128, T], F32) for i in range(2)]
            pt = [ps(ph, "t0p%d" % i, [128, 512]) for i in range(4)]
            ba = [Buf(), Buf()]
            bb = [Buf(), Buf()]
            bp = [Buf() for _ in range(4)]
            pi = 0
            S.begin_capture()
            for kt in range(KT):
                a = xa[kt % 2]
                b = xb[kt % 2]
                S.dma("sp", a[:], xin[:, kt * 128:(kt + 1) * 128].rearrange("(t p) f -> p t f", p=128), writes=[ba[kt % 2]])
                for c, (c0, cl) in enumerate(CH5):
                    p = pt[pi % 4]
                    bpp = bp[pi % 4]
                    pi += 1
                    for i in range(cl // 128):
                        t = c0 // 128 + i
                        S.op("pe", lambda e, p=p, a=a, t=t, i=i: e.transpose(out=p[:, i * 128:(i + 1) * 128], in_=a[:, t, :], identity=identf[:]),
                             reads=[ba[kt % 2], bC], writes=[bpp])
                    en = "dve" if c % 2 == 0 else "act"
                    if en == "dve":
                        S.op("dve", lambda e, p=p, b=b, c0=c0, cl=cl: e.tensor_copy(out=b[:, c0:c0 + cl], in_=p[:, 0:cl]), reads=[bpp], writes=[bb[kt % 2]])
                    else:
                        S.op("act", lambda e, p=p, b=b, c0=c0, cl=cl: e.copy(out=b[:, c0:c0 + cl], in_=p[:, 0:cl]), reads=[bpp], writes=[bb[kt % 2]])
                S.dma("sp", xT[kt * 128:(kt + 1) * 128, :], b[:], reads=[bb[kt % 2]], writes=[bxT[kt]])
            lt0 = S.end_capture()
            S.replay(lt0, lmods0)
            S.barrier()
        mods0_stack.close()

        def at_alloc():
            ATS["st"] = ExitStack()
            ATS["AT"] = sb(ATS["st"], "AT", [128, KT, T], BF16)
            return ATS["AT"]

        def at_free():
            S.barrier()
            ATS["st"].close()
            ATS["AT"] = None

        def rev(ap, n):
            return bass.AP(tensor=ap.tensor, offset=ap.offset + n - 1, ap=[list(ap.ap[0]), [-1, n]])

        ZO = dscr("ZO", [ODD_IN, T])
        RSTDd = dscr("RSTDd", [1, T])
        bRd = Buf("RSTDd")
        have_rstd = [False]
        ZU = dscr("ZU", [1024, T], BF16)
        ZY = dscr("ZY", [1024, T])
        def phase_norm(l):
            AT = ATS["AT"]
            with ExitStack() as ph:
                xa = [sb(ph, "nx%d" % i, [128, T], F32) for i in range(2)]
                sq = [sb(ph, "nsq%d" % i, [128, T], BF16) for i in range(2)]
                tmp = [sb(ph, "ntmp%d" % i, [128, T], F32) for i in range(2)]
                RSTD = sb(ph, "RSTD", [128, T], F32)
                pss = [ps(ph, "nps%d" % i, [128, 512]) for i in range(5)]
                bx = [Buf(), Buf()]
                bsq = [Buf(), Buf()]
                btmp = [Buf(), Buf()]
                bps = [Buf() for _ in range(5)]
                bR = Buf()
                if have_rstd[0]:
                    S.dma("sp", RSTD[:], bass.AP(tensor=RSTDd.tensor, offset=RSTDd.offset, ap=[[0, 128], [1, T]]), reads=[bRd], writes=[bR])
                else:
                    for kt in range(KT):
                        i = kt % 2
                        S.dma("sp", xa[i][:], xT[kt * 128:(kt + 1) * 128, :], reads=[bxT[kt]], writes=[bx[i]])
                        S.op("act", lambda e, i=i: e.activation(out=sq[i][:], in_=xa[i][:], func=AF.Square), reads=[bx[i]], writes=[bsq[i]])
                        for c, (c0, cl) in enumerate(CH5):
                            S.op("pe", lambda e, i=i, c=c, c0=c0, cl=cl, kt=kt: e.matmul(
                                pss[c][:, 0:cl], lhsT=onesb[:], rhs=sq[i][:, c0:c0 + cl], start=(kt == 0), stop=(kt == KT - 1)),
                                reads=[bsq[i], bC], writes=[bps[c]])
                    for c, (c0, cl) in enumerate(CH5):
                        S.op("act", lambda e, c=c, c0=c0, cl=cl: e.activation(out=RSTD[:, c0:c0 + cl], in_=pss[c][:, 0:cl], func=AF.Sqrt,
                                                                             bias=epsT[:], scale=1.0 / D), reads=[bps[c], bC], writes=[bR])
                    S.op("dve", lambda e: e.reciprocal(out=RSTD[:], in_=RSTD[:]), reads=[bR], writes=[bR])
                for kt in range(KT):
                    i = kt % 2
                    S.dma("sp", xa[i][:], xT[kt * 128:(kt + 1) * 128, :], reads=[bxT[kt]], writes=[bx[i]])
                    for (a0, a1, r) in ((0, CTXL, 1), (CTXL, T, 0)):
                        S.op("dve", lambda e, i=i, a0=a0, a1=a1, r=r, kt=kt: e.scalar_tensor_tensor(
                            out=tmp[i][:, a0:a1], in0=xa[i][:, a0:a1], scalar=GS[:, l, kt, r:r + 1], in1=RSTD[:, a0:a1],
                            op0=ALU.mult, op1=ALU.mult), reads=[bx[i], bR, bMODS], writes=[btmp[i]])
                        S.op("act", lambda e, i=i, a0=a0, a1=a1, r=r, kt=kt: e.activation(
                            out=AT[:, kt, a0:a1], in_=tmp[i][:, a0:a1], func=AF.Identity, bias=MODS[:, l, kt, r:r + 1], scale=1.0),
                            reads=[btmp[i], bMODS], writes=[bAT[kt]])
                S.barrier()

        def gemm_tok(l, W, ncols, Z):
            AT = ATS["AT"]
            with ExitStack() as ph:
                wc = [sb(ph, "gw%d" % i, [128, KT, 512], BF16) for i in range(2)]
                st = [sb(ph, "gst%d" % i, [128, 512], F32) for i in range(4)]
                pp = [ps(ph, "gps%d" % i, [128, 512]) for i in range(4)]
                bw = [Buf(), Buf()]
                bst = [Buf() for _ in range(4)]
                bpp = [Buf() for _ in range(4)]
                bz = Buf()
                it = 0
                for n in range(ncols // 512):
                    w = wc[n % 2]
                    S.dma("poolq", w[:], W[:, n * 512:(n + 1) * 512].rearrange("(kt p) n -> p kt n", p=128), writes=[bw[n % 2]])
                    for t in range(NT):
                        j = it % 4
                        it += 1
                        for kt in range(KT):
                            S.op("pe", lambda e, j=j, w=w, kt=kt, t=t: e.matmul(
                                pp[j][:], lhsT=AT[:, kt, t * 128:(t + 1) * 128], rhs=w[:, kt, :], start=(kt == 0), stop=(kt == KT - 1)),
                                reads=[bw[n % 2], bAT[kt]], writes=[bpp[j]])
                        if it % 2 == 0:
                            S.op("dve", lambda e, j=j: e.tensor_copy(out=st[j][:], in_=pp[j][:]), reads=[bpp[j]], writes=[bst[j]])
                        else:
                            S.op("act", lambda e, j=j: e.copy(out=st[j][:], in_=pp[j][:]), reads=[bpp[j]], writes=[bst[j]])
                        S.dma("sp", Z[t * 128:(t + 1) * 128, n * 512:(n + 1) * 512], st[j][:], reads=[bst[j]], writes=[bz])
                S.barrier()

        def headnorm_rope(ph_bufs, src, nh, gtile, cos_t, sin_t, dst, eng2="pool", ntile=1, cs_tstride=0):
            sq, ss, tn, ta, tb_ = ph_bufs["sq"], ph_bufs["ss"], ph_bufs["tn"], ph_bufs["ta"], ph_bufs["tb"]
            bsrc, bscr, bdst = ph_bufs["bsrc"], ph_bufs["bscr"], ph_bufs["bdst"]
            H = nh * ntile
            W_ = H * 128
            s3 = src.rearrange("p (h d) -> p h d", d=128)
            S.op("dve", lambda e: e.tensor_tensor(out=sq[:, 0:W_], in0=src, in1=src, op=ALU.mult), reads=[bsrc], writes=[bscr])
            S.op("dve", lambda e: e.tensor_reduce(out=ss[:, 0:H], in_=sq[:, 0:W_].rearrange("p (h d) -> p h d", d=128), axis=AX.X, op=ALU.add),
                 reads=[bscr], writes=[bscr])
            S.op("pool", lambda e: e.tensor_scalar(out=ss[:, 0:H], in0=ss[:, 0:H], scalar1=1.0 / 128, scalar2=EPS, op0=ALU.mult, op1=ALU.add), reads=[bscr], writes=[bscr])
            S.op("pool", lambda e: e.tensor_tensor(out=ss[:, 0:H], in0=ss[:, 0:H], in1=mhalf[:, 0:H], op=ALU.pow), reads=[bscr, bC], writes=[bscr])
            rb = vw(ss[:], [[1, H], [0, 128]])
            S.op("dve", lambda e: e.tensor_tensor(out=tn[:, 0:W_].rearrange("p (h d) -> p h d", d=128), in0=s3, in1=rb, op=ALU.mult),
                 reads=[bsrc, bscr], writes=[bscr])
            tn3 = tn[:, 0:W_].rearrange("p (h d) -> p h d", d=128)
            S.op(eng2, lambda e: e.tensor_tensor(out=tn3, in0=tn3, in1=vw(gtile[:], [[0, H], [1, 128]]), op=ALU.mult), reads=[bscr, bC], writes=[bscr])
            x0 = vw(tn[:, 0:1], [[nh * 128, ntile], [128, nh], [2, 64]], off=0)
            x1 = vw(tn[:, 0:1], [[nh * 128, ntile], [128, nh], [2, 64]], off=1)
            d0 = vw(dst, [[nh * 128, ntile], [128, nh], [2, 64]], off=0)
            d1 = vw(dst, [[nh * 128, ntile], [128, nh], [2, 64]], off=1)
            cb = vw(cos_t, [[cs_tstride, ntile], [0, nh], [1, 64]])
            sbb = vw(sin_t, [[cs_tstride, ntile], [0, nh], [1, 64]])
            a3 = vw(ta[:, 0:1], [[nh * 64, ntile], [64, nh], [1, 64]])
            b3 = vw(tb_[:, 0:1], [[nh * 64, ntile], [64, nh], [1, 64]])
            S.op("dve", lambda e: e.tensor_tensor(out=a3, in0=x0, in1=cb, op=ALU.mult), reads=[bscr, ph_bufs["bcs"]], writes=[ph_bufs["ba"]])
            S.op(eng2, lambda e: e.tensor_tensor(out=b3, in0=x1, in1=sbb, op=ALU.mult), reads=[bscr, ph_bufs["bcs"]], writes=[ph_bufs["bb"]])
            S.op("dve", lambda e: e.tensor_tensor(out=d0, in0=a3, in1=b3, op=ALU.subtract), reads=[ph_bufs["ba"], ph_bufs["bb"]], writes=[bdst])
            S.op("dve", lambda e: e.tensor_tensor(out=a3, in0=x0, in1=sbb, op=ALU.mult), reads=[bscr, ph_bufs["bcs"]], writes=[ph_bufs["ba"]])
            S.op(eng2, lambda e: e.tensor_tensor(out=b3, in0=x1, in1=cb, op=ALU.mult), reads=[bscr, ph_bufs["bcs"]], writes=[ph_bufs["bb"]])
            S.op("dve", lambda e: e.tensor_tensor(out=d1, in0=a3, in1=b3, op=ALU.add), reads=[ph_bufs["ba"], ph_bufs["bb"]], writes=[bdst])

        def gemm_tok_tail(W, Z, n_list, stack):
            AT = ATS["AT"]
            wc = [sb(stack, "gtw%d" % i, [128, KT, 512], BF16) for i in range(2)]
            st = [sb(stack, "gtst%d" % i, [128, 512], F32) for i in range(4)]
            pp = [ps(stack, "gtps%d" % i, [128, 512]) for i in range(4)]
            bw = [Buf(), Buf()]
            bst = [Buf() for _ in range(4)]
            bpp = [Buf() for _ in range(4)]
            bz = Buf()
            it = 0
            for ni, n in enumerate(n_list):
                w = wc[ni % 2]
                S.dma("poolq", w[:], W[:, n * 512:(n + 1) * 512].rearrange("(kt p) n -> p kt n", p=128), writes=[bw[ni % 2]])
                for t in range(NT):
                    j_ = it % 4
                    it += 1
                    for kt in range(KT):
                        S.op("pe", lambda e, j_=j_, w=w, kt=kt, t=t: e.matmul(
                            pp[j_][:], lhsT=AT[:, kt, t * 128:(t + 1) * 128], rhs=w[:, kt, :], start=(kt == 0), stop=(kt == KT - 1)),
                            reads=[bw[ni % 2], bAT[kt]], writes=[bpp[j_]])
                    S.op("act", lambda e, j_=j_: e.copy(out=st[j_][:], in_=pp[j_][:]), reads=[bpp[j_]], writes=[bst[j_]])
                    S.dma("sp", Z[t * 128:(t + 1) * 128, n * 512:(n + 1) * 512], st[j_][:], reads=[bst[j_]], writes=[bz])

        def even_mixers(l, j):
            AT = ATS["AT"]
            with ExitStack() as ph:
                CS = [sb(ph, "CS%d" % i, [128, 2, 64], F32) for i in range(1)]
                bCS = [Buf()]
                CS3 = sb(ph, "CS3", [128, 3, 2, 64], F32)
                bCS3 = Buf()
                KG = sb(ph, "KG", [128, 128], F32)
                QG = sb(ph, "QG", [128, 128], F32)
                SG = sb(ph, "SG", [128, 512], F32)
                WST = sb(ph, "WST", [128, 4, 128], BF16)
                BST = sb(ph, "BST", [128, 4], F32)
                kT = sb(ph, "kT", [128, 4, T], BF16)
                VA = sb(ph, "VA", [128, NT, 4, 130], BF16)
                raw = [sb(ph, "raw%d" % i, [128, 2048], F32) for i in range(2)]
                sq = sb(ph, "hsq", [128, 1536], F32)
                tn = sb(ph, "htn", [128, 1536], F32)
                ta = sb(ph, "hta", [128, 768], F32)
                tb_ = sb(ph, "htb", [128, 768], F32)
                ss = sb(ph, "hss", [128, 16], F32)
                rot = sb(ph, "rot", [128, 12, 128], BF16)
                pT = [ps(ph, "epT%d" % i, [128, 4, 128], BF16) for i in range(2)]
                braw = [Buf(), Buf()]
                buv, brot, bkT, bVA, bqT, bATT, bsgt, bmix, bvnb, bsgu, brc = (Buf() for _ in range(11))
                bESs = [[Buf() for _ in range(NT)] for _ in range(2)]
                bpT = [Buf(), Buf()]
                bpS = [Buf() for _ in range(3)]
                bpO = [Buf(), Buf()]
                bpG = Buf()
                hb = dict(sq=sq, ss=ss, tn=tn, ta=ta, tb=tb_, bscr=Buf(), ba=Buf(), bb=Buf(), bdst=brot, bsrc=None)

                def load_cs(t):
                    i = 0
                    S.dma("sp", CS[i][:, 0, :], rope_cos[t * 128:(t + 1) * 128, :], writes=[bCS[i]])
                    S.dma("sp", CS[i][:, 1, :], rope_sin[t * 128:(t + 1) * 128, :], writes=[bCS[i]])
                    hb["bcs"] = bCS[i]
                    return CS[i][:, 0, :], CS[i][:, 1, :]
                kg = ev_k_g[j:j + 1, :]
                S.dma("sp", KG[:], bass.AP(tensor=kg.tensor, offset=kg.offset, ap=[[0, 128], [1, 128]]), writes=[bC])
                S.op("dve", lambda e: e.memset(VA[:], 1.0), writes=[bVA])

                def kv_E(tg):
                    r = raw[tg % 2]
                    for i in range(3):
                        t = tg * 3 + i
                        S.dma("sp", r[:, i * 512:(i + 1) * 512], ZE[t * 128:(t + 1) * 128, 0:512], writes=[braw[tg % 2]])
                        S.dma("sp", CS3[:, i, 0, :], rope_cos[t * 128:(t + 1) * 128, :], writes=[bCS3])
                        S.dma("sp", CS3[:, i, 1, :], rope_sin[t * 128:(t + 1) * 128, :], writes=[bCS3])
                    hb["bsrc"] = braw[tg % 2]
                    hb["bcs"] = bCS3
                    headnorm_rope(hb, r[:, 0:1536], 4, KG, CS3[:, 0, 0, :], CS3[:, 0, 1, :], rot[:, 0:12, :], ntile=3, cs_tstride=128, eng2="dve")

                def kv_P(tg):
                    for i in range(3):
                        t = tg * 3 + i
                        S.dma("poolq", VA[:, t, :, 0:128], ZE[t * 128:(t + 1) * 128, 512:1024].rearrange("p (h d) -> p h d", d=128), writes=[bVA])
                        p = pT[i % 2]
                        for h in range(4):
                            S.op("pe", lambda e, p=p, h=h, i=i: e.transpose(out=p[:, h, :], in_=rot[:, i * 4 + h, :], identity=identb[:]), reads=[brot, bC], writes=[bpT[i % 2]])
                        S.op("dve", lambda e, p=p, t=t: e.tensor_copy(out=kT[:, :, t * 128:(t + 1) * 128], in_=p[:]), reads=[bpT[i % 2]], writes=[bkT])

                phg = ExitStack()
                S.begin_capture()
                gemm_tok_tail(ev_w_in[j], ZE, list(range(2, EVEN_IN // 512)), phg)
                lg = S.end_capture()
                NG_ = NT // 3
                cut = [len(lg) * k // (NG_ + 1) for k in range(NG_ + 2)]
                kv_E(0)
                for tg in range(NG_):
                    S.replay(lg[cut[tg]:cut[tg + 1]])
                    kv_P(tg)
                    if tg + 1 < NG_:
                        kv_E(tg + 1)
                S.replay(lg[cut[NG_]:])
                S.barrier()
                phg.close()

                qTs = [sb(ph, "qT%d" % i, [128, 12, 256], BF16) for i in range(2)]
                bqTs = [Buf(), Buf()]
                ESs = [sb(ph, "ES%d" % i, [128, NT, 256], BF16) for i in range(2)]
                ATTs = [sb(ph, "ATT%d" % i, [128, 2, 1536], BF16) for i in range(2)]
                bATTs = [Buf(), Buf()]
                uv = sb(ph, "uv", [128, 1024], F32)
                mixtok = sb(ph, "mixtok", [128, 2048], BF16)
                vnb = sb(ph, "vnb", [128, 4, 128], BF16)
                rc = sb(ph, "rc", [128, 4], F32)
                pS = [ps(ph, "epS%d" % i, [128, 512]) for i in range(3)]
                pO = [ps(ph, "epO%d" % i, [128, 512]) for i in range(2)]
                pG = ps(ph, "epG", [128, 512])
                qg = ev_q_g[j:j + 1, :]
                S.dma("sp", QG[:], bass.AP(tensor=qg.tensor, offset=qg.offset, ap=[[0, 128], [1, 128]]), writes=[bC])
                sg_ = ev_sgu_g[j:j + 1, :]
                S.dma("sp", SG[:], bass.AP(tensor=sg_.tensor, offset=sg_.offset, ap=[[0, 128], [1, 512]]), writes=[bC])
                WSn = sq[:, 0:512].rearrange("p (g q) -> p g q", q=128)
                S.dma("sp", WSn, ev_ws[j].rearrange("g p q -> p g q"), writes=[hb["bscr"]])
                load_T(BST[:], ev_bs[j], 4, pG, bpG, bC)
                S.op("dve", lambda e: e.tensor_scalar(out=QG[:], in0=QG[:], scalar1=128.0 ** -0.5, scalar2=None, op0=ALU.mult), reads=[bC], writes=[bC])
                for g in range(4):
                    S.op("pe", lambda e, g=g: e.transpose(out=pS[0][:, g * 128:(g + 1) * 128], in_=WSn[:, g, :], identity=identf[:]), reads=[bC, hb["bscr"]], writes=[bpS[0]])
                S.op("dve", lambda e: e.tensor_copy(out=WST[:].rearrange("p g q -> p (g q)"), in_=pS[0][:]), reads=[bpS[0]], writes=[bC])

                chunks = [([0, 1], [0, 1])] + [([2 + 2 * c, 3 + 2 * c], list(range(NT))) for c in range(8)]
                cnt = {"t": 0, "si": 0, "oi": 0}

                def qprep_tile(ci, tl):
                    qtiles, ktiles = chunks[ci]
                    t = qtiles[tl]
                    qT = qTs[ci % 2]
                    bqT = bqTs[ci % 2]
                    tcount = cnt["t"]
                    cnt["t"] += 1
                    r = raw[tcount % 2]
                    br = braw[tcount % 2]
                    S.dma("sp", r[:, 0:1536], ZE[t * 128:(t + 1) * 128, 1024:2560], writes=[br])
                    hb["bsrc"] = br
                    c_t, s_t = load_cs(t)
                    headnorm_rope(hb, r[:, 0:1536], 12, QG, c_t, s_t, rot[:, :, :])
                    for hg in range(3):
                        p = pT[hg % 2]
                        for hh in range(4):
                            h = hg * 4 + hh
                            S.op("pe", lambda e, p=p, hh=hh, h=h: e.transpose(out=p[:, hh, :], in_=rot[:, h, :], identity=identb[:]),
                                 reads=[brot, bC], writes=[bpT[hg % 2]])
                        S.op("dve", lambda e, p=p, hg=hg, tl=tl: e.tensor_copy(out=qT[:, hg * 4:(hg + 1) * 4, tl * 128:(tl + 1) * 128], in_=p[:]),
                             reads=[bpT[hg % 2]], writes=[bqT])

                def qk_ops(ci, h):
                    qtiles, ktiles = chunks[ci]
                    Q = len(qtiles) * 128
                    qT = qTs[ci % 2]
                    bqT = bqTs[ci % 2]
                    par = (ci * 12 + h) % 2
                    kvh = h // 3
                    ops = []
                    for kt in ktiles:
                        def f(kt=kt):
                            si = cnt["si"]
                            cnt["si"] += 1
                            pss_ = pS[si % 3]
                            bps_ = bpS[si % 3]
                            S.op("pe", lambda e: e.matmul(pss_[:, 0:Q], lhsT=kT[:, kvh, kt * 128:(kt + 1) * 128], rhs=qT[:, h, 0:Q], start=True, stop=True),
                                 reads=[bkT, bqT], writes=[bps_])
                            S.op("act", lambda e: e.activation(out=ESs[par][:, kt, 0:Q], in_=pss_[:, 0:Q], func=AF.Exp),
                                 reads=[bps_], writes=[bESs[par][kt]])
                        ops.append(f)
                    return ops

                def pv_ops(ci, h):
                    qtiles, ktiles = chunks[ci]
                    nq = len(qtiles)
                    ATT = ATTs[ci % 2]
                    bATT = bATTs[ci % 2]
                    par = (ci * 12 + h) % 2
                    kvh = h // 3
                    ops = []
                    for qs in range(nq):
                        for ki, kt in enumerate(ktiles):
                            def f(qs=qs, ki=ki, kt=kt):
                                if ki == 0:
                                    cnt["oi"] += 1
                                oi = cnt["oi"]
                                po = pO[oi % 2]
                                bpo = bpO[oi % 2]
                                S.op("pe", lambda e: e.matmul(po[:, 0:129], lhsT=ESs[par][:, kt, qs * 128:(qs + 1) * 128], rhs=VA[:, kt, kvh, 0:129],
                                                              start=(ki == 0), stop=(ki == len(ktiles) - 1)), reads=[bESs[par][kt], bVA], writes=[bpo])
                                if ki == len(ktiles) - 1:
                                    S.op("dve", lambda e: e.reciprocal(out=rc[:, qs:qs + 1], in_=po[:, 128:129]), reads=[bpo], writes=[brc])
                                    S.op("dve", lambda e: e.tensor_scalar(out=ATT[:, qs, h * 128:(h + 1) * 128], in0=po[:, 0:128], scalar1=rc[:, qs:qs + 1],
                                                                          scalar2=None, op0=ALU.mult), reads=[bpo, brc], writes=[bATT])
                            ops.append(f)
                    return ops

                def gate_tile(ci, tl):
                    qtiles, ktiles = chunks[ci]
                    t = qtiles[tl]
                    ATT = ATTs[ci % 2]
                    bATT = bATTs[ci % 2]
                    tcount = cnt["t"]
                    cnt["t"] += 1
                    r = raw[tcount % 2]
                    br = braw[tcount % 2]
                    S.dma("sp", r[:, 0:2048], ZE[t * 128:(t + 1) * 128, 3584:5632], writes=[br])
                    S.dma("sp", uv[:], ZE[t * 128:(t + 1) * 128, 2560:3584], writes=[buv])
                    S.op("act", lambda e, r=r: e.activation(out=r[:, 0:2048], in_=r[:, 0:2048], func=AF.Silu), reads=[br], writes=[br])
                    S.op("dve", lambda e, tl=tl, r=r: e.tensor_tensor(out=mixtok[:, 0:1536], in0=ATT[:, tl, :], in1=r[:, 0:1536], op=ALU.mult),
                         reads=[bATT, br], writes=[bmix])
                    bv = uv[:, 512:1024]
                    bscr = hb["bscr"]
                    S.op("pool", lambda e: e.tensor_tensor(out=sq[:, 0:512], in0=bv, in1=bv, op=ALU.mult), reads=[buv], writes=[bscr])
                    S.op("dve", lambda e: e.tensor_reduce(out=ss[:, 0:4], in_=sq[:, 0:512].rearrange("p (h d) -> p h d", d=128), axis=AX.X, op=ALU.add),
                         reads=[bscr], writes=[bscr])
                    S.op("pool", lambda e: e.tensor_scalar(out=ss[:, 0:4], in0=ss[:, 0:4], scalar1=1.0 / 128, scalar2=EPS, op0=ALU.mult, op1=ALU.add), reads=[bscr], writes=[bscr])
                    S.op("pool", lambda e: e.tensor_tensor(out=ss[:, 0:4], in0=ss[:, 0:4], in1=mhalf[:, 0:4], op=ALU.pow), reads=[bscr, bC], writes=[bscr])
                    S.op("dve", lambda e: e.tensor_tensor(out=tn[:, 0:512].rearrange("p (h d) -> p h d", d=128), in0=bv.rearrange("p (h d) -> p h d", d=128),
                                                          in1=vw(ss[:], [[1, 4], [0, 128]]), op=ALU.mult), reads=[buv, bscr], writes=[bscr])
                    S.op("pool", lambda e: e.tensor_tensor(out=vnb[:].rearrange("p g d -> p (g d)"), in0=tn[:, 0:512], in1=SG[:], op=ALU.mult),
                         reads=[bscr, bC], writes=[bvnb])
                    for g in range(4):
                        S.op("pe", lambda e, g=g: e.matmul(pG[:, g * 128:(g + 1) * 128], lhsT=WST[:, g, :], rhs=vnb[:, g, :], start=True, stop=True),
                             reads=[bC, bvnb], writes=[bpG])
                    for g in range(4):
                        S.op("dve", lambda e, g=g: e.scalar_tensor_tensor(
                            out=tn[:, 1024 + g * 128:1024 + (g + 1) * 128], in0=pG[:, g * 128:(g + 1) * 128], scalar=BST[:, g:g + 1],
                            in1=uv[:, g * 128:(g + 1) * 128], op0=ALU.add, op1=ALU.mult), reads=[bpG, bC, buv], writes=[bscr])
                    S.op("pool", lambda e, r=r: e.tensor_tensor(out=mixtok[:, 1536:2048], in0=tn[:, 1024:1536], in1=r[:, 1536:2048], op=ALU.mult),
                         reads=[bscr, br], writes=[bmix])
                    for hg in range(4):
                        p = pT[hg % 2]
                        for hh in range(4):
                            k_ = hg * 4 + hh
                            S.op("pe", lambda e, p=p, hh=hh, k_=k_: e.transpose(out=p[:, hh, :], in_=mixtok[:, k_ * 128:(k_ + 1) * 128], identity=identb[:]),
                                 reads=[bmix, bC], writes=[bpT[hg % 2]])
                        S.op("dve", lambda e, p=p, hg=hg, t=t: e.tensor_copy(out=AT[:, hg * 4:(hg + 1) * 4, t * 128:(t + 1) * 128], in_=p[:]),
                             reads=[bpT[hg % 2]], writes=[bAT[hg * 4 + i_] for i_ in range(4)])

                NCH = len(chunks)
                qprep_tile(0, 0)
                qprep_tile(0, 1)
                units = [(ci, h) for ci in range(NCH) for h in range(12)]
                for f in qk_ops(*units[0]):
                    f()
                for ui, (ci, h) in enumerate(units):
                    pv = pv_ops(ci, h)
                    qk = qk_ops(*units[ui + 1]) if ui + 1 < len(units) else []
                    nqk, npv = len(qk), len(pv)
                    a = b = 0
                    while a < nqk or b < npv:
                        if a < nqk and a * npv <= b * nqk:
                            qk[a]()
                            a += 1
                        else:
                            pv[b]()
                            b += 1
                    if ci > 0 and h == 0:
                        gate_tile(ci - 1, 0)
                    if ci > 0 and h == 3:
                        gate_tile(ci - 1, 1)
                    if ci + 1 < NCH and h == 5:
                        qprep_tile(ci + 1, 0)
                    if ci + 1 < NCH and h == 8:
                        qprep_tile(ci + 1, 1)
                gate_tile(NCH - 1, 0)
                gate_tile(NCH - 1, 1)
                S.barrier()

        def out_proj(l, Wout, last):
            AT = ATS["AT"]
            with ExitStack() as ph:
                wo = [sb(ph, "wo%d" % i, [128, KT, 512], BF16) for i in range(2)]
                xa = [sb(ph, "ox%d" % i, [128, T], F32) for i in range(2)]
                ost = [sb(ph, "ost%d" % i, [128, 4, 128], F32) for i in range(2)]
                pp = [ps(ph, "ops%d" % i, [128, 512]) for i in range(5)]
                ptr = [ps(ph, "optr%d" % i, [128, 4, 128]) for i in range(2)]
                bw = [Buf(), Buf()]
                bx = [Buf(), Buf()]
                bost = [Buf(), Buf()]
                bpp = [Buf() for _ in range(5)]
                bptr = [Buf(), Buf()]
                bout = Buf()
                oi = 0
                do_stats = (not last) and ("stats" not in skip)
                if do_stats:
                    acc = sb(ph, "oacc", [128, T], F32)
                    sqt = sb(ph, "osqt", [128, T], F32)
                    onesf = sb(ph, "onesf", [128, 128], F32)
                    bacc, bsqt, bof = Buf(), Buf(), Buf()
                    S.op("pool", lambda e: e.memset(onesf[:], 1.0), writes=[bof])
                job = ModsJob(l + 1, ph) if (l + 1 < nlayers and "mods" not in skip) else None
                S.dma("poolq", wo[0][:], Wout[:, 0:512].rearrange("(kt p) n -> p kt n", p=128), writes=[bw[0]])
                if job:
                    job.dma(0)
                    job.dma(1)
                for mg in range(4):
                    w = wo[mg % 2]
                    if mg + 1 < 4:
                        S.dma("poolq", wo[(mg + 1) % 2][:], Wout[:, (mg + 1) * 512:(mg + 2) * 512].rearrange("(kt p) n -> p kt n", p=128), writes=[bw[(mg + 1) % 2]])
                    for m in range(4):
                        mi = mg * 4 + m
                        if job:
                            job.dma(2 * mi + 2)
                            job.dma(2 * mi + 3)
                        x_ = xa[mi % 2]
                        bx_ = bx[mi % 2]
                        S.dma("sp", x_[:], xT[mi * 128:(mi + 1) * 128, :], reads=[bxT[mi]], writes=[bx_])
                        for kt in range(KT):
                            for c, (c0, cl) in enumerate(CH5):
                                S.op("pe", lambda e, c=c, c0=c0, cl=cl, kt=kt, w=w, m=m: e.matmul(
                                    pp[c][:, 0:cl], lhsT=w[:, kt, m * 128:(m + 1) * 128], rhs=AT[:, kt, c0:c0 + cl],
                                    start=(kt == 0), stop=(kt == KT - 1)), reads=[bw[mg % 2], bAT[kt]], writes=[bpp[c]])
                        for c, (c0, cl) in enumerate(CH5):
                            segs = [(0, CTXL, 1), (CTXL, 512, 0)] if c == 0 else [(0, cl, 0)]
                            for (a0, a1, r) in segs:
                                S.op("dve", lambda e, c=c, c0=c0, a0=a0, a1=a1, r=r, x_=x_, mi=mi: e.scalar_tensor_tensor(
                                    out=x_[:, c0 + a0:c0 + a1], in0=pp[c][:, a0:a1], scalar=MODS[:, l, 32 + mi, r:r + 1],
                                    in1=x_[:, c0 + a0:c0 + a1], op0=ALU.mult, op1=ALU.add), reads=[bpp[c], bx_, bMODS], writes=[bx_])
                        if job:
                            job.compute(2 * mi)
                            job.compute(2 * mi + 1)
                        if do_stats:
                            if mi == 0:
                                S.op("act", lambda e, x_=x_: e.activation(out=acc[:], in_=x_[:], func=AF.Square), reads=[bx_], writes=[bacc])
                            else:
                                S.op("act", lambda e, x_=x_: e.activation(out=sqt[:], in_=x_[:], func=AF.Square), reads=[bx_], writes=[bsqt])
                                S.op("pool", lambda e: e.tensor_tensor(out=acc[:], in0=acc[:], in1=sqt[:], op=ALU.add), reads=[bacc, bsqt], writes=[bacc])
                        if not last:
                            S.dma("sp", xT[mi * 128:(mi + 1) * 128, :], x_[:], reads=[bx_], writes=[bxT[mi]])
                        else:
                            for tg in range(4):
                                p = ptr[oi % 2]
                                o_ = ost[oi % 2]
                                bp_, bo_ = bptr[oi % 2], bost[oi % 2]
                                oi += 1
                                for i in range(4):
                                    tt = tg * 4 + i
                                    S.op("pe", lambda e, p=p, i=i, tt=tt, x_=x_: e.transpose(
                                        out=p[:, i, :], in_=x_[:, CTXL + tt * 128:CTXL + (tt + 1) * 128], identity=identf[:]),
                                        reads=[bx_, bC], writes=[bp_])
                                S.op("act", lambda e, p=p, o_=o_: e.copy(out=o_[:], in_=p[:]), reads=[bp_], writes=[bo_])
                                S.dma("sp", out.rearrange("(t p) f -> p t f", p=128)[:, tg * 4:(tg + 1) * 4, mi * 128:(mi + 1) * 128], o_[:],
                                      reads=[bo_], writes=[bout])
                if do_stats:
                    for c, (c0, cl) in enumerate(CH5):
                        S.op("pe", lambda e, c=c, c0=c0, cl=cl: e.matmul(pp[c][:, 0:cl], lhsT=onesf[:], rhs=acc[:, c0:c0 + cl], start=True, stop=True),
                             reads=[bof, bacc], writes=[bpp[c]])
                        S.op("act", lambda e, c=c, c0=c0, cl=cl: e.activation(out=sqt[:, c0:c0 + cl], in_=pp[c][:, 0:cl], func=AF.Sqrt, bias=epsT[:], scale=1.0 / D),
                             reads=[bpp[c], bC], writes=[bsqt])
                    S.op("dve", lambda e: e.reciprocal(out=sqt[0:1, :], in_=sqt[0:1, :]), reads=[bsqt], writes=[bsqt])
                    S.dma("sp", RSTDd[:, :], sqt[0:1, :], reads=[bsqt], writes=[bRd])
                    have_rstd[0] = True
                S.barrier()

        def gemm_ch(W, ncols, Z):
            AT = ATS["AT"]
            with ExitStack() as ph:
                wc = [sb(ph, "cw%d" % i, [128, KT, 512], BF16) for i in range(2)]
                st = [sb(ph, "cst%d" % i, [128, T], F32) for i in range(2)]
                pp = [ps(ph, "cps%d" % i, [128, 512]) for i in range(5)]
                bw = [Buf(), Buf()]
                bst = [Buf(), Buf()]
                bpp = [Buf() for _ in range(5)]
                bz = Buf()
                for mg in range(ncols // 512):
                    w = wc[mg % 2]
                    S.dma("poolq", w[:], W[:, mg * 512:(mg + 1) * 512].rearrange("(kt p) n -> p kt n", p=128), writes=[bw[mg % 2]])
                    for m in range(4):
                        mi = mg * 4 + m
                        s_ = st[mi % 2]
                        for kt in range(KT):
                            for c, (c0, cl) in enumerate(CH5):
                                S.op("pe", lambda e, c=c, c0=c0, cl=cl, kt=kt, w=w, m=m: e.matmul(
                                    pp[c][:, 0:cl], lhsT=w[:, kt, m * 128:(m + 1) * 128], rhs=AT[:, kt, c0:c0 + cl],
                                    start=(kt == 0), stop=(kt == KT - 1)), reads=[bw[mg % 2], bAT[kt]], writes=[bpp[c]])
                        for c, (c0, cl) in enumerate(CH5):
                            if c % 2 == 0:
                                S.op("dve", lambda e, c=c, c0=c0, cl=cl, s_=s_: e.tensor_copy(out=s_[:, c0:c0 + cl], in_=pp[c][:, 0:cl]), reads=[bpp[c]], writes=[bst[mi % 2]])
                            else:
                                S.op("act", lambda e, c=c, c0=c0, cl=cl, s_=s_: e.copy(out=s_[:, c0:c0 + cl], in_=pp[c][:, 0:cl]), reads=[bpp[c]], writes=[bst[mi % 2]])
                        S.dma("sp", Z[mi * 128:(mi + 1) * 128, :], s_[:], reads=[bst[mi % 2]], writes=[bz])
                S.barrier()

        def s5_phase(j):
            GB = 8
            NC8 = 288
            bzu = Buf()
            with ExitStack() as ph0:
                ut = [sb(ph0, "ut%d" % i, [128, T], F32) for i in range(2)]
                ub = [sb(ph0, "ub%d" % i, [128, 8, NC8], BF16) for i in range(2)]
                but = [Buf(), Buf()]
                bub = [Buf(), Buf()]
                for m in range(8):
                    i = m % 2
                    S.dma("sp", ut[i][:], ZO[m * 128:(m + 1) * 128, :], writes=[but[i]])
                    if i == 0:
                        S.op("dve", lambda e, i=i: e.tensor_copy(out=ub[i][:], in_=ut[i][:].rearrange("p (c s) -> p s c", s=8)), reads=[but[i]], writes=[bub[i]])
                    else:
                        S.op("act", lambda e, i=i: e.copy(out=ub[i][:], in_=ut[i][:].rearrange("p (c s) -> p s c", s=8)), reads=[but[i]], writes=[bub[i]])
                    S.dma("sp", ZU[m * 128:(m + 1) * 128, :], ub[i][:].rearrange("p s c -> p (s c)"), reads=[bub[i]], writes=[bzu])

                S.barrier()
            with ExitStack() as ph:
                phs = ExitStack()
                TT = lambda en, o, a, b, op, rd, wr: S.op(en, lambda e: e.tensor_tensor(out=o, in0=a, in1=b, op=op), reads=rd, writes=wr)
                PW = sb(ph, "PW", [128, 2, 9, 64], F32)
                PN = sb(ph, "PN", [128, 2, 8, 64], F32)
                SELW = sb(ph, "SELW", [128, 2, 8, 64], F32)
                SELC = sb(ph, "SELC", [128, 2, 8, 64], F32)
                SELR = sb(ph, "SELR", [128, 2, 8, 64], F32)
                BB = sb(ph, "BB", [128, 2, 64, 16], F32)
                CC = sb(ph, "CC", [128, 2, 64, 16], F32)
                A1 = sb(ph, "A1", [128, 2, 64], F32)
                A2 = sb(ph, "A2", [128, 2, 64], F32)
                MASK = sb(ph, "MASK", [128, 2, 128], F32)
                JS = 8
                NS = NC8 // JS
                AJ1 = sb(ph, "AJ1", [128, 2, 64], F32)
                AJ2 = sb(ph, "AJ2", [128, 2, 64], F32)
                J2 = 6
                NS2 = NS // J2
                AK1 = sb(ph, "AK1", [128, 2, 64], F32)
                AK2 = sb(ph, "AK2", [128, 2, 64], F32)
                pq = [ps(ph, "s5p%d" % i, [128, 512]) for i in range(4)]
                px = [ps(ph, "s5x%d" % i, [128, 512]) for i in range(2)]
                py = [ps(ph, "s5y%d" % i, [128, 512]) for i in range(2)]
                bS = Buf()
                bpq = [Buf() for _ in range(4)]
                bpx = [Buf(), Buf()]
                bpy = [Buf(), Buf()]
                L0 = sb(phs, "L0", [64, 2, 2, 64], F32)
                LR = sb(phs, "LR", [128, 64], F32)
                LI = sb(phs, "LI", [128, 64], F32)
                DT = sb(phs, "DT", [128, 64], F32)
                TM = sb(phs, "TM", [128, 24, 64], F32)
                BB0 = sb(phs, "BB0", [128, 2, 64, 16], F32)
                Cn = sb(phs, "Cn", [128, 2, 8, 2, 64], F32)
                E8 = sb(phs, "E8", [8, 128], F32)
                G8f = sb(phs, "G8f", [8, 128], F32)
                G8b = sb(phs, "G8b", [8, 128], F32)
                w1t = sb(phs, "s5w1", [128, 1024], F32)
                w2t = sb(phs, "s5w2", [128, 1024], F32)

                for ri, nm in enumerate(("s5_lam_re", "s5_lam_im")):
                    S.dma("sp", L0[:, ri, :, :], od[nm][j].rearrange("d g n -> g d n"), writes=[bS])
                for d in range(2):
                    ld = od["s5_log_dt"][j, d:d + 1, :]
                    S.dma("sp", DT[64 * d:64 * d + 64, :], bass.AP(tensor=ld.tensor, offset=ld.offset, ap=[[0, 64], [1, 64]]), writes=[bS])
                    for ri, nm in enumerate(("s5_b_re", "s5_b_im")):
                        S.dma("sp", BB0[64 * d:64 * d + 64, ri, :, :], od[nm][j, d].rearrange("g n q -> n g q"), writes=[bS])
                    for ri, nm in enumerate(("s5_c_re", "s5_c_im")):
                        S.dma("sp", Cn[:, ri, :, d, :], od[nm][j, d].rearrange("(gb g8) p n -> (g8 p) gb n", g8=8), writes=[bS])
                for ri, dst in enumerate((LR, LI)):
                    S.op("pe", lambda e, ri=ri: e.transpose(out=pq[0][:, ri * 64:(ri + 1) * 64], in_=L0[:, ri, :, :].rearrange("g d n -> g (d n)"), identity=identf[0:64, 0:64]),
                         reads=[bS, bC], writes=[bpq[0]])
                S.op("dve", lambda e: e.tensor_copy(out=LR[:], in_=pq[0][:, 0:64]), reads=[bpq[0]], writes=[bS])
                S.op("dve", lambda e: e.tensor_copy(out=LI[:], in_=pq[0][:, 64:128]), reads=[bpq[0]], writes=[bS])
                pi_ = 1
                for ri in range(2):
                    for gb in range(8):
                        p_ = pq[pi_ % 4]
                        bp_ = bpq[pi_ % 4]
                        pi_ += 1
                        S.op("pe", lambda e, p_=p_, ri=ri, gb=gb: e.transpose(out=p_[:, 0:128], in_=Cn[:, ri, gb, :, :].rearrange("p d n -> p (d n)"), identity=identf[:]),
                             reads=[bS, bC], writes=[bp_])
                        S.op("dve", lambda e, p_=p_, ri=ri, gb=gb: e.tensor_copy(out=CC[:, ri, gb * 8:(gb + 1) * 8, :].rearrange("p g q -> p (g q)"), in_=p_[:, 0:128]),
                             reads=[bp_], writes=[bS])

                tm = lambda k: TM[:, k, :]
                V = lambda o, a, b, op: TT("dve", o, a, b, op, [bS], [bS])
                S.op("act", lambda e: e.activation(out=DT[:], in_=DT[:], func=AF.Exp), reads=[bS], writes=[bS])
                V(tm(0), LR[:], DT[:], ALU.mult)
                V(tm(1), LI[:], DT[:], ALU.mult)
                S.op("act", lambda e: e.activation(out=tm(2), in_=tm(0), func=AF.Exp), reads=[bS], writes=[bS])
                S.op("act", lambda e: e.activation(out=tm(3), in_=tm(1), func=AF.Sin, scale=1.0 / 16), reads=[bS], writes=[bS])
                S.op("act", lambda e: e.activation(out=tm(5), in_=tm(1), func=AF.Sin, scale=1.0 / 8), reads=[bS], writes=[bS])
                V(tm(4), tm(3), tm(3), ALU.mult)
                S.op("dve", lambda e: e.tensor_scalar(out=tm(4), in0=tm(4), scalar1=-2.0, scalar2=1.0, op0=ALU.mult, op1=ALU.add), reads=[bS], writes=[bS])
                for _ in range(3):
                    V(tm(6), tm(4), tm(4), ALU.mult)
                    V(tm(7), tm(5), tm(5), ALU.mult)
                    V(tm(8), tm(4), tm(5), ALU.mult)
                    V(tm(4), tm(6), tm(7), ALU.subtract)
                    S.op("dve", lambda e: e.tensor_scalar(out=tm(5), in0=tm(8), scalar1=2.0, scalar2=None, op0=ALU.mult), reads=[bS], writes=[bS])
                ar, ai = PW[:, 0, 1, :], PW[:, 1, 1, :]
                V(ar, tm(2), tm(4), ALU.mult)
                V(ai, tm(2), tm(5), ALU.mult)
                S.op("dve", lambda e: e.memset(PW[:, 0, 0, :], 1.0), reads=[], writes=[bS])
                S.op("dve", lambda e: e.memset(PW[:, 1, 0, :], 0.0), reads=[], writes=[bS])
                S.op("dve", lambda e: e.memset(PN[:, 0, 0, :], 1.0), reads=[], writes=[bS])
                S.op("dve", lambda e: e.memset(PN[:, 1, 0, :], 0.0), reads=[], writes=[bS])

                def cmul(or_, oi_, xr, xi, yr, yi):
                    V(tm(9), xr, yr, ALU.mult)
                    V(tm(10), xi, yi, ALU.mult)
                    V(tm(11), xr, yi, ALU.mult)
                    V(tm(12), xi, yr, ALU.mult)
                    V(or_, tm(9), tm(10), ALU.subtract)
                    V(oi_, tm(11), tm(12), ALU.add)
                for k in range(2, 9):
                    cmul(PW[:, 0, k, :], PW[:, 1, k, :], PW[:, 0, k - 1, :], PW[:, 1, k - 1, :], ar, ai)
                V(tm(13), tm(2), tm(2), ALU.mult)
                S.op("dve", lambda e: e.reciprocal(out=tm(13), in_=tm(13)), reads=[bS], writes=[bS])
                V(PN[:, 0, 1, :], ar, tm(13), ALU.mult)
                V(PN[:, 1, 1, :], ai, tm(13), ALU.mult)
                S.op("dve", lambda e: e.tensor_scalar(out=PN[:, 1, 1, :], in0=PN[:, 1, 1, :], scalar1=-1.0, scalar2=None, op0=ALU.mult), reads=[bS], writes=[bS])
                for k in range(2, 8):
                    cmul(PN[:, 0, k, :], PN[:, 1, k, :], PN[:, 0, k - 1, :], PN[:, 1, k - 1, :], PN[:, 0, 1, :], PN[:, 1, 1, :])
                V(tm(14), LR[:], LR[:], ALU.mult)
                V(tm(15), LI[:], LI[:], ALU.mult)
                V(tm(14), tm(14), tm(15), ALU.add)
                S.op("dve", lambda e: e.reciprocal(out=tm(14), in_=tm(14)), reads=[bS], writes=[bS])
                S.op("dve", lambda e: e.tensor_scalar(out=tm(15), in0=ar, scalar1=-1.0, scalar2=None, op0=ALU.add), reads=[bS], writes=[bS])
                V(tm(16), tm(15), LR[:], ALU.mult)
                V(tm(17), ai, LI[:], ALU.mult)
                V(tm(16), tm(16), tm(17), ALU.add)
                V(tm(18), ai, LR[:], ALU.mult)
                V(tm(19), tm(15), LI[:], ALU.mult)
                V(tm(18), tm(18), tm(19), ALU.subtract)
                V(tm(16), tm(16), tm(14), ALU.mult)
                V(tm(18), tm(18), tm(14), ALU.mult)
                fr = vw(tm(16), [[1, 64], [0, 16]])
                fi = vw(tm(18), [[1, 64], [0, 16]])
                b0r, b0i = BB0[:, 0, :, :], BB0[:, 1, :, :]
                w1 = w1t[:].rearrange("p (g q) -> p g q", q=16)
                w2 = w2t[:].rearrange("p (g q) -> p g q", q=16)
                V(w1, b0r, fr, ALU.mult)
                V(w2, b0i, fi, ALU.mult)
                V(BB[:, 0, :, :], w1, w2, ALU.subtract)
                V(w1, b0r, fi, ALU.mult)
                V(w2, b0i, fr, ALU.mult)
                V(BB[:, 1, :, :], w1, w2, ALU.add)
                for ri in range(2):
                    for s_ in range(8):
                        for (lo, hi, fwd) in ((0, 64, True), (64, 128, False)):
                            kw_ = 7 - s_ if fwd else s_
                            kc_ = 7 - s_ if fwd else s_
                            kr_ = s_ + 1 if fwd else 8 - s_
                            S.op("pool", lambda e, lo=lo, hi=hi, ri=ri, s_=s_, kw_=kw_: e.tensor_copy(out=SELW[lo:hi, ri, s_, :], in_=PW[lo:hi, ri, kw_, :]), reads=[bS], writes=[bS])
                            S.op("dve", lambda e, lo=lo, hi=hi, ri=ri, s_=s_, kc_=kc_: e.tensor_copy(out=SELC[lo:hi, ri, s_, :], in_=PN[lo:hi, ri, kc_, :]), reads=[bS], writes=[bS])
                            S.op("act", lambda e, lo=lo, hi=hi, ri=ri, s_=s_, kr_=kr_: e.copy(out=SELR[lo:hi, ri, s_, :], in_=PW[lo:hi, ri, kr_, :]), reads=[bS], writes=[bS])
                for tl_ in (SELC, SELR):
                    S.op("pool", lambda e, tl_=tl_: e.tensor_scalar(out=tl_[:, 1, :, :], in0=tl_[:, 1, :, :], scalar1=-1.0, scalar2=None, op0=ALU.mult), reads=[bS], writes=[bS])
                S.op("pool", lambda e: e.tensor_scalar(out=CC[:, 1, :, :], in0=CC[:, 1, :, :], scalar1=-1.0, scalar2=None, op0=ALU.mult), reads=[bS], writes=[bS])
                S.op("dve", lambda e: e.tensor_copy(out=A1[:, 0, :], in_=PW[:, 0, 8, :]), reads=[bS], writes=[bS])
                S.op("dve", lambda e: e.tensor_copy(out=A1[:, 1, :], in_=PW[:, 0, 8, :]), reads=[bS], writes=[bS])
                S.op("dve", lambda e: e.tensor_scalar(out=A2[:, 0, :], in0=PW[:, 1, 8, :], scalar1=-1.0, scalar2=None, op0=ALU.mult), reads=[bS], writes=[bS])
                S.op("dve", lambda e: e.tensor_copy(out=A2[:, 1, :], in_=PW[:, 1, 8, :]), reads=[bS], writes=[bS])
                S.op("dve", lambda e: e.tensor_copy(out=tm(20), in_=PW[:, 0, 8, :]), reads=[bS], writes=[bS])
                S.op("dve", lambda e: e.tensor_copy(out=tm(21), in_=PW[:, 1, 8, :]), reads=[bS], writes=[bS])
                for it_ in range(3):
                    V(tm(22), tm(20), tm(20), ALU.mult)
                    V(tm(23), tm(21), tm(21), ALU.mult)
                    V(tm(19), tm(20), tm(21), ALU.mult)
                    V(tm(20), tm(22), tm(23), ALU.subtract)
                    S.op("dve", lambda e: e.tensor_scalar(out=tm(21), in0=tm(19), scalar1=2.0, scalar2=None, op0=ALU.mult), reads=[bS], writes=[bS])
                S.op("dve", lambda e: e.tensor_copy(out=AJ1[:, 0, :], in_=tm(20)), reads=[bS], writes=[bS])
                S.op("dve", lambda e: e.tensor_copy(out=AJ1[:, 1, :], in_=tm(20)), reads=[bS], writes=[bS])
                S.op("dve", lambda e: e.tensor_scalar(out=AJ2[:, 0, :], in0=tm(21), scalar1=-1.0, scalar2=None, op0=ALU.mult), reads=[bS], writes=[bS])
                S.op("dve", lambda e: e.tensor_copy(out=AJ2[:, 1, :], in_=tm(21)), reads=[bS], writes=[bS])
                cmul(tm(14), tm(15), tm(20), tm(21), tm(20), tm(21))
                cmul(tm(16), tm(17), tm(14), tm(15), tm(14), tm(15))
                cmul(tm(18), tm(19), tm(16), tm(17), tm(14), tm(15))
                S.op("dve", lambda e: e.tensor_copy(out=AK1[:, 0, :], in_=tm(18)), reads=[bS], writes=[bS])
                S.op("dve", lambda e: e.tensor_copy(out=AK1[:, 1, :], in_=tm(18)), reads=[bS], writes=[bS])
                S.op("dve", lambda e: e.tensor_scalar(out=AK2[:, 0, :], in0=tm(19), scalar1=-1.0, scalar2=None, op0=ALU.mult), reads=[bS], writes=[bS])
                S.op("dve", lambda e: e.tensor_copy(out=AK2[:, 1, :], in_=tm(19)), reads=[bS], writes=[bS])
                for tl_, fillcmp in ((E8, None), (G8f, None), (G8b, None)):
                    S.op("pool", lambda e, tl_=tl_: e.memset(tl_[:], 1.0), reads=[], writes=[bS])
                S.op("pool", lambda e: e.affine_select(out=E8[:], in_=E8[:], pattern=[[1, 128]], compare_op=ALU.is_ge, fill=0.0, base=0, channel_multiplier=-16), reads=[bS], writes=[bS])
                S.op("pool", lambda e: e.affine_select(out=E8[:], in_=E8[:], pattern=[[-1, 128]], compare_op=ALU.is_ge, fill=0.0, base=15, channel_multiplier=16), reads=[bS], writes=[bS])
                S.op("pool", lambda e: e.affine_select(out=G8f[:], in_=G8f[:], pattern=[[1, 128]], compare_op=ALU.is_ge, fill=0.0, base=0, channel_multiplier=-16), reads=[bS], writes=[bS])
                S.op("pool", lambda e: e.affine_select(out=G8b[:], in_=G8b[:], pattern=[[-1, 128]], compare_op=ALU.is_ge, fill=0.0, base=15, channel_multiplier=16), reads=[bS], writes=[bS])
                for d, G_ in enumerate((G8f, G8b)):
                    S.op("pe", lambda e, d=d, G_=G_: e.matmul(pq[d][:, 0:128], lhsT=E8[:], rhs=G_[:], start=True, stop=True), reads=[bS], writes=[bpq[d]])
                    S.op("dve", lambda e, d=d: e.tensor_copy(out=MASK[:, d, :], in_=pq[d][:, 0:128]), reads=[bpq[d]], writes=[bS])

                def prod_tab(dstr, dsti, SEL, SRC, g0, neg_im, bdst):
                    def sv(ri):
                        a = SEL[:, ri, :, :]
                        return vw(a, [[1, GB], [64, 8], [0, 16]], off=g0)
                    def bv(ri):
                        a = SRC[:, ri, :, :]
                        return vw(a, [[16, GB], [0, 8], [1, 16]], off=g0 * 16)
                    o4 = lambda a: a.rearrange("p g (s q) -> p g s q", q=16)
                    T1, T2 = o4(t1[:]), o4(t2[:])
                    TT("dve", T1, sv(0), bv(0), ALU.mult, [bS], [bt1])
                    TT("pool", T2, sv(1), bv(1), ALU.mult, [bS], [bt2])
                    TT("dve", o4(dstr), T1, T2, ALU.subtract, [bt1, bt2], [bdst])
                    TT("dve", T1, sv(0), bv(1), ALU.mult, [bS, bdst], [bt1])
                    TT("pool", T2, sv(1), bv(0), ALU.mult, [bS, bdst], [bt2])
                    if neg_im:
                        S.op("dve", lambda e: e.scalar_tensor_tensor(out=dsti, in0=t1[:], scalar=-1.0, in1=t2[:], op0=ALU.mult, op1=ALU.subtract),
                             reads=[bt1, bt2], writes=[bdst])
                    else:
                        TT("dve", o4(dsti), T1, T2, ALU.add, [bt1, bt2], [bdst])

                S.barrier()
                phs.close()
                WINtab = sb(ph, "WINtab", [128, 2, GB, 128], F32)
                CT7tab = sb(ph, "CT7tab", [128, 2, GB, 128], F32)
                ROUTtab = sb(ph, "ROUTtab", [128, 2, GB, 128], BF16)
                WIND = [sb(ph, "WIND%d" % d, [128, 2, GB, 128], F32) for d in range(2)]
                ROUTDs = [[sb(ph, "ROUTD%d_%d" % (d, i), [128, 2, GB, 128], BF16) for d in range(2)] for i in range(2)]
                t1 = sb(ph, "s5t1", [128, GB, 128], F32)
                t2 = sb(ph, "s5t2", [128, GB, 128], F32)
                WINTs = [sb(ph, "WINT%d" % i, [128, GB, 2, 128], BF16) for i in range(2)]
                TOEPs = [sb(ph, "TOEP%d" % i, [128, GB, 2, 128], BF16) for i in range(2)]
                U8bs = [sb(ph, "U8b%d" % i, [128, GB, NC8], BF16) for i in range(2)]
                X = sb(ph, "X", [128, 2, GB, NC8], F32)
                Hs = sb(ph, "Hs", [128, 2, GB, NC8], BF16)
                Hb = sb(ph, "Hb", [128, 2, GB, NC8], BF16)
                Hc = [sb(ph, "Hc%d" % i, [128, 2, GB], F32) for i in range(2)]
                XA = [sb(ph, "XA%d" % i, [128, 2, GB, NS], F32) for i in range(3)]
                PB = sb(ph, "PB", [128, 2, GB, NS], F32)
                QB = sb(ph, "QB", [128, 2, GB, NS], F32)
                GH = sb(ph, "GH", [128, 2, GB, NS], F32)
                bXA = [Buf(), Buf(), Buf()]
                bGH = Buf()
                XB = [sb(ph, "XB%d" % i, [128, 2, GB, NS2], F32) for i in range(2)]
                PB2 = sb(ph, "PB2", [128, 2, GB, NS2], F32)
                QB2 = sb(ph, "QB2", [128, 2, GB, NS2], F32)
                G2 = sb(ph, "G2", [128, 2, GB, NS2], F32)
                bXB = [Buf(), Buf()]
                bG2 = Buf()
                Pq = sb(ph, "Pq", [128, 2, GB], F32)
                Qq = sb(ph, "Qq", [128, 2, GB], F32)
                bWt, bCt, bRt, bt1, bt2, bWINT, bTOEP, bU8, bX, bHs, bHb, bY, bzy = (Buf() for _ in range(13))
                bHc = [Buf(), Buf()]
                bPq, bQq = Buf(), Buf()

                bWD = Buf()
                bRDs, bWINTs, bTOEPs, bU8s = [Buf(), Buf()], [Buf(), Buf()], [Buf(), Buf()], [Buf(), Buf()]
                for d in range(2):
                    S.op("pool", lambda e, d=d: e.memset(WIND[d][:], 0.0), reads=[], writes=[bWD])
                    for i_ in range(2):
                        S.op("pool", lambda e, d=d, i_=i_: e.memset(ROUTDs[i_][d][:], 0.0), reads=[], writes=[bRDs[i_]])
                S.op("pool", lambda e: e.memset(Hb[:], 0.0), reads=[], writes=[bHb])
                Yv = X[:, 0, :, :]

                def stageA(gbi):
                    g0 = gbi * GB
                    par = gbi % 2
                    prod_tab(WINtab[:, 0, :, :], WINtab[:, 1, :, :], SELW, BB, g0, False, bWt)
                    prod_tab(CT7tab[:, 0, :, :], CT7tab[:, 1, :, :], SELC, CC, g0, False, bCt)
                    prod_tab(ROUTtab[:, 0, :, :], ROUTtab[:, 1, :, :], SELR, CC, g0, False, bRt)
                    for d in range(2):
                        lo, hi = 64 * d, 64 * d + 64
                        S.op("act", lambda e, d=d, lo=lo, hi=hi: e.copy(out=WIND[d][lo:hi, :, :, :], in_=WINtab[lo:hi, :, :, :]), reads=[bWt], writes=[bWD])
                        S.op("act", lambda e, d=d, lo=lo, hi=hi: e.copy(out=ROUTDs[par][d][lo:hi, :, :, :], in_=ROUTtab[lo:hi, :, :, :]), reads=[bRt], writes=[bRDs[par]])
                    for s_ in range(8):
                        src = ZU[g0 * 16:(g0 + GB) * 16, s_ * NC8:(s_ + 1) * NC8].rearrange("(g q) c -> q g c", q=16)
                        S.dma("sp", U8bs[par][16 * s_:16 * s_ + 16, :, :], src, reads=[bzu], writes=[bU8s[par]])
                    for g in range(GB):
                        p_ = pq[g % 2]
                        bp_ = bpq[g % 2]
                        for ri in range(2):
                            S.op("pe", lambda e, p_=p_, ri=ri, g=g: e.transpose(out=p_[:, ri * 128:(ri + 1) * 128], in_=WINtab[:, ri, g, :], identity=identf[:]),
                                 reads=[bWt, bC], writes=[bp_])
                        S.op("act", lambda e, p_=p_, g=g: e.copy(out=WINTs[par][:, g, :, :].rearrange("p r n -> p (r n)"), in_=p_[:, 0:256]), reads=[bp_], writes=[bWINTs[par]])
                        p2 = pq[2 + g % 2]
                        bp2 = bpq[2 + g % 2]
                        for d in range(2):
                            lo, hi = 64 * d, 64 * d + 64
                            S.op("pe", lambda e, p2=p2, d=d, g=g: e.matmul(p2[:, d * 128:(d + 1) * 128], lhsT=WIND[d][:, 0, g, :], rhs=CT7tab[:, 0, g, :], start=True, stop=False),
                                 reads=[bWD, bCt], writes=[bp2])
                            S.op("pe", lambda e, p2=p2, d=d, g=g: e.matmul(p2[:, d * 128:(d + 1) * 128], lhsT=WIND[d][:, 1, g, :], rhs=CT7tab[:, 1, g, :], start=False, stop=True),
                                 reads=[bWD, bCt], writes=[bp2])
                        S.op("dve", lambda e, p2=p2, g=g: e.tensor_tensor(out=TOEPs[par][:, g, :, :].rearrange("p d n -> p (d n)"), in0=p2[:, 0:256], in1=MASK[:].rearrange("p d n -> p (d n)"), op=ALU.mult),
                             reads=[bp2, bS], writes=[bTOEPs[par]])

                def stageB(gbi):
                    g0 = gbi * GB
                    par = gbi % 2
                    xi_ = 0
                    for g in range(GB):
                        for ri in range(2):
                            p_ = px[xi_ % 2]
                            bp_ = bpx[xi_ % 2]
                            xi_ += 1
                            S.op("pe", lambda e, p_=p_, g=g, ri=ri: e.matmul(p_[:, 0:NC8], lhsT=WINTs[par][:, g, ri, :], rhs=U8bs[par][:, g, :], start=True, stop=True),
                                 reads=[bWINTs[par], bU8s[par]], writes=[bp_])
                            S.op("act", lambda e, p_=p_, g=g, ri=ri: e.copy(out=vw(X[0:64, ri, g, 0:1], [[1, NS], [NS, JS]]),
                                                                           in_=vw(p_[0:64, 0:1], [[JS, NS], [1, JS]])), reads=[bp_], writes=[bX])
                            S.op("dve", lambda e, p_=p_, g=g, ri=ri: e.tensor_copy(out=vw(X[64:128, ri, g, 0:1], [[1, 4], [NS, JS]]),
                                                                                  in_=vw(p_[64:128, 0:1], [[-JS, 4], [-1, JS]], off=31)), reads=[bp_], writes=[bX])
                            S.op("dve", lambda e, p_=p_, g=g, ri=ri: e.tensor_copy(out=vw(X[64:128, ri, g, 0:1], [[1, 32], [NS, JS]], off=4),
                                                                                  in_=vw(p_[64:128, 0:1], [[-JS, 32], [-1, JS]], off=287)), reads=[bp_], writes=[bX])
                    a1b = vw(A1[:, 0, 0:1], [[64, 2], [1, GB], [0, NS]], off=g0)
                    a2b = vw(A2[:, 0, 0:1], [[64, 2], [1, GB], [0, NS]], off=g0)
                    aJ1b = vw(AJ1[:, 0, 0:1], [[64, 2], [1, GB], [0, NS2]], off=g0)
                    aJ2b = vw(AJ2[:, 0, 0:1], [[64, 2], [1, GB], [0, NS2]], off=g0)
                    aK1 = AK1[:, :, g0:g0 + GB]
                    aK2 = AK2[:, :, g0:g0 + GB]
                    x8v = lambda jj: vw(X[:, 0, 0, 0:1], [[GB * NC8, 2], [NC8, GB], [1, NS]], off=jj * NS)
                    hsv = lambda jj: vw(Hs[:, 0, 0, 0:1], [[GB * NC8, 2], [NC8, GB], [JS, NS]], off=jj)
                    swp = lambda t_, n_: vw(t_[:, 0, 0, 0:1], [[-GB * n_, 2], [n_, GB], [1, n_]], off=GB * n_)
                    lv2 = lambda t_, j2: vw(t_[:, 0, 0, 0:1], [[GB * NS, 2], [NS, GB], [J2, NS2]], off=j2)

                    def cmad(dst, bdst, src, bsrc, xin, bxin, m1, m2, P_, Q_, n_):
                        S.op("dve", lambda e: e.tensor_tensor(out=P_[:], in0=src[:], in1=m1, op=ALU.mult), reads=[bsrc, bS], writes=[bPq])
                        S.op("pool", lambda e: e.tensor_tensor(out=Q_[:], in0=swp(src, n_), in1=m2, op=ALU.mult), reads=[bsrc, bS], writes=[bQq])
                        S.op("dve", lambda e: e.tensor_tensor(out=P_[:], in0=P_[:], in1=xin, op=ALU.add), reads=[bPq, bxin], writes=[bPq])
                        S.op("dve", lambda e: e.tensor_tensor(out=dst[:], in0=P_[:], in1=Q_[:], op=ALU.add), reads=[bPq, bQq], writes=[bdst])

                    S.op("dve", lambda e: e.tensor_copy(out=XA[0][:], in_=x8v(0)), reads=[bX], writes=[bXA[0]])
                    for jj in range(1, JS):
                        cmad(XA[jj % 2], bXA[jj % 2], XA[(jj - 1) % 2], bXA[(jj - 1) % 2], x8v(jj), bX, a1b, a2b, PB, QB, NS)
                    XS1, bXS1 = XA[(JS - 1) % 2], bXA[(JS - 1) % 2]
                    S.op("dve", lambda e: e.tensor_copy(out=XB[0][:], in_=lv2(XS1, 0)), reads=[bXS1], writes=[bXB[0]])
                    for j2 in range(1, J2):
                        cmad(XB[j2 % 2], bXB[j2 % 2], XB[(j2 - 1) % 2], bXB[(j2 - 1) % 2], lv2(XS1, j2), bXS1, aJ1b, aJ2b, PB2, QB2, NS2)
                    XS2, bXS2 = XB[(J2 - 1) % 2], bXB[(J2 - 1) % 2]
                    S.op("pool", lambda e: e.memset(G2[:], 0.0), reads=[], writes=[bG2])
                    for C in range(NS2 - 1):
                        cur = G2[:, :, :, C]
                        csw = vw(G2[:, 0, 0, 0:1], [[-GB * NS2, 2], [NS2, GB]], off=GB * NS2 + C)
                        S.op("dve", lambda e, cur=cur: e.tensor_tensor(out=Pq[:], in0=cur, in1=aK1, op=ALU.mult), reads=[bG2, bS], writes=[bPq])
                        S.op("dve", lambda e, csw=csw: e.tensor_tensor(out=Qq[:], in0=csw, in1=aK2, op=ALU.mult), reads=[bG2, bS], writes=[bQq])
                        S.op("dve", lambda e: e.tensor_tensor(out=Pq[:], in0=Pq[:], in1=Qq[:], op=ALU.add), reads=[bPq, bQq], writes=[bPq])
                        S.op("dve", lambda e, C=C: e.tensor_tensor(out=G2[:, :, :, C + 1], in0=Pq[:], in1=XS2[:, :, :, C], op=ALU.add), reads=[bPq, bXS2], writes=[bG2])
                    S.op("dve", lambda e: e.tensor_copy(out=lv2(GH, 0), in_=G2[:]), reads=[bG2], writes=[bGH])
                    gsrc, bgsrc = G2, bG2
                    for j2 in range(J2 - 1):
                        gd_, bgd_ = XB[j2 % 2], bXB[j2 % 2]
                        cmad(gd_, bgd_, gsrc, bgsrc, lv2(XS1, j2), bXS1, aJ1b, aJ2b, PB2, QB2, NS2)
                        S.op("act", lambda e, gd_=gd_, j2=j2: e.copy(out=lv2(GH, j2 + 1), in_=gd_[:]), reads=[bgd_], writes=[bGH])
                        gsrc, bgsrc = gd_, bgd_
                    S.op("act", lambda e: e.copy(out=hsv(0), in_=GH[:]), reads=[bGH], writes=[bHs])
                    hsrc, bhsrc = GH, bGH
                    for jj in range(JS - 1):
                        hd, bhd = XA[jj % 3], bXA[jj % 3]
                        cmad(hd, bhd, hsrc, bhsrc, x8v(jj), bX, a1b, a2b, PB, QB, NS)
                        S.op("act", lambda e, hd=hd, jj=jj: e.copy(out=hsv(jj + 1), in_=hd[:]), reads=[bhd], writes=[bHs])
                        hsrc, bhsrc = hd, bhd
                    S.op("dve", lambda e: e.tensor_copy(out=Hb[64:128, :, :, 0:32].rearrange("p r g c -> p (r g) c"),
                                                         in_=vw(Hs[64:128, 0, 0, 0:1], [[NC8, 2 * GB], [-1, 32]], off=31)), reads=[bHs], writes=[bHb])
                    S.op("dve", lambda e: e.tensor_copy(out=Hb[64:128, :, :, 32:NC8].rearrange("p r g c -> p (r g) c"),
                                                         in_=vw(Hs[64:128, 0, 0, 0:1], [[NC8, 2 * GB], [-1, 256]], off=NC8 - 1)), reads=[bHs], writes=[bHb])
                    for g in range(GB):
                        p_ = py[g % 2]
                        bp_ = bpy[g % 2]
                        for d in range(2):
                            S.op("pe", lambda e, p_=p_, g=g, d=d: e.matmul(p_[:, 0:NC8], lhsT=TOEPs[par][:, g, d, :], rhs=U8bs[par][:, g, :], start=(d == 0), stop=False),
                                 reads=[bTOEPs[par], bU8s[par]], writes=[bp_])
                        for d in range(2):
                            Hsrc = Hs if d == 0 else Hb
                            bH = bHs if d == 0 else bHb
                            for ri in range(2):
                                S.op("pe", lambda e, p_=p_, g=g, d=d, ri=ri, Hsrc=Hsrc: e.matmul(p_[:, 0:NC8], lhsT=ROUTDs[par][d][:, ri, g, :], rhs=Hsrc[:, ri, g, :], start=False, stop=(d == 1 and ri == 1)),
                                     reads=[bRDs[par], bH], writes=[bp_])
                        if g % 2 == 0:
                            S.op("act", lambda e, g=g, p_=p_: e.copy(out=Yv[:, g, :], in_=p_[:, 0:NC8]), reads=[bp_], writes=[bX])
                        else:
                            S.op("dve", lambda e, g=g, p_=p_: e.tensor_copy(out=Yv[:, g, :], in_=p_[:, 0:NC8]), reads=[bp_], writes=[bX])
                    for t_ in range(8):
                        dst = ZY[g0 * 16:(g0 + GB) * 16, t_ * NC8:(t_ + 1) * NC8].rearrange("(g p) c -> p g c", p=16)
                        S.dma("sp", dst, Yv[16 * t_:16 * t_ + 16, :, :], reads=[bX], writes=[bzy])

                NB = 64 // GB
                stageA(0)
                for gbi in range(NB):
                    S.begin_capture()
                    stageB(gbi)
                    lb = S.end_capture()
                    la = []
                    if gbi + 1 < NB:
                        S.begin_capture()
                        stageA(gbi + 1)
                        la = S.end_capture()
                    S.replay(lb, la)
                S.barrier()

        def s5_finish(j):
            AT = ATS["AT"]
            with ExitStack() as ph:
                YT = sb(ph, "YT", [128, 8, T], BF16)
                yz = [sb(ph, "yz%d" % i, [128, T], F32) for i in range(2)]
                uu = [sb(ph, "uu%d" % i, [128, T], F32) for i in range(2)]
                gw = [sb(ph, "gw%d" % i, [128, 8, 128], BF16) for i in range(2)]
                DS = sb(ph, "DS", [128, 8], F32)
                GBI = sb(ph, "GBI", [128, 8], F32)
                pp = [ps(ph, "fps%d" % i, [128, 512]) for i in range(5)]
                byz = [Buf(), Buf()]
                buu = [Buf(), Buf()]
                bgw = [Buf(), Buf()]
                bpp = [Buf() for _ in range(5)]
                bYT = [Buf() for _ in range(8)]
                bK = Buf()
                load_T(DS[:], od["s5_d"][j].rearrange("(m p) -> m p", p=128), 8, pp[0], bpp[0], bK)
                load_T(GBI[:], od["s5_glu_b"][j].rearrange("(m p) -> m p", p=128), 8, pp[0], bpp[0], bK)
                for m in range(8):
                    i = m % 2
                    S.dma("sp", yz[i][:], ZY[m * 128:(m + 1) * 128, :], writes=[byz[i]])
                    S.dma("sp", uu[i][:], ZO[m * 128:(m + 1) * 128, :], writes=[buu[i]])
                    S.op("dve", lambda e, i=i, m=m: e.scalar_tensor_tensor(
                        out=uu[i][:].rearrange("p (c s) -> p c s", s=8), in0=uu[i][:].rearrange("p (c s) -> p c s", s=8), scalar=DS[:, m:m + 1],
                        in1=yz[i][:].rearrange("p (s c) -> p c s", s=8), op0=ALU.mult, op1=ALU.add), reads=[buu[i], byz[i], bK], writes=[buu[i]])
                    S.op("dve", lambda e, i=i: e.tensor_tensor(out=yz[i][:], in0=uu[i][:], in1=uu[i][:], op=ALU.mult), reads=[buu[i], byz[i]], writes=[byz[i]])
                    S.op("dve", lambda e, i=i: e.tensor_scalar(out=yz[i][:], in0=yz[i][:], scalar1=0.044715, scalar2=1.0, op0=ALU.mult, op1=ALU.add), reads=[byz[i]], writes=[byz[i]])
                    S.op("dve", lambda e, i=i: e.tensor_tensor(out=yz[i][:], in0=yz[i][:], in1=uu[i][:], op=ALU.mult), reads=[buu[i], byz[i]], writes=[byz[i]])
                    S.op("act", lambda e, i=i: e.activation(out=yz[i][:], in_=yz[i][:], func=AF.Sigmoid, scale=2.0 * math.sqrt(2.0 / math.pi)), reads=[byz[i]], writes=[byz[i]])
                    S.op("dve", lambda e, i=i, m=m: e.tensor_tensor(out=YT[:, m, :], in0=yz[i][:], in1=uu[i][:], op=ALU.mult), reads=[buu[i], byz[i]], writes=[bYT[m]])
                for mo in range(8):
                    i = mo % 2
                    S.dma("poolq", gw[i][:], od["s5_glu_w"][j][:, mo * 128:(mo + 1) * 128].rearrange("(kt p) n -> p kt n", p=128), writes=[bgw[i]])
                    S.dma("sp", uu[i][:], ZO[2048 + mo * 128:2048 + (mo + 1) * 128, :], writes=[buu[i]])
                    S.op("act", lambda e, i=i: e.activation(out=uu[i][:], in_=uu[i][:], func=AF.Silu), reads=[buu[i]], writes=[buu[i]])
                    for kt in range(8):
                        for c, (c0, cl) in enumerate(CH5):
                            S.op("pe", lambda e, c=c, c0=c0, cl=cl, kt=kt, i=i: e.matmul(pp[c][:, 0:cl], lhsT=gw[i][:, kt, :], rhs=YT[:, kt, c0:c0 + cl],
                                                                                       start=(kt == 0), stop=(kt == 7)), reads=[bgw[i], bYT[kt]], writes=[bpp[c]])
                    for c, (c0, cl) in enumerate(CH5):
                        S.op("act", lambda e, c=c, c0=c0, cl=cl, i=i, mo=mo: e.activation(out=yz[i][:, c0:c0 + cl], in_=pp[c][:, 0:cl], func=AF.Sigmoid, bias=GBI[:, mo:mo + 1], scale=1.0),
                             reads=[bpp[c], bK], writes=[byz[i]])
                    S.op("dve", lambda e, i=i, mo=mo: e.tensor_tensor(out=yz[i][:], in0=yz[i][:], in1=YT[:, mo, :], op=ALU.mult), reads=[byz[i], bYT[mo]], writes=[byz[i]])
                    S.op("dve", lambda e, i=i, mo=mo: e.tensor_tensor(out=AT[:, mo, :], in0=yz[i][:], in1=uu[i][:], op=ALU.mult), reads=[byz[i], buu[i]], writes=[bAT[mo]])
                S.barrier()

        def lru_phase(j):
            AT = ATS["AT"]
            with ExitStack() as ph:
                xd = sb(ph, "lxd", [128, T], F32)
                xc = sb(ph, "lxc", [128, T], F32)
                xcb = sb(ph, "lxcb", [128, T], BF16)
                rrd = [sb(ph, "lrr%d" % d, [128, T], F32) for d in range(2)]
                iid = [sb(ph, "lii%d" % d, [128, T], F32) for d in range(2)]
                aad = [sb(ph, "laa%d" % d, [128, T], F32) for d in range(2)]
                brrd, biid, baad, bWd = [Buf(), Buf()], [Buf(), Buf()], [Buf(), Buf()], [Buf(), Buf()]
                hh = [sb(ph, "lhh%d" % d, [128, T], F32) for d in range(2)]
                gd = sb(ph, "lgd", [128, T], F32)
                WA = [sb(ph, "lwa%d" % d, [128, 128], BF16) for d in range(2)]
                WX = [sb(ph, "lwx%d" % d, [128, 128], BF16) for d in range(2)]
                CW = sb(ph, "lcw", [128, 8, 4], F32)
                CB = sb(ph, "lcb", [128, 8], F32)
                LM = sb(ph, "llm", [128, 2, 8], F32)
                BA = sb(ph, "lba", [128, 2, 8], F32)
                BX = sb(ph, "lbx", [128, 2, 8], F32)
                pp = [ps(ph, "lps%d" % i, [128, 512]) for i in range(4)]
                bK, bxd, bxc, bxcb, brr, bii, baa, bbb, bgd, bW = (Buf() for _ in range(10))
                bhh = [Buf(), Buf()]
                bpp = [Buf() for _ in range(4)]
                for k in range(4):
                    load_T(CW[:, :, k], od["lru_conv_w"][j, k, :].rearrange("(h p) -> h p", p=128), 8, pp[0], bpp[0], bK)
                load_T(CB[:], od["lru_conv_b"][j].rearrange("(h p) -> h p", p=128), 8, pp[0], bpp[0], bK)
                for d in range(2):
                    for tl_, nm in ((LM, "lru_lam"), (BA, "lru_ba"), (BX, "lru_bx")):
                        load_T(tl_[:, d, :], od[nm][j, d, :].rearrange("(h p) -> h p", p=128), 8, pp[0], bpp[0], bK)
                S.op("act", lambda e: e.activation(out=LM[:], in_=LM[:], func=AF.Exp, scale=-1.0), reads=[bK], writes=[bK])
                S.op("act", lambda e: e.activation(out=LM[:], in_=LM[:], func=AF.Ln, bias=1.0, scale=1.0), reads=[bK], writes=[bK])
                S.op("dve", lambda e: e.tensor_scalar(out=LM[:], in0=LM[:], scalar1=-8.0, scalar2=None, op0=ALU.mult), reads=[bK], writes=[bK])
                pi_ = 0
                for h in range(8):
                    S.dma("sp", xd[:], ZO[1024 + h * 128:1024 + (h + 1) * 128, :], writes=[bxd])
                    S.dma("sp", gd[:], ZO[3072 + h * 128:3072 + (h + 1) * 128, :], writes=[bgd])
                    S.op("act", lambda e: e.activation(out=gd[:], in_=gd[:], func=AF.Silu), reads=[bgd], writes=[bgd])
                    for (a0, a1) in ((0, CTXL), (CTXL, T)):
                        S.op("dve", lambda e, a0=a0, a1=a1, h=h: e.tensor_scalar(out=xc[:, a0:a1], in0=xd[:, a0:a1], scalar1=CW[:, h, 1:2], scalar2=CB[:, h:h + 1], op0=ALU.mult, op1=ALU.add),
                             reads=[bxd, bK], writes=[bxc])
                        for (kk, so, do, n_) in ((0, 0, 1, a1 - a0 - 1), (2, 1, 0, a1 - a0 - 1), (3, 2, 0, a1 - a0 - 2)):
                            S.op("dve", lambda e, a0=a0, kk=kk, so=so, do=do, n_=n_, h=h: e.scalar_tensor_tensor(
                                out=xc[:, a0 + do:a0 + do + n_], in0=xd[:, a0 + so:a0 + so + n_], scalar=CW[:, h, kk:kk + 1], in1=xc[:, a0 + do:a0 + do + n_],
                                op0=ALU.mult, op1=ALU.add), reads=[bxd, bxc, bK], writes=[bxc])
                    S.op("act", lambda e: e.copy(out=xcb[:], in_=xc[:]), reads=[bxc], writes=[bxcb])
                    for d in range(2):
                        S.dma("poolq", WA[d][:], od["lru_wa"][j, d, h], writes=[bWd[d]])
                        S.dma("poolq", WX[d][:], od["lru_wx"][j, d, h], writes=[bWd[d]])
                    for d in range(2):
                        for (Wt, Bt, dst, bdst) in ((WA[d], BA, rrd[d], brrd[d]), (WX[d], BX, iid[d], biid[d])):
                            for c, (c0, cl) in enumerate(CH5):
                                p_ = pp[pi_ % 4]
                                bp_ = bpp[pi_ % 4]
                                pi_ += 1
                                S.op("pe", lambda e, p_=p_, Wt=Wt, c0=c0, cl=cl: e.matmul(p_[:, 0:cl], lhsT=Wt[:], rhs=xcb[:, c0:c0 + cl], start=True, stop=True),
                                     reads=[bWd[d], bxcb], writes=[bp_])
                                S.op("act", lambda e, p_=p_, Bt=Bt, dst=dst, c0=c0, cl=cl, d=d, h=h: e.activation(out=dst[:, c0:c0 + cl], in_=p_[:, 0:cl], func=AF.Sigmoid, bias=Bt[:, d, h:h + 1], scale=1.0),
                                     reads=[bp_, bK], writes=[bdst])
                    for d in range(2):
                        S.op("act", lambda e, d=d, h=h: e.activation(out=aad[d][:], in_=rrd[d][:], func=AF.Exp, scale=LM[:, d, h:h + 1]), reads=[brrd[d], bK], writes=[baad[d]])
                    for d in range(2):
                        S.op("dve", lambda e, d=d: e.tensor_tensor(out=rrd[d][:], in0=aad[d][:], in1=aad[d][:], op=ALU.mult), reads=[baad[d]], writes=[brrd[d]])
                        S.op("pool", lambda e, d=d: e.tensor_tensor(out=iid[d][:], in0=iid[d][:], in1=xc[:], op=ALU.mult), reads=[biid[d], bxc], writes=[biid[d]])
                    for d in range(2):
                        S.op("act", lambda e, d=d: e.activation(out=rrd[d][:], in_=rrd[d][:], func=AF.Sqrt, bias=1.0, scale=-1.0), reads=[brrd[d]], writes=[brrd[d]])
                    for d in range(2):
                        S.op("dve", lambda e, d=d: e.tensor_tensor(out=iid[d][:], in0=iid[d][:], in1=rrd[d][:], op=ALU.mult), reads=[biid[d], brrd[d]], writes=[biid[d]])
                        aa, bb = aad[d], iid[d]
                        if d == 0:
                            S.op("dve", lambda e, aa=aa, bb=bb: e.tensor_tensor_scan(out=hh[0][:], data0=aa[:], data1=bb[:], initial=0.0, op0=ALU.mult, op1=ALU.add),
                                 reads=[baad[d], biid[d]], writes=[bhh[0]])
                        else:
                            S.op("dve", lambda e, aa=aa, bb=bb: e.tensor_tensor_scan(out=rev(hh[1][:, 0:CTXL], CTXL), data0=rev(aa[:, 0:CTXL], CTXL), data1=rev(bb[:, 0:CTXL], CTXL),
                                                                                     initial=0.0, op0=ALU.mult, op1=ALU.add), reads=[baad[d], biid[d]], writes=[bhh[1]])
                            S.op("dve", lambda e, aa=aa, bb=bb: e.tensor_tensor_scan(out=rev(hh[1][:, CTXL:T], T - CTXL), data0=rev(aa[:, CTXL:T], T - CTXL), data1=rev(bb[:, CTXL:T], T - CTXL),
                                                                                     initial=hh[1][:, 0:1], op0=ALU.mult, op1=ALU.add), reads=[baad[d], biid[d], bhh[1]], writes=[bhh[1]])
                    S.op("dve", lambda e: e.tensor_tensor(out=hh[0][:], in0=hh[0][:], in1=hh[1][:], op=ALU.add), reads=[bhh[0], bhh[1]], writes=[bhh[0]])
                    S.op("pool", lambda e, h=h: e.tensor_tensor(out=AT[:, 8 + h, :], in0=hh[0][:], in1=gd[:], op=ALU.mult), reads=[bhh[0], bgd], writes=[bAT[8 + h]])
                S.barrier()

        for l in range(nlayers):
            j = l // 2
            last = (l == nlayers - 1)
            at_alloc()
            if "norm" not in skip:
                phase_norm(l)
            if l % 2 == 0:
                if "gemm_tok" not in skip:
                    gemm_tok(l, ev_w_in[j], 1024, ZE)
                if "evmix" not in skip:
                    even_mixers(l, j)
                if "outp" not in skip:
                    out_proj(l, ev_w_out[j], last)
            else:
                if "gemm_ch" not in skip:
                    gemm_ch(od["od_w_in"][j], ODD_IN, ZO)
                at_free()
                if "s5" not in skip:
                    s5_phase(j)
                at_alloc()
                if "s5f" not in skip:
                    s5_finish(j)
                if "lru" not in skip:
                    lru_phase(j)
                if "outp" not in skip:
                    out_proj(l, od["od_w_out"][j], last)
            at_free()
        S.barrier()
        print("ninst", S.ninst, "nwaits", S.nwaits)
    return nc


def rope_tables():
    rows = 2048 // 64
    row = np.repeat(np.arange(rows), 64).astype(np.float32)
    col = np.tile(np.arange(64), rows).astype(np.float32)
    inv = (10000.0 ** (-np.arange(0, 64, 2, dtype=np.float32) / 64)).astype(np.float32)
    ang = np.concatenate([row[:, None] * inv, col[:, None] * inv], axis=-1).astype(np.float32)
    cos = np.concatenate([np.ones((CTXL, 64), np.float32), np.cos(ang).astype(np.float32)], 0)
    sin = np.concatenate([np.zeros((CTXL, 64), np.float32), np.sin(ang).astype(np.float32)], 0)
    return np.ascontiguousarray(cos), np.ascontiguousarray(sin)


_WNAMES = ["ada_w", "ada_b", "norm_g", "ev_w_in", "ev_w_out", "ev_q_g", "ev_k_g", "ev_sgu_g", "ev_ws", "ev_bs",
           "od_w_in", "od_w_out", "s5_lam_re", "s5_lam_im", "s5_log_dt", "s5_b_re", "s5_b_im", "s5_c_re", "s5_c_im",
           "s5_d", "s5_glu_w", "s5_glu_b", "lru_conv_w", "lru_conv_b", "lru_lam", "lru_wa", "lru_ba", "lru_wx", "lru_bx"]


def make_in_maps(inputs, ncores=8):
    cos, sin = rope_tables()
    f = lambda a: np.ascontiguousarray(np.asarray(a, dtype=np.float32))
    shared = {k: f(inputs[k]) for k in _WNAMES}
    maps = []
    for core in range(ncores):
        b = core % 4
        m = dict(shared)
        m["xin"] = np.ascontiguousarray(np.concatenate([f(inputs["ctx"][b]), f(inputs["x"][b])], axis=0))
        m["cond"] = np.ascontiguousarray(np.stack([f(inputs["c"][b]), f(inputs["c_ctx"])], axis=0))
        m["rope_cos"] = cos
        m["rope_sin"] = sin
        maps.append(m)
    return maps


def kernel(**inputs):
    nc = build()
    maps = make_in_maps(inputs)
    res = run_bass_kernel_spmd(nc, maps, core_ids=list(range(8)))
    return np.stack([np.asarray(res.results[b]["out"], dtype=np.float32) for b in range(4)], axis=0)
```
